# Optimizing a Trainium2 kernel written in Bass

```python
import math
import jax, jax.numpy as jnp
from jax import lax
import numpy as np

D_MODEL = 1024
BATCH = 4
SEQ = 8192
DEPTH = 1

EPS = 1e-6
ROPE_THETA = 10000.0
NEG = -1e30
M_HEADS = 4
M_DQK = 64
M_DV = 128
M_CHUNK = 64
M_CONV = 4
M_QK = M_HEADS * M_DQK
M_V = M_HEADS * M_DV
N_HEADS = 8
N_KV = 2
N_HG = N_HEADS // N_KV
N_DH = 64
N_Q = N_HEADS * N_DH
N_KVW = N_KV * N_DH
CMP_LEN = 32
CMP_STRIDE = 16
CMP_HIDDEN = 256
SEL_BLOCK = 64
SEL_TOPN = 16
WINDOW = 512
Q_BLOCK = 128
FORCE_BONUS = 1e4
D_FF = 2816
FFN_CONV = 3
IN_SIZES = (M_QK, M_QK, M_V, M_V, M_HEADS, M_HEADS,
            N_Q, N_KVW, N_KVW, N_KVW, N_KVW, N_KVW, N_KVW, 3 * N_HEADS,
            D_MODEL, D_MODEL)
D_IN = sum(IN_SIZES)

kernel_name = "hybrid_mlstm_nsa_convffn"


def rmsnorm(x, w):
    xf = x.astype(jnp.float32)
    y = xf * lax.rsqrt(jnp.mean(xf * xf, axis=-1, keepdims=True) + EPS)
    return (y * w.astype(jnp.float32)).astype(x.dtype)


def causal_dwconv(x, w, b):
    K = w.shape[0]
    S = x.shape[1]
    xp = jnp.pad(x, ((0, 0), (K - 1, 0), (0, 0)))
    y = xp[:, 0:S] * w[0]
    for k in range(1, K):
        y = y + xp[:, k:k + S] * w[k]
    return y + b


def rope_tables(S):
    pos = jnp.arange(S, dtype=jnp.float32)
    inv = ROPE_THETA ** (-jnp.arange(0, N_DH, 2, dtype=jnp.float32) / N_DH)
    ang = pos[:, None] * inv[None, :]
    return jnp.cos(ang), jnp.sin(ang)


def apply_rope(x, cos, sin):
    x1, x2 = jnp.split(x.astype(jnp.float32), 2, axis=-1)
    c = cos[None, :, None, :]
    s = sin[None, :, None, :]
    return jnp.concatenate([x1 * c - x2 * s, x2 * c + x1 * s], axis=-1).astype(x.dtype)


def mlstm_chunkwise(q, k, v, i_pre, logf):
    B, NH, S, DQK = q.shape
    DV = v.shape[-1]
    L = M_CHUNK
    NC = S // L
    q = q.reshape(B, NH, NC, L, DQK) * (DQK ** -0.5)
    k = k.reshape(B, NH, NC, L, DQK)
    v = v.reshape(B, NH, NC, L, DV)
    i_pre = i_pre.reshape(B, NH, NC, L)
    logf = logf.reshape(B, NH, NC, L)
    b = jnp.cumsum(logf, axis=-1)
    g = b[..., -1]
    a = g[..., None] - b + i_pre
    a_max = jnp.max(a, axis=-1)
    w = jnp.exp(a - a_max[..., None])
    dC = jnp.einsum('bhcl,bhcld,bhcle->bhcde', w, k, v)
    dn = jnp.einsum('bhcl,bhcld->bhcd', w, k)

    def step(carry, inp):
        C, n, m = carry
        dC_c, dn_c, g_c, am_c = inp
        m_new = jnp.maximum(g_c + m, am_c)
        decay = jnp.exp(g_c + m - m_new)
        scl = jnp.exp(am_c - m_new)
        C_new = decay[..., None, None] * C + scl[..., None, None] * dC_c
        n_new = decay[..., None] * n + scl[..., None] * dn_c
        return (C_new, n_new, m_new), (C, n, m)

    init = (jnp.zeros((B, NH, DQK, DV), jnp.float32),
            jnp.zeros((B, NH, DQK), jnp.float32),
            jnp.zeros((B, NH), jnp.float32))
    xs = (jnp.moveaxis(dC, 2, 0), jnp.moveaxis(dn, 2, 0),
          jnp.moveaxis(g, 2, 0), jnp.moveaxis(a_max, 2, 0))
    _, (C_prev, n_prev, m_prev) = lax.scan(step, init, xs)
    C_prev = jnp.moveaxis(C_prev, 0, 2)
    n_prev = jnp.moveaxis(n_prev, 0, 2)
    m_prev = jnp.moveaxis(m_prev, 0, 2)

    causal = jnp.tril(jnp.ones((L, L), dtype=bool))
    D = b[..., :, None] - b[..., None, :] + i_pre[..., None, :]
    D = jnp.where(causal, D, NEG)
    inter = b + m_prev[..., None]
    m_t = jnp.maximum(inter, jnp.max(D, axis=-1))
    Dw = jnp.exp(D - m_t[..., None])
    s = jnp.einsum('bhctd,bhcsd->bhcts', q, k) * Dw
    isc = jnp.exp(inter - m_t)
    num = isc[..., None] * jnp.einsum('bhctd,bhcde->bhcte', q, C_prev) \
        + jnp.einsum('bhcts,bhcse->bhcte', s, v)
    den = isc * jnp.einsum('bhctd,bhcd->bhct', q, n_prev) + jnp.sum(s, axis=-1)
    h = num / jnp.maximum(jnp.abs(den), jnp.exp(-m_t))[..., None]
    return h.reshape(B, NH, S, DV)


def compress_blocks(kv, pe, w1, w2):
    B, G, S, dh = kv.shape
    ncmp = (S - CMP_LEN) // CMP_STRIDE + 1
    idx = (jnp.arange(ncmp) * CMP_STRIDE)[:, None] + jnp.arange(CMP_LEN)[None, :]
    blk = kv[:, :, idx] + pe
    flat = blk.reshape(B, G, ncmp, CMP_LEN * dh)
    return jax.nn.gelu(flat @ w1, approximate=False) @ w2


def masked_softmax(s, mask):
    s = jnp.where(mask, s.astype(jnp.float32), NEG)
    p = jax.nn.softmax(s, axis=-1)
    return jnp.where(mask, p, 0.0)


def nsa_branches(q, kc, vc, ks, vs, kw, vw):
    B, G, HG, S, dh = q.shape
    nqb = S // Q_BLOCK
    nsel = S // SEL_BLOCK
    topn = min(SEL_TOPN, nsel)
    ncmp = kc.shape[2]
    scale = dh ** -0.5
    cstart = jnp.arange(ncmp) * CMP_STRIDE
    cend = cstart + CMP_LEN - 1
    sstart = jnp.arange(nsel) * SEL_BLOCK
    send = sstart + SEL_BLOCK - 1
    overlap = ((cstart[:, None] <= send[None, :]) & (cend[:, None] >= sstart[None, :])).astype(jnp.float32)
    ksb = ks.reshape(B, G, nsel, SEL_BLOCK, dh)
    vsb = vs.reshape(B, G, nsel, SEL_BLOCK, dh)
    kwp = jnp.pad(kw, ((0, 0), (0, 0), (WINDOW, 0), (0, 0)))
    vwp = jnp.pad(vw, ((0, 0), (0, 0), (WINDOW, 0), (0, 0)))
    qblocks = jnp.moveaxis(q.reshape(B, G, HG, nqb, Q_BLOCK, dh), 3, 0)
    gather = jax.vmap(jax.vmap(lambda blocks, ix: blocks[ix]))
    jsel = jnp.arange(nsel)
    win_off = jnp.arange(Q_BLOCK + WINDOW)
    sb_off = jnp.arange(SEL_BLOCK)

    def one_block(args):
        qi, qb = args
        s0 = qi * Q_BLOCK
        tpos = s0 + jnp.arange(Q_BLOCK)
        cmask = cend[None, :] <= tpos[:, None]
        pc = masked_softmax(jnp.einsum('bghqd,bgnd->bghqn', qb, kc) * scale, cmask)
        o_cmp = jnp.einsum('bghqn,bgnd->bghqd', pc.astype(vc.dtype), vc)
        imp = jnp.einsum('bghqn,nj->bgqj', pc, overlap)
        cur = tpos // SEL_BLOCK
        forced = (jsel[None, :] == 0) | (jsel[None, :] == cur[:, None]) | (jsel[None, :] == cur[:, None] - 1)
        imp = jnp.where(forced, FORCE_BONUS, imp)
        imp = jnp.where(jsel[None, :] > cur[:, None], NEG, imp)
        _, idx = lax.top_k(imp, topn)
        ksel = gather(ksb, idx).reshape(B, G, Q_BLOCK, topn * SEL_BLOCK, dh)
        vsel = gather(vsb, idx).reshape(B, G, Q_BLOCK, topn * SEL_BLOCK, dh)
        spos = (idx[..., None] * SEL_BLOCK + sb_off).reshape(B, G, Q_BLOCK, topn * SEL_BLOCK)
        smask = (spos <= tpos[None, None, :, None])[:, :, None]
        ps = masked_softmax(jnp.einsum('bghqd,bgqmd->bghqm', qb, ksel) * scale, smask)
        o_slc = jnp.einsum('bghqm,bgqmd->bghqd', ps.astype(vsel.dtype), vsel)
        kwin = lax.dynamic_slice_in_dim(kwp, s0, Q_BLOCK + WINDOW, axis=2)
        vwin = lax.dynamic_slice_in_dim(vwp, s0, Q_BLOCK + WINDOW, axis=2)
        kpos = s0 - WINDOW + win_off
        wmask = (kpos[None, :] <= tpos[:, None]) & (kpos[None, :] > tpos[:, None] - WINDOW) & (kpos[None, :] >= 0)
        pw = masked_softmax(jnp.einsum('bghqd,bgkd->bghqk', qb, kwin) * scale, wmask)
        o_win = jnp.einsum('bghqk,bgkd->bghqd', pw.astype(vwin.dtype), vwin)
        return o_cmp, o_slc, o_win

    o_cmp, o_slc, o_win = lax.map(one_block, (jnp.arange(nqb), qblocks))
    unblock = lambda o: jnp.moveaxis(o, 0, 3).reshape(B, G, HG, S, dh)
    return unblock(o_cmp), unblock(o_slc), unblock(o_win)


def hybrid_layer(x, cos, sin, norm1_w, w_in, m_conv_w, m_conv_b, m_igate_b, m_fgate_b, m_out_norm_w,
                 q_norm_w, kcmp_norm_w, kslc_norm_w, kwin_norm_w,
                 cmp_k_pe, cmp_k_w1, cmp_k_w2, cmp_v_pe, cmp_v_w1, cmp_v_w2,
                 w_up_m, w_up_n, merge_gate_b, w_out,
                 norm2_w, ffn_w_up, ffn_conv_w, ffn_conv_b, ffn_w_down):
    B, S, _ = x.shape
    h = rmsnorm(x, norm1_w)
    proj = h @ w_in
    split_pts = np.cumsum(IN_SIZES)[:-1].tolist()
    (mq, mk, mv, mo, mi, mf, nq, kc, vc, ks, vs, kw, vw, ng, gm, gn) = jnp.split(proj, split_pts, axis=-1)

    qk = jax.nn.silu(causal_dwconv(jnp.concatenate([mq, mk], axis=-1), m_conv_w, m_conv_b))
    mq, mk = jnp.split(qk, 2, axis=-1)
    to_mheads = lambda t, d: t.reshape(B, S, M_HEADS, d).transpose(0, 2, 1, 3).astype(jnp.float32)
    i_pre = (mi + m_igate_b).astype(jnp.float32).transpose(0, 2, 1)
    logf = jax.nn.log_sigmoid((mf + m_fgate_b).astype(jnp.float32)).transpose(0, 2, 1)
    hm = mlstm_chunkwise(to_mheads(mq, M_DQK), to_mheads(mk, M_DQK), to_mheads(mv, M_DV), i_pre, logf)
    hm = rmsnorm(hm.transpose(0, 2, 1, 3), m_out_norm_w)
    hm = hm.reshape(B, S, M_V).astype(x.dtype) * jax.nn.sigmoid(mo)

    qn = apply_rope(rmsnorm(nq.reshape(B, S, N_HEADS, N_DH), q_norm_w), cos, sin)
    kv_heads = lambda t: t.reshape(B, S, N_KV, N_DH)
    to_groups = lambda t: t.transpose(0, 2, 1, 3)
    kc_t = to_groups(apply_rope(rmsnorm(kv_heads(kc), kcmp_norm_w), cos, sin))
    ks_t = to_groups(apply_rope(rmsnorm(kv_heads(ks), kslc_norm_w), cos, sin))
    kw_t = to_groups(apply_rope(rmsnorm(kv_heads(kw), kwin_norm_w), cos, sin))
    vc_t = to_groups(kv_heads(vc))
    vs_t = to_groups(kv_heads(vs))
    vw_t = to_groups(kv_heads(vw))
    qg = qn.reshape(B, S, N_KV, N_HG, N_DH).transpose(0, 2, 3, 1, 4)
    kcmp = compress_blocks(kc_t, cmp_k_pe, cmp_k_w1, cmp_k_w2)
    vcmp = compress_blocks(vc_t, cmp_v_pe, cmp_v_w1, cmp_v_w2)
    o_cmp, o_slc, o_win = nsa_branches(qg, kcmp, vcmp, ks_t, vs_t, kw_t, vw_t)
    gates = jax.nn.sigmoid(ng.reshape(B, S, N_KV, N_HG, 3).transpose(0, 2, 3, 1, 4))
    on = gates[..., 0:1] * o_cmp + gates[..., 1:2] * o_slc + gates[..., 2:3] * o_win
    on = on.transpose(0, 3, 1, 2, 4).reshape(B, S, N_Q)

    y = jax.nn.sigmoid(gm + merge_gate_b[0]) * (hm @ w_up_m) \
        + jax.nn.sigmoid(gn + merge_gate_b[1]) * (on @ w_up_n)
    x = x + y @ w_out

    h2 = rmsnorm(x, norm2_w)
    a, v = jnp.split(h2 @ ffn_w_up, 2, axis=-1)
    a = jax.nn.gelu(causal_dwconv(a, ffn_conv_w, ffn_conv_b), approximate=False)
    return x + (a * v) @ ffn_w_down


def setup_inputs(seed: int = 0) -> dict:
    key = jax.random.key(seed)
    ks = jax.random.split(key, 32)
    f32 = jnp.float32
    nrm = lambda k, shape, s: jax.random.normal(k, shape, f32) * s
    gain = lambda k, shape: 1.0 + 0.02 * jax.random.normal(k, shape, f32)
    Ld = DEPTH
    return {
        "x": nrm(ks[0], (BATCH, SEQ, D_MODEL), 1.0),
        "norm1_w": gain(ks[1], (Ld, D_MODEL)),
        "w_in": nrm(ks[2], (Ld, D_MODEL, D_IN), D_MODEL ** -0.5),
        "m_conv_w": nrm(ks[3], (Ld, M_CONV, 2 * M_QK), M_CONV ** -0.5),
        "m_conv_b": nrm(ks[4], (Ld, 2 * M_QK), 0.01),
        "m_igate_b": nrm(ks[5], (Ld, M_HEADS), 0.1),
        "m_fgate_b": jnp.linspace(3.0, 6.0, M_HEADS, dtype=f32)[None, :] + nrm(ks[6], (Ld, M_HEADS), 0.01),
        "m_out_norm_w": gain(ks[7], (Ld, M_HEADS, M_DV)),
        "q_norm_w": gain(ks[8], (Ld, N_DH)),
        "kcmp_norm_w": gain(ks[9], (Ld, N_DH)),
        "kslc_norm_w": gain(ks[10], (Ld, N_DH)),
        "kwin_norm_w": gain(ks[11], (Ld, N_DH)),
        "cmp_k_pe": nrm(ks[12], (Ld, CMP_LEN, N_DH), 0.02),
        "cmp_k_w1": nrm(ks[13], (Ld, CMP_LEN * N_DH, CMP_HIDDEN), (CMP_LEN * N_DH) ** -0.5),
        "cmp_k_w2": nrm(ks[14], (Ld, CMP_HIDDEN, N_DH), CMP_HIDDEN ** -0.5),
        "cmp_v_pe": nrm(ks[15], (Ld, CMP_LEN, N_DH), 0.02),
        "cmp_v_w1": nrm(ks[16], (Ld, CMP_LEN * N_DH, CMP_HIDDEN), (CMP_LEN * N_DH) ** -0.5),
        "cmp_v_w2": nrm(ks[17], (Ld, CMP_HIDDEN, N_DH), CMP_HIDDEN ** -0.5),
        "w_up_m": nrm(ks[18], (Ld, M_V, D_MODEL), M_V ** -0.5),
        "w_up_n": nrm(ks[19], (Ld, N_Q, D_MODEL), N_Q ** -0.5),
        "merge_gate_b": nrm(ks[20], (Ld, 2, D_MODEL), 0.01),
        "w_out": nrm(ks[21], (Ld, D_MODEL, D_MODEL), D_MODEL ** -0.5),
        "norm2_w": gain(ks[22], (Ld, D_MODEL)),
        "ffn_w_up": nrm(ks[23], (Ld, D_MODEL, 2 * D_FF), D_MODEL ** -0.5),
        "ffn_conv_w": nrm(ks[24], (Ld, FFN_CONV, D_FF), FFN_CONV ** -0.5),
        "ffn_conv_b": nrm(ks[25], (Ld, D_FF), 0.01),
        "ffn_w_down": nrm(ks[26], (Ld, D_FF, D_MODEL), D_FF ** -0.5),
    }


def reference(x, norm1_w, w_in, m_conv_w, m_conv_b, m_igate_b, m_fgate_b, m_out_norm_w,
              q_norm_w, kcmp_norm_w, kslc_norm_w, kwin_norm_w,
              cmp_k_pe, cmp_k_w1, cmp_k_w2, cmp_v_pe, cmp_v_w1, cmp_v_w2,
              w_up_m, w_up_n, merge_gate_b, w_out,
              norm2_w, ffn_w_up, ffn_conv_w, ffn_conv_b, ffn_w_down):
    cos, sin = rope_tables(x.shape[1])
    layer_params = (norm1_w, w_in, m_conv_w, m_conv_b, m_igate_b, m_fgate_b, m_out_norm_w,
                    q_norm_w, kcmp_norm_w, kslc_norm_w, kwin_norm_w,
                    cmp_k_pe, cmp_k_w1, cmp_k_w2, cmp_v_pe, cmp_v_w1, cmp_v_w2,
                    w_up_m, w_up_n, merge_gate_b, w_out,
                    norm2_w, ffn_w_up, ffn_conv_w, ffn_conv_b, ffn_w_down)
    for layer in range(DEPTH):
        x = hybrid_layer(x, cos, sin, *[p[layer] for p in layer_params])
    return x
```

```python
import math
from contextlib import ExitStack
import numpy as np
import ml_dtypes
import concourse.bass as bass
import concourse.mybir as mybir
from concourse.bass_utils import run_bass_kernel_spmd

F32 = mybir.dt.float32
BF16 = mybir.dt.bfloat16
AF = mybir.ActivationFunctionType
ALU = mybir.AluOpType
AX = mybir.AxisListType
EPS = 1e-6
NEGB = -30000.0


class Buf:
    __slots__ = ("name", "w", "r", "dsem", "dcnt", "excl", "rg")

    def __init__(self, name, excl=False):
        self.name = name
        self.rg = None
        self.excl = excl
        self.w = None
        self.r = {}
        self.dsem = None
        self.dcnt = 0


class Trk:
    ENG = ("pe", "act", "dve", "pool", "sp")

    def __init__(self, nc):
        self.nc = nc
        self.e = {"pe": nc.tensor, "act": nc.scalar, "dve": nc.vector,
                  "pool": nc.gpsimd, "sp": nc.sync}
        self.sems = {}
        self.cnt = {}
        for k in ("pe", "act", "dve", "pool"):
            self.sems[k] = nc.alloc_semaphore("s_" + k)
            self.cnt[k] = 0
        self.known = {k: {} for k in self.ENG}
        self.nd = 0
        self.dbufs = []
        self.ninstr = 0

    def _wait(self, eng, deps):
        kn = self.known[eng]
        best = {}
        for (s, v) in deps:
            if s == eng and v > self.cnt[eng]:
                continue
            if kn.get(s, 0) < v and best.get(s, 0) < v:
                best[s] = v
        for s, v in best.items():
            self.e[eng].wait_ge(self.sems[s], v)
            kn[s] = v

    @staticmethod
    def _deps(reads, writes):
        deps = []
        for b in reads:
            if b.w is not None:
                deps.append(b.w)
        for b in writes:
            if b.w is not None:
                deps.append(b.w)
            deps.extend(b.r.items())
        return deps

    def op(self, eng, fn, reads=(), writes=(), sig=True, rg="f"):
        ex = [b for b in reads if b.excl]
        if ex:
            reads = [b for b in reads if not b.excl]
            writes = list(writes) + ex
        deps = self._deps(reads, writes)
        if eng == "pe":
            drop = set()
            for b in writes:
                if b.w is not None and b.w[0] == "pe" and not ({b.rg, rg} == {0, 64}):
                    drop.add(b.w)
            keep = set()
            for b in writes:
                if b.w is not None and b.w[0] == "pe" and ({b.rg, rg} == {0, 64}):
                    keep.add(b.w)
                for it in b.r.items():
                    if it[0] == "pe":
                        keep.add(it)
            for b in reads:
                if b.w is not None and b.w[0] == "pe":
                    keep.add(b.w)
            deps = [d_ for d_ in deps if not (d_ in drop and d_ not in keep)]
            for b in writes:
                b.rg = rg
        self._wait(eng, deps)
        ins = fn()
        self.ninstr += 1
        if sig:
            self.cnt[eng] += 1
            ins.then_inc(self.sems[eng], 1)
            v = self.cnt[eng]
        else:
            v = self.cnt[eng] + 1
        for b in reads:
            b.r[eng] = v
        for b in writes:
            b.w = (eng, v)
            b.r = {}
        return ins

    def dma(self, q, pairs, reads=(), writes=(), sembuf=None):
        sb = sembuf if sembuf is not None else (writes[0] if writes else reads[0])
        if sb.dsem is None:
            key = "d%d" % self.nd
            self.nd += 1
            sb.dsem = key
            self.sems[key] = self.nc.alloc_semaphore(key)
            self.dbufs.append(sb)
        deps = self._deps(reads, writes)
        if sb.dcnt > 0:
            deps.append((sb.dsem, sb.dcnt))
        self._wait(q, deps)
        for (o, i) in pairs:
            self.e[q].dma_start(out=o, in_=i).then_inc(self.sems[sb.dsem], 16)
            sb.dcnt += 16
            self.ninstr += 1
        ev = (sb.dsem, sb.dcnt)
        for b in reads:
            b.r[ev[0]] = ev[1]
        for b in writes:
            b.w = ev
            b.r = {}
        return ev

    def barrier(self):
        deps = [(k, self.cnt[k]) for k in ("pe", "act", "dve", "pool") if self.cnt[k] > 0]
        deps += [(b.dsem, b.dcnt) for b in self.dbufs if b.dcnt > 0]
        for eng in self.ENG:
            self._wait(eng, deps)


def pipeline(items, body, skew, maxact=2):
    it = iter(items)
    act = []
    done = False
    while True:
        if not done and len(act) < maxact and (not act or act[-1][1] >= skew):
            try:
                act.append([body(next(it)), 0])
            except StopIteration:
                done = True
        if not act:
            if done:
                break
            continue
        for a in list(act):
            try:
                next(a[0])
                a[1] += 1
            except StopIteration:
                act.remove(a)


class TB:
    def __init__(self, t, name, excl=False):
        self.t = t
        self.b = Buf(name, excl)

    def __getitem__(self, idx):
        return self.t[idx]


IN_OFF = {}
_o = 0
for _n, _s in (("mq", 256), ("mk", 256), ("mv", 512), ("mo", 512), ("mi", 4), ("mf", 4),
               ("nq", 512), ("kc", 128), ("vc", 128), ("ks", 128), ("vs", 128), ("kw", 128),
               ("vw", 128), ("ng", 24), ("gm", 1024), ("gn", 1024)):
    IN_OFF[_n] = (_o, _s)
    _o += _s
D_IN = _o
D_FF = 2816

WA = {}
_o = 0
for _n in ("mv", "kc", "ks", "kw", "mi", "mf", "vs", "vw", "mo", "mk", "mq", "vc"):
    WA[_n] = _o
    _o += IN_OFF[_n][1]
WA_N = _o
TA_KNW, TA_QNW, TA_BIG, TA_BFG, TA_MONW, TA_N = 0, 384, 896, 900, 904, 1416


def build(NT, QLO, RESET_TILE, debug=0):
    S = NT * 128
    NOWN = NT - QLO
    NCMP = 8 * NT - 1
    NCC = (8 * NT + 127) // 128
    NCPAD = NCC * 128
    NSEL = 2 * NT
    nc = bass.Bass("TRN2", target_bir_lowering=False)
    T = Trk(nc)

    def din(name, shape, dt=F32):
        return nc.dram_tensor(name, list(shape), dt, kind="ExternalInput").ap()

    x_d = din("x", [S, 1024])
    cos_d = din("cos", [S, 32])
    sin_d = din("sin", [S, 32])
    w_in_d = din("w_in", [1024, D_IN])
    n1w_d = din("n1w", [128, 8])
    n2w_d = din("n2w", [128, 8])
    tabA_d = din("tabA", [128, TA_N])
    cw_d = din("cw", [128, 16])
    cb_d = din("cb", [128, 4])
    fcw_d = din("fcw", [128, 66])
    fcb_d = din("fcb", [128, 22])
    mgb_d = din("mgb", [1, 2048])
    pe2_d = din("pe2", [32, 256])
    cst_d = din("cst", [128, 128 * 3 + 2 + 64 + 1])
    ovl_d = din("ovl", [128, NCC * NSEL])
    ewin_d = din("ewin", [64, S])
    cbm_d = din("cbm", [128, 1024])
    cmask_d = din("cmask", [NOWN, 128, NCPAD], BF16)
    imptab_d = din("imptab", [NOWN, 128, NSEL])
    kb_d = din("kb", [128, NT])
    rv_d = din("rv", [128, NOWN])
    w1k_d = din("cmp_k_w1", [2048, 256])
    w2k_d = din("cmp_k_w2", [256, 64])
    w1v_d = din("cmp_v_w1", [2048, 256])
    w2v_d = din("cmp_v_w2", [256, 64])
    wupm_d = din("w_up_m", [512, 1024])
    wupn_d = din("w_up_n", [512, 1024])
    wout_d = din("w_out", [1024, 1024])
    fup_d = din("ffn_w_up", [1024, 2 * D_FF])
    fdn_d = din("ffn_w_down", [D_FF, 1024])
    out_d = nc.dram_tensor("out", [NOWN, 128, 1024], F32, kind="ExternalOutput").ap()
    out_b = Buf("out")
    hmT_d = nc.dram_tensor("hmT_scr", [NOWN, 128, 512], BF16, kind="Internal").ap()
    onT_d = nc.dram_tensor("onT_scr", [NOWN, 128, 512], BF16, kind="Internal").ap()
    kcs_d = nc.dram_tensor("kcs_scr", [128, S], BF16, kind="Internal").ap()
    vcs_d = nc.dram_tensor("vcs_scr", [128, S], BF16, kind="Internal").ap()
    kvs_db = Buf("kvs_scr")
    hmT_db = [Buf("hmTd%d" % i) for i in range(NOWN)]
    onT_db = [Buf("onTd%d" % i) for i in range(NOWN)]
    xmid_db = [Buf("xmid%d" % i) for i in range(NOWN)]
    dbg = {}
    if debug:
        def dout(name, shape, dt=F32):
            dbg[name] = (nc.dram_tensor(name, list(shape), dt, kind="ExternalOutput").ap(), Buf(name))
            return dbg[name]

    def V(fn, r=(), w=()):
        return T.op("dve", fn, [a.b for a in r], [a.b for a in w])

    def A(fn, r=(), w=()):
        return T.op("act", fn, [a.b for a in r], [a.b for a in w])

    def G(fn, r=(), w=()):
        return T.op("pool", fn, [a.b for a in r], [a.b for a in w])

    def P(fn, r=(), w=(), sig=True, rg="f"):
        return T.op("pe", fn, [a.b for a in r], [a.b for a in w], sig, rg)

    def D(pairs, r=(), w=(), q="sp", sembuf=None):
        if sembuf is None:
            cand = [a for a in list(w) + list(r) if isinstance(a, TB)]
            sembuf = cand[0].b if cand else None
        return T.dma(q, pairs, [a if isinstance(a, Buf) else a.b for a in r],
                     [a if isinstance(a, Buf) else a.b for a in w], sembuf)

    main = ExitStack()

    def mk(es, name, shape, dt=F32):
        return TB(es.enter_context(nc.sbuf_tensor("sb_" + name, list(shape), dt)), name)

    PS = [TB(nc.alloc_psum_tensor("ps%d" % i, [128, 512], F32), "ps%d" % i, True) for i in range(8)]

    def bfv(ps, ncols):
        return ps[:, 0:ncols // 2].bitcast(BF16)

    cst = mk(main, "cst", [128, 128 * 3 + 2 + 64 + 1])
    D([(cst[:], cst_d)], w=[cst])
    identf = cst[:, 0:128]
    U2 = cst[:, 128:256]
    onesf = cst[:, 256:384]
    m01 = cst[:, 384:386]
    mask_st = cst[:, 386:450]
    flag = cst[:, 450:451]
    identb = mk(main, "identb", [128, 128], BF16)
    V(lambda: nc.vector.tensor_copy(out=identb[:], in_=identf), [cst], [identb])
    onesb = mk(main, "onesb", [128, 128], BF16)
    V(lambda: nc.vector.tensor_copy(out=onesb[:], in_=onesf), [cst], [onesb])
    tabA = mk(main, "tabA", [128, TA_N])
    D([(tabA[:], tabA_d)], w=[tabA])
    n1w = mk(main, "n1w", [128, 8])
    D([(n1w[:], n1w_d)], w=[n1w])
    n2w = mk(main, "n2w", [128, 8])
    D([(n2w[:], n2w_d)], w=[n2w])
    kbias = mk(main, "kbias", [128, NT])
    D([(kbias[:], kb_d)], w=[kbias])
    rvt = mk(main, "rvt", [128, NOWN])
    D([(rvt[:], rv_d)], w=[rvt])

    def load_w(es_stage, dst_tb, dst_ap, src_ap, shape, scale_ap=None, eng="dve", scale_tb=None):
        st = es_stage[0][es_stage[1] % len(es_stage[0])]
        es_stage[1] += 1
        if len(shape) == 2:
            sv = st[:, 0:shape[1]]
        else:
            sv = st[:, 0:shape[1] * shape[2]].rearrange("p (a n) -> p a n", a=shape[1])
        D([(sv, src_ap)], w=[st])
        if scale_ap is None:
            if eng == "dve":
                V(lambda: nc.vector.tensor_copy(out=dst_ap, in_=sv), [st], [dst_tb])
            elif eng == "act":
                A(lambda: nc.scalar.copy(out=dst_ap, in_=sv), [st], [dst_tb])
            else:
                G(lambda: nc.gpsimd.tensor_copy(out=dst_ap, in_=sv), [st], [dst_tb])
        else:
            V(lambda: nc.vector.tensor_tensor(out=dst_ap, in0=sv, in1=scale_ap, op=ALU.mult), [st, scale_tb], [dst_tb])

    nhalf = mk(main, "nhalf", [128, 8])
    G(lambda: nc.gpsimd.memset(nhalf[:], -0.5), [], [nhalf])

    def rstd_pow(rs, ss, n):
        w = ss.t.shape[1]
        G(lambda: nc.gpsimd.tensor_scalar(out=rs[:], in0=ss[:], scalar1=1.0 / n, scalar2=EPS, op0=ALU.mult, op1=ALU.add), [ss], [rs])
        G(lambda: nc.gpsimd.tensor_tensor(out=rs[:], in0=rs[:], in1=nhalf[:, 0:w], op=ALU.pow), [rs, nhalf], [rs])

    def rmsnorm_T(xt, tmp, ss, rs, xn, xnT, psT, evac="dve"):
        A(lambda: nc.scalar.activation(out=tmp[:], in_=xt[:], func=AF.Square, accum_out=ss[:]), [xt], [tmp, ss])
        rstd_pow(rs, ss, 1024)
        A(lambda: nc.scalar.activation(out=xn[:], in_=xt[:], func=AF.Copy, scale=rs[:]), [xt, rs], [xn])
        pv = bfv(psT, 1024)
        for c in range(8):
            P(lambda: nc.tensor.transpose(out=pv[:, c * 128:(c + 1) * 128], in_=xn[:, c * 128:(c + 1) * 128],
                                          identity=identb[:]), [xn, identb], [psT], sig=(c == 7))
        if evac == "dve":
            V(lambda: nc.vector.tensor_copy(out=xnT[:].rearrange("p c t -> p (c t)"), in_=pv), [psT], [xnT])
        else:
            A(lambda: nc.scalar.copy(out=xnT[:].rearrange("p c t -> p (c t)"), in_=pv), [psT], [xnT])

    kvs = ExitStack()
    KE = [mk(kvs, "KE%d" % g, [128, S], BF16) for g in range(2)]
    kwT = mk(kvs, "kwT", [128, S], BF16)
    vsA = mk(kvs, "vsA", [128, NT, 2, 65], BF16)
    vwA = mk(kvs, "vwA", [128, NT, 2, 65], BF16)
    kcmpT = mk(kvs, "kcmpT", [128, NCPAD], BF16)
    vcmp = mk(kvs, "vcmp", [128, NCC, 2, 64], BF16)
    G(lambda: nc.gpsimd.memset(vsA[:], 1.0), [], [vsA])
    G(lambda: nc.gpsimd.memset(vwA[:], 1.0), [], [vwA])
    G(lambda: nc.gpsimd.memset(kcmpT[:], 0.0), [], [kcmpT])
    G(lambda: nc.gpsimd.memset(vcmp[:], 0.0), [], [vcmp])

    pa0 = ExitStack()
    pa = ExitStack()
    wA = mk(pa, "wA", [128, 8, WA_N], BF16)
    with ExitStack() as stg:
        stage = [[mk(stg, "stgA%d" % i, [128, 2048]) for i in range(2)], 0]
        w_in_v = w_in_d.rearrange("(c p) n -> p c n", p=128)
        for name in ("mv", "kc", "ks", "kw", "mi", "mf", "vs", "vw", "mo", "mk", "mq", "vc"):
            so, sn = IN_OFF[name]
            do = WA[name]
            for o in range(0, sn, 256):
                n = min(256, sn - o)
                load_w(stage, wA, wA[:, :, do + o:do + o + n], w_in_v[:, :, so + o:so + o + n], [128, 8, n],
                       scale_ap=n1w[:, :].unsqueeze(2).broadcast_to([128, 8, n]), scale_tb=n1w)
        T.barrier()
    xt2 = [mk(pa, "xt%d" % i, [128, 1024]) for i in range(2)]
    two = lambda nm, shp, dt=F32: [mk(pa, "%s_%d" % (nm, i), shp, dt) for i in range(2)]
    three = lambda nm, shp, dt=F32: [mk(pa, "%s_%d" % (nm, i), shp, dt) for i in range(4)]
    ss1_2, rs1_2 = two("ss1", [128, 1]), two("rs1", [128, 1])
    xn_2 = two("xn", [128, 1024], BF16)
    xnT_2 = two("xnT", [128, 8, 128], BF16)
    cs2 = two("cs", [128, 64])
    ksq_2 = two("ksq", [128, 384])
    ss6_2, rs6_2 = two("ss6", [128, 6]), two("rs6", [128, 6])
    kn_2 = two("kn", [128, 6, 64])
    rt_2 = [two("rt%d" % i, [128, 6, 32]) for i in range(4)]
    kr_2 = two("kr", [128, 6, 64], BF16)
    convb_2 = two("convb", [128, 4, 131])
    cacc_2 = two("cacc", [128, 4, 128])
    cw = mk(pa, "cw", [128, 4, 4])
    cb = mk(pa, "cb", [128, 4])
    D([(cw[:].rearrange("p a b -> p (a b)"), cw_d)], w=[cw])
    D([(cb[:], cb_d)], w=[cb])
    zg_2, sp_2, ip_2 = two("zg", [128, 4]), two("sp", [128, 4]), two("ip", [128, 4])
    sp8_2, es_2, expg_2 = two("sp8", [128, 8]), two("es", [128, 4]), two("expg", [128, 8])
    kvst_2 = two("kvst", [128, 2, 128], BF16)
    kqT2 = three("kqT", [128, 4, 128], BF16)
    ktil2 = three("ktil", [128, 4, 64], BF16)
    vaug2 = three("vaug", [128, 4, 129], BF16)
    esm2 = three("esm", [128, 4, 64])
    eb82 = three("eb8", [128, 4])
    egp2 = three("egp", [128, 2, 2])
    sigo2 = three("sigo", [128, 512])
    for v_ in vaug2:
        G(lambda: nc.gpsimd.memset(v_[:], 1.0), [], [v_])
    for c_ in convb_2:
        V(lambda: nc.vector.memset(c_[:], 0.0), [], [c_])
    Cst = mk(pa, "Cst", [128, 2, 129])
    snap = [mk(pa, "snap%d" % i, [128, 2, 129], BF16) for i in range(8)]
    V(lambda: nc.vector.memset(Cst[:], 0.0), [], [Cst])
    V(lambda: nc.vector.memset(snap[0][:], 0.0), [], [snap[0]])
    PT = mk(pa, "PT", [128, 4, 64], BF16)
    d4 = [mk(pa, "d4_%d" % i, [128, 4]) for i in range(3)]
    hraw = mk(pa, "hraw", [128, 4, 128])
    ss4 = mk(pa, "ss4", [128, 4])
    rs4 = mk(pa, "rs4", [128, 4])
    hm = mk(pa, "hm", [128, 512], BF16)
    hmTs = mk(pa, "hmTs", [128, 512], BF16)
    lnc = math.log(0.125)
    psT, pA_, pB_, psS, psKV, psSTm, psO0, psO1 = PS
    prot = {"i": 0}

    def a_front(t):
        own = t >= QLO
        needq = t >= QLO - 1
        p_ = t % 2
        h_ = t % 4
        xt, cs = xt2[p_], cs2[p_]
        vaug, kqT, ktil, esm, eb8, egp, sigo = vaug2[h_], kqT2[h_], ktil2[h_], esm2[h_], eb82[h_], egp2[h_], sigo2[h_]
        ss1, rs1, xn, xnT, ksq, ss6, rs6, kn, kr = ss1_2[p_], rs1_2[p_], xn_2[p_], xnT_2[p_], ksq_2[p_], ss6_2[p_], rs6_2[p_], kn_2[p_], kr_2[p_]
        rt = [rt_2[i][p_] for i in range(4)]
        convb, convn, cacc = convb_2[p_], convb_2[1 - p_], cacc_2[p_]
        zg, sp, ip, sp8, es_, expg, kvst = zg_2[p_], sp_2[p_], ip_2[p_], sp8_2[p_], es_2[p_], expg_2[p_], kvst_2[p_]
        sqj = xn
        D([(xt[:], x_d[t * 128:(t + 1) * 128, :])], w=[xt])
        D([(cs[:, 0:32], cos_d[t * 128:(t + 1) * 128, :]), (cs[:, 32:64], sin_d[t * 128:(t + 1) * 128, :])], w=[cs])
        A(lambda: nc.scalar.activation(out=sqj[:], in_=xt[:], func=AF.Square, accum_out=ss1[:]), [xt], [sqj, ss1])
        yield
        rstd_pow(rs1, ss1, 1024)
        yield
        A(lambda: nc.scalar.activation(out=xn[:], in_=xt[:], func=AF.Copy, scale=rs1[:]), [xt, rs1], [xn])
        yield
        pv = bfv(psT, 1024)
        for c in range(8):
            P(lambda: nc.tensor.transpose(out=pv[:, c * 128:(c + 1) * 128], in_=xn[:, c * 128:(c + 1) * 128], identity=identb[:]),
              [xn, identb], [psT], sig=(c == 7))
        V(lambda: nc.vector.tensor_copy(out=xnT[:].rearrange("p c t -> p (c t)"), in_=pv), [psT], [xnT])
        yield

        def bank():
            prot["i"] += 1
            return (pA_, pB_)[prot["i"] % 2]

        def proj_tm(ps, col0, ncols, wcol):
            for c in range(8):
                P(lambda: nc.tensor.matmul(out=ps[:, col0:col0 + ncols], lhsT=xnT[:, c, :], rhs=wA[:, c, wcol:wcol + ncols],
                                           start=(c == 0), stop=(c == 7)), [xnT, wA], [ps], sig=(c == 7))

        def proj_fm(ps, col0, wcol):
            for c in range(8):
                P(lambda: nc.tensor.matmul(out=ps[:, col0:col0 + 128], lhsT=wA[:, c, wcol:wcol + 128], rhs=xnT[:, c, :],
                                           start=(c == 0), stop=(c == 7)), [xnT, wA], [ps], sig=(c == 7))

        psK = bank()
        proj_tm(psK, 0, 392, WA["kc"])
        A(lambda: nc.scalar.activation(out=ksq[:], in_=psK[:, 0:384], func=AF.Square), [psK], [ksq])
        V(lambda: nc.vector.tensor_copy(out=kn[:].rearrange("p a d -> p (a d)"), in_=psK[:, 0:384]), [psK], [kn])
        V(lambda: nc.vector.tensor_tensor(out=zg[:], in0=psK[:, 388:392], in1=tabA[:, TA_BFG:TA_BFG + 4], op=ALU.add), [psK, tabA], [zg])
        V(lambda: nc.vector.tensor_tensor(out=ip[:], in0=psK[:, 384:388], in1=tabA[:, TA_BIG:TA_BIG + 4], op=ALU.add), [psK, tabA], [ip])
        yield
        psMV = bank()
        proj_tm(psMV, 0, 512, WA["mv"])
        A(lambda: nc.scalar.copy(out=vaug[:, :, 0:128], in_=psMV[:, :].rearrange("p (h d) -> p h d", h=4)), [psMV], [vaug])
        yield
        psV = bank()
        proj_tm(psV, 0, 256, WA["vs"])
        proj_fm(psV, 256, WA["vc"])
        tsl = slice(t * 128, (t + 1) * 128)
        A(lambda: nc.scalar.copy(out=vsA[:, t, :, 0:64], in_=psV[:, 0:128].rearrange("p (g d) -> p g d", g=2)), [psV], [vsA])
        A(lambda: nc.scalar.copy(out=vwA[:, t, :, 0:64], in_=psV[:, 128:256].rearrange("p (g d) -> p g d", g=2)), [psV], [vwA])
        V(lambda: nc.vector.tensor_copy(out=kvst[:, 1, :], in_=psV[:, 256:384]), [psV], [kvst])
        yield
        psF = bank()
        proj_fm(psF, 0, WA["mk"])
        proj_fm(psF, 128, WA["mk"] + 128)
        if needq:
            proj_fm(psF, 256, WA["mq"])
            proj_fm(psF, 384, WA["mq"] + 128)
        nch = 4 if needq else 2
        A(lambda: nc.scalar.copy(out=convb[:, 0:nch, 3:131], in_=psF[:, 0:nch * 128].rearrange("p (a t) -> p a t", a=nch)), [psF], [convb])
        yield
        if own:
            psMO = bank()
            proj_tm(psMO, 0, 512, WA["mo"])
            A(lambda: nc.scalar.activation(out=sigo[:], in_=psMO[:, 0:512], func=AF.Sigmoid), [psMO], [sigo])
            yield
        def chain_keys():
            V(lambda: nc.vector.tensor_reduce(out=ss6[:], in_=ksq[:].rearrange("p (a d) -> p a d", a=6), axis=AX.X, op=ALU.add), [ksq], [ss6])
            yield
            rstd_pow(rs6, ss6, 64)
            yield
            V(lambda: nc.vector.tensor_tensor(out=kn[:], in0=kn[:], in1=rs6[:, :].unsqueeze(2).broadcast_to([128, 6, 64]), op=ALU.mult), [kn, rs6], [kn])
            V(lambda: nc.vector.tensor_tensor(out=kn[:], in0=kn[:], in1=tabA[:, TA_KNW:TA_KNW + 384].rearrange("p (a d) -> p a d", a=6),
                                              op=ALU.mult), [kn, tabA], [kn])
            yield
            cosb = cs[:, 0:32].unsqueeze(1).broadcast_to([128, 6, 32])
            sinb = cs[:, 32:64].unsqueeze(1).broadcast_to([128, 6, 32])
            V(lambda: nc.vector.tensor_tensor(out=rt[0][:], in0=kn[:, :, 0:32], in1=cosb, op=ALU.mult), [kn, cs], [rt[0]])
            G(lambda: nc.gpsimd.tensor_tensor(out=rt[1][:], in0=kn[:, :, 32:64], in1=sinb, op=ALU.mult), [kn, cs], [rt[1]])
            V(lambda: nc.vector.tensor_tensor(out=rt[2][:], in0=kn[:, :, 32:64], in1=cosb, op=ALU.mult), [kn, cs], [rt[2]])
            G(lambda: nc.gpsimd.tensor_tensor(out=rt[3][:], in0=kn[:, :, 0:32], in1=sinb, op=ALU.mult), [kn, cs], [rt[3]])
            yield
            V(lambda: nc.vector.tensor_tensor(out=kr[:, :, 0:32], in0=rt[0][:], in1=rt[1][:], op=ALU.subtract), [rt[0], rt[1]], [kr])
            G(lambda: nc.gpsimd.tensor_tensor(out=kr[:, :, 32:64], in0=rt[2][:], in1=rt[3][:], op=ALU.add), [rt[2], rt[3]], [kr])
            yield
            pk = bfv(psS, 512)
            P(lambda: nc.tensor.transpose(out=pk[:, 0:128], in_=kr[:, 0:2, :].rearrange("p a d -> p (a d)"), identity=identb[:]), [kr, identb], [psS], sig=False)
            P(lambda: nc.tensor.transpose(out=pk[:, 128:256], in_=kr[:, 4:6, :].rearrange("p a d -> p (a d)"), identity=identb[:]), [kr, identb], [psS], sig=False)
            for g in range(2):
                P(lambda: nc.tensor.transpose(out=pk[0:64, 256 + g * 128:256 + (g + 1) * 128], in_=kr[:, 2 + g, :], identity=identb[:]), [kr, identb], [psS],
                  sig=(g == 1))
            A(lambda: nc.scalar.copy(out=kvst[:, 0, :], in_=pk[:, 0:128]), [psS], [kvst])
            A(lambda: nc.scalar.copy(out=kwT[:, tsl], in_=pk[:, 128:256]), [psS], [kwT])
            for g in range(2):
                V(lambda: nc.vector.tensor_copy(out=KE[g][0:64, tsl], in_=pk[0:64, 256 + g * 128:256 + (g + 1) * 128]), [psS], [KE[g]])
            D([(kcs_d[:, tsl], kvst[:, 0, :]), (vcs_d[:, tsl], kvst[:, 1, :])], r=[kvst], w=[kvs_db])
            yield

        def chain_conv():
            for j in range(nch):
                V(lambda: nc.vector.tensor_scalar(out=cacc[:, j, :], in0=convb[:, j, 0:128], scalar1=cw[:, j, 0:1], scalar2=None, op0=ALU.mult),
                  [convb, cw], [cacc])
                for k in range(1, 4):
                    V(lambda: nc.vector.scalar_tensor_tensor(out=cacc[:, j, :], in0=convb[:, j, k:k + 128], scalar=cw[:, j, k:k + 1],
                                                             in1=cacc[:, j, :], op0=ALU.mult, op1=ALU.add), [convb, cw, cacc], [cacc])
                A(lambda: nc.scalar.activation(out=kqT[:, j, :], in_=cacc[:, j, :], func=AF.Silu, bias=cb[:, j:j + 1]), [cacc, cb], [kqT])
                yield
            G(lambda: nc.gpsimd.tensor_copy(out=convn[:, 0:nch, 0:3], in_=convb[:, 0:nch, 128:131]), [convb], [convn])
            yield

        def chain_gates():
            A(lambda: nc.scalar.activation(out=zg[:], in_=zg[:], func=AF.Exp, scale=-1.0), [zg], [zg])
            yield
            A(lambda: nc.scalar.activation(out=sp[:], in_=zg[:], func=AF.Ln, bias=1.0), [zg], [sp])
            yield
            V(lambda: nc.vector.tensor_scalar(out=sp8[:, 0:4], in0=sp[:], scalar1=m01[:, 0:1], scalar2=None, op0=ALU.mult), [sp, cst], [sp8])
            V(lambda: nc.vector.tensor_scalar(out=sp8[:, 4:8], in0=sp[:], scalar1=m01[:, 1:2], scalar2=None, op0=ALU.mult), [sp, cst], [sp8])
            yield

        chains = [chain_keys(), chain_conv(), chain_gates()]
        while chains:
            for g_ in list(chains):
                if next(g_, "end") == "end":
                    chains.remove(g_)
            yield
        P(lambda: nc.tensor.matmul(out=psS[:, 256:260], lhsT=U2, rhs=sp[:], start=True, stop=True), [cst, sp], [psS])
        P(lambda: nc.tensor.matmul(out=psS[:, 264:272], lhsT=onesf, rhs=sp8[:], start=True, stop=True), [cst, sp8], [psS])
        pkt = psS[:, 272:400].bitcast(BF16)
        for j in range(2):
            P(lambda: nc.tensor.transpose(out=pkt[:, j * 128:(j + 1) * 128], in_=kqT[:, j, :], identity=identb[:]), [kqT, identb], [psS], sig=(j == 1))
        V(lambda: nc.vector.tensor_tensor(out=es_[:], in0=psS[:, 256:260], in1=ip[:], op=ALU.add), [psS, ip], [es_])
        A(lambda: nc.scalar.activation(out=eb8[:], in_=psS[:, 256:260], func=AF.Exp, scale=-1.0, bias=lnc), [psS], [eb8])
        A(lambda: nc.scalar.activation(out=expg[:], in_=psS[:, 264:272], func=AF.Exp, scale=-1.0), [psS], [expg])
        A(lambda: nc.scalar.activation(out=es_[:], in_=es_[:], func=AF.Exp), [es_], [es_])
        egv = expg[:, :].rearrange("p (ch c par) -> p ch c par", ch=2, c=2)
        V(lambda: nc.vector.tensor_copy(out=egp[0:64, :, :], in_=egv[0:64, :, :, 0]), [expg], [egp])
        V(lambda: nc.vector.tensor_copy(out=egp[64:128, :, :], in_=egv[64:128, :, :, 1]), [expg], [egp])
        V(lambda: nc.vector.tensor_tensor(out=ktil[:], in0=pkt.rearrange("p (h d) -> p h d", h=4),
                                          in1=es_[:, :].unsqueeze(2).broadcast_to([128, 4, 64]), op=ALU.mult), [psS, es_], [ktil])
        if own:
            V(lambda: nc.vector.tensor_tensor(out=esm[:], in0=mask_st.unsqueeze(1).broadcast_to([128, 4, 64]),
                                              in1=es_[:, :].unsqueeze(2).broadcast_to([128, 4, 64]), op=ALU.mult), [cst, es_], [esm])
        yield

    def a_scan(t):
        h_ = t % 4
        vaug, ktil, egp = vaug2[h_], ktil2[h_], egp2[h_]
        if RESET_TILE is not None and t == RESET_TILE:
            sn = snap[(2 * t) % 8]
            V(lambda: nc.vector.tensor_scalar(out=Cst[:], in0=Cst[:], scalar1=flag, scalar2=None, op0=ALU.mult), [Cst, cst], [Cst])
            V(lambda: nc.vector.tensor_scalar(out=sn[:], in0=sn[:], scalar1=flag, scalar2=None, op0=ALU.mult), [sn, cst], [sn])
        for ch in range(2):
            rows = slice(ch * 64, ch * 64 + 64)
            kvv = psKV[:, 0:258].rearrange("p (c e) -> p c e", c=2)
            for h in range(4):
                c, par = h // 2, h % 2
                P(lambda: nc.tensor.matmul(out=kvv[par * 64:(par + 1) * 64, c, :], lhsT=ktil[rows, h, :], rhs=vaug[rows, h, :],
                                           start=True, stop=True), [ktil, vaug], [psKV], sig=(h == 3), rg=ch * 64)
            V(lambda: nc.vector.tensor_tensor(out=Cst[:], in0=kvv, in1=Cst[:], op=ALU.add), [psKV, Cst], [Cst])
            yield
            V(lambda: nc.vector.tensor_tensor(out=Cst[:], in0=Cst[:], in1=egp[:, ch, :].unsqueeze(2).broadcast_to([128, 2, 129]),
                                              op=ALU.mult), [Cst, egp], [Cst])
            yield
            nx = snap[(2 * t + ch + 1) % 8]
            A(lambda: nc.scalar.copy(out=nx[:], in_=Cst[:]), [Cst], [nx])
            yield

    def a_out(t):
        h_ = t % 4
        vaug, kqT, esm, eb8, sigo = vaug2[h_], kqT2[h_], esm2[h_], eb82[h_], sigo2[h_]
        psO = (psO0, psO1)
        for ch in range(2):
            rows = slice(ch * 64, ch * 64 + 64)
            csl = slice(ch * 64, ch * 64 + 64)
            Cbf = snap[(2 * t + ch) % 8]
            stp = psSTm[:, 0:256].rearrange("p (h s) -> p h s", h=4)
            for h in range(4):
                c, par = h // 2, h % 2
                prow = slice(par * 64, par * 64 + 64)
                P(lambda: nc.tensor.matmul(out=stp[rows, h, :], lhsT=kqT[prow, c, csl], rhs=kqT[prow, 2 + c, csl],
                                           start=True, stop=True), [kqT], [psSTm], sig=True, rg=par * 64)
            V(lambda: nc.vector.tensor_tensor(out=PT[rows], in0=stp[rows], in1=esm[rows], op=ALU.mult), [psSTm, esm], [PT])
            yield
            for h in range(4):
                c, par = h // 2, h % 2
                prow = slice(par * 64, par * 64 + 64)
                ov = psO[c][:, 0:258].rearrange("p (a e) -> p a e", a=2)
                P(lambda: nc.tensor.matmul(out=ov[rows, par, :], lhsT=kqT[prow, 2 + c, csl], rhs=Cbf[prow, c, :],
                                           start=True, stop=False), [kqT, Cbf], [psO[c]], sig=True, rg=par * 64)
                P(lambda: nc.tensor.matmul(out=ov[rows, par, :], lhsT=PT[rows, h, :], rhs=vaug[rows, h, :],
                                           start=False, stop=True), [PT, vaug], [psO[c]], sig=True, rg=ch * 64)
            yield
        oi = t - QLO
        for c in range(2):
            ov = psO[c][:, 0:258].rearrange("p (a e) -> p a e", a=2)
            V(lambda: nc.vector.tensor_tensor(out=d4[0][:, 2 * c:2 * c + 2], in0=ov[:, :, 128], in1=eb8[:, 2 * c:2 * c + 2], op=ALU.mult),
              [psO[c], eb8], [d4[0]])
        V(lambda: nc.vector.scalar_tensor_tensor(out=d4[1][:], in0=d4[0][:], scalar=-1.0, in1=d4[0][:], op0=ALU.mult, op1=ALU.max), [d4[0]], [d4[1]])
        V(lambda: nc.vector.tensor_scalar(out=d4[1][:], in0=d4[1][:], scalar1=1.0, scalar2=None, op0=ALU.max), [d4[1]], [d4[1]])
        V(lambda: nc.vector.reciprocal(out=d4[1][:], in_=d4[1][:]), [d4[1]], [d4[1]])
        V(lambda: nc.vector.tensor_tensor(out=d4[2][:], in0=d4[1][:], in1=eb8[:], op=ALU.mult), [d4[1], eb8], [d4[2]])
        for c in range(2):
            ov = psO[c][:, 0:258].rearrange("p (a e) -> p a e", a=2)
            V(lambda: nc.vector.tensor_tensor(out=hraw[:, 2 * c:2 * c + 2, :], in0=ov[:, :, 0:128],
                                              in1=d4[2][:, 2 * c:2 * c + 2].unsqueeze(2).broadcast_to([128, 2, 128]), op=ALU.mult),
              [psO[c], d4[2]], [hraw])
        yield
        for h in range(4):
            A(lambda: nc.scalar.activation(out=hm[:, h * 128:(h + 1) * 128], in_=hraw[:, h, :], func=AF.Square, accum_out=ss4[:, h:h + 1]),
              [hraw], [hm, ss4])
        yield
        rstd_pow(rs4, ss4, 128)
        yield
        V(lambda: nc.vector.tensor_tensor(out=hraw[:], in0=hraw[:], in1=rs4[:, :].unsqueeze(2).broadcast_to([128, 4, 128]), op=ALU.mult),
          [hraw, rs4], [hraw])
        G(lambda: nc.gpsimd.tensor_tensor(out=hraw[:], in0=hraw[:], in1=tabA[:, TA_MONW:TA_MONW + 512].rearrange("p (h d) -> p h d", h=4),
                                          op=ALU.mult), [hraw, tabA], [hraw])
        yield
        V(lambda: nc.vector.tensor_tensor(out=hm[:], in0=hraw[:].rearrange("p h d -> p (h d)"), in1=sigo[:], op=ALU.mult), [hraw, sigo], [hm])
        yield
        ph = bfv(psSTm, 512)
        for c in range(4):
            P(lambda: nc.tensor.transpose(out=ph[:, c * 128:(c + 1) * 128], in_=hm[:, c * 128:(c + 1) * 128], identity=identb[:]),
              [hm, identb], [psSTm], sig=(c == 3))
        A(lambda: nc.scalar.copy(out=hmTs[:], in_=ph), [psSTm], [hmTs])
        D([(hmT_d[oi], hmTs[:])], r=[hmTs], w=[hmT_db[oi]])
        yield

    fs = {"next": 0, "act": [], "done": -1}
    SKEW = 7

    def ftick(limit):
        if fs["next"] < NT and fs["next"] <= limit and len(fs["act"]) < 2 and (not fs["act"] or fs["act"][-1][2] >= SKEW):
            fs["act"].append([fs["next"], a_front(fs["next"]), 0])
            fs["next"] += 1
        for a in list(fs["act"]):
            if next(a[1], "end") == "end":
                fs["act"].remove(a)
                fs["done"] = max(fs["done"], a[0])
            else:
                a[2] += 1

    prev_out = None
    for t in range(NT):
        while fs["done"] < t:
            ftick(t + 1)
            if prev_out is not None and next(prev_out, "end") == "end":
                prev_out = None
        sc_ = a_scan(t)
        while sc_ is not None or prev_out is not None:
            if sc_ is not None and next(sc_, "end") == "end":
                sc_ = None
            if prev_out is not None and next(prev_out, "end") == "end":
                prev_out = None
            ftick(t + 2)
            ftick(t + 2)
        prev_out = a_out(t) if t >= QLO else None
    if prev_out is not None:
        for _ in prev_out:
            ftick(NT)
    while fs["act"]:
        ftick(NT)
    T.barrier()
    pa.close()
    pa2 = ExitStack()
    kcT = mk(pa2, "kcT", [128, S], BF16)
    vcT = mk(pa2, "vcT", [128, S], BF16)
    D([(kcT[:], kcs_d)], r=[kvs_db], w=[kcT])
    D([(vcT[:], vcs_d)], r=[kvs_db], w=[vcT])
    w1 = [mk(pa2, "w1k", [128, 32, 256], BF16), mk(pa2, "w1v", [128, 32, 256], BF16)]
    w2 = [mk(pa2, "w2k", [128, 2, 64], BF16), mk(pa2, "w2v", [128, 2, 64], BF16)]
    pe2f = mk(pa2, "pe2f", [32, 256])
    pe2b = mk(pa2, "pe2b", [32, 256], BF16)
    peT = mk(pa2, "peT", [128, 2, 32], BF16)
    bias4 = mk(pa2, "bias4", [128, 4])
    gel = [[mk(pa2, "gel%d%d" % (kv, g), [128, 2, NCPAD], BF16) for g in range(2)] for kv in range(2)]
    with ExitStack() as stg:
        stage = [[mk(stg, "stgB%d" % i, [128, 2048]) for i in range(2)], 0]
        for kv, wd in enumerate((w1k_d, w1v_d)):
            wv = wd.rearrange("(l d) n -> d l n", d=64)
            for lq in range(4):
                st = stage[0][stage[1] % 2]
                stage[1] += 1
                sv = st[:, :].rearrange("p (a n) -> p a n", a=8)
                D([(sv[0:64], wv[:, lq * 8:(lq + 1) * 8, :]), (sv[64:128], wv[:, lq * 8:(lq + 1) * 8, :])], w=[st])
                V(lambda: nc.vector.tensor_copy(out=w1[kv][:, lq * 8:(lq + 1) * 8, :], in_=sv), [st], [w1[kv]])
        for kv, wd in enumerate((w2k_d, w2v_d)):
            load_w(stage, w2[kv], w2[kv][:], wd.rearrange("(c p) n -> p c n", p=128), [128, 2, 64])
        T.barrier()
    D([(pe2f[:], pe2_d)], w=[pe2f])
    V(lambda: nc.vector.tensor_copy(out=pe2b[:], in_=pe2f[:]), [pe2f], [pe2b])
    for kv in range(2):
        for g in range(2):
            G(lambda: nc.gpsimd.memset(gel[kv][g][:], 0.0), [], [gel[kv][g]])
    ppe = bfv(PS[4], 128)
    for kv in range(2):
        P(lambda: nc.tensor.transpose(out=ppe[:, kv * 32:(kv + 1) * 32], in_=pe2b[:, kv * 128:(kv + 1) * 128], identity=identb[0:32, 0:32]),
          [pe2b, identb], [PS[4]])
    V(lambda: nc.vector.tensor_copy(out=peT[:].rearrange("p a l -> p (a l)"), in_=ppe[:, 0:64]), [PS[4]], [peT])
    for kv in range(2):
        for hc in range(2):
            i4 = kv * 2 + hc
            for l in range(32):
                P(lambda: nc.tensor.matmul(out=PS[5][:, i4:i4 + 1], lhsT=w1[kv][0:64, l, hc * 128:(hc + 1) * 128], rhs=peT[0:64, kv, l:l + 1],
                                           start=(l == 0), stop=(l == 31)), [w1[kv], peT], [PS[5]], sig=(l == 31))
    V(lambda: nc.vector.tensor_copy(out=bias4[:], in_=PS[5][:, 0:4]), [PS[5]], [bias4])
    bi = 0
    for kv, src in enumerate((kcT, vcT)):
        srcv = src[:, :].rearrange("p (n s) -> p n s", s=16)
        for hc in range(2):
            pss = (PS[(2 * bi) % 4], PS[(2 * bi + 1) % 4])
            bi += 1
            for l in range(32):
                for g in range(2):
                    rows = slice(g * 64, g * 64 + 64)
                    P(lambda: nc.tensor.matmul(out=pss[g][:, 0:NCMP], lhsT=w1[kv][rows, l, hc * 128:(hc + 1) * 128],
                                               rhs=srcv[rows, l // 16:l // 16 + NCMP, l % 16], start=(l == 0), stop=(l == 31)),
                      [w1[kv], src], [pss[g]], sig=(l == 31), rg=g * 64)
            for g in range(2):
                A(lambda: nc.scalar.activation(out=gel[kv][g][:, hc, 0:NCMP], in_=pss[g][:, 0:NCMP], func=AF.Gelu,
                                               bias=bias4[:, kv * 2 + hc:kv * 2 + hc + 1]), [pss[g], bias4], [gel[kv][g]])
    for g in range(2):
        rows = slice(g * 64, g * 64 + 64)
        for hc in range(2):
            P(lambda: nc.tensor.matmul(out=PS[6][rows, 0:NCMP], lhsT=w2[0][:, hc, :], rhs=gel[0][g][:, hc, 0:NCMP],
                                       start=(hc == 0), stop=(hc == 1)), [w2[0], gel[0][g]], [PS[6]], sig=(hc == 1))
        V(lambda: nc.vector.tensor_copy(out=kcmpT[rows, 0:NCMP], in_=PS[6][rows, 0:NCMP]), [PS[6]], [kcmpT])
        for cc in range(NCC):
            for hc in range(2):
                P(lambda: nc.tensor.matmul(out=PS[7][:, (cc * 2 + g) * 64:(cc * 2 + g + 1) * 64], lhsT=gel[1][g][:, hc, cc * 128:(cc + 1) * 128],
                                           rhs=w2[1][:, hc, :], start=(hc == 0), stop=(hc == 1)), [w2[1], gel[1][g]], [PS[7]], sig=(hc == 1))
    A(lambda: nc.scalar.copy(out=vcmp[:].rearrange("p c g d -> p (c g d)"), in_=PS[7][:, 0:NCC * 128]), [PS[7]], [vcmp])
    T.barrier()
    if debug == 1:
        for nm, tb in (("d_kwT", kwT), ("d_kcT", kcT), ("d_vcT", vcT)):
            o, ob = dout(nm, [128, S], BF16)
            D([(o, tb[:])], r=[tb], w=[ob])
        o, ob = dout("d_vsA", [128, NT * 130], BF16)
        D([(o, vsA[:].rearrange("p t g d -> p (t g d)"))], r=[vsA], w=[ob])
        o, ob = dout("d_hmT", [NOWN, 128, 512], BF16)
        D([(o[i], hmT_d[i]) for i in range(NOWN)], r=hmT_db, w=[ob])
        o, ob = dout("d_kcmpT", [128, NCPAD], BF16)
        D([(o, kcmpT[:])], r=[kcmpT], w=[ob])
        o, ob = dout("d_vcmp", [128, NCC * 128], BF16)
        D([(o, vcmp[:].rearrange("p c g d -> p (c g d)"))], r=[vcmp], w=[ob])
        T.barrier()
        return nc, T, dbg
    pa2.close()
    pa0.close()

    pb = ExitStack()
    wB = mk(pb, "wB", [128, 8, 536], BF16)
    ovl = mk(pb, "ovl", [128, NCC, NSEL], BF16)
    cbm = mk(pb, "cbm", [128, 1024], BF16)
    with ExitStack() as stg:
        stage = [[mk(stg, "stgC%d" % i, [128, 2048]) for i in range(2)], 0]
        w_in_v = w_in_d.rearrange("(c p) n -> p c n", p=128)
        for (do, so, sn) in ((0, IN_OFF["nq"][0], 512), (512, IN_OFF["ng"][0], 24)):
            for o in range(0, sn, 256):
                n = min(256, sn - o)
                load_w(stage, wB, wB[:, :, do + o:do + o + n], w_in_v[:, :, so + o:so + o + n], [128, 8, n],
                       scale_ap=n1w[:, :].unsqueeze(2).broadcast_to([128, 8, n]), scale_tb=n1w)
        for o in range(0, S, 2048):
            n = min(2048, S - o)
            st = stage[0][stage[1] % 2]
            stage[1] += 1
            D([(st[64:128, 0:n], ewin_d[:, o:o + n])], w=[st])
            V(lambda: nc.vector.tensor_copy(out=KE[0][64:128, o:o + n], in_=st[64:128, 0:n]), [st], [KE[0]])
            G(lambda: nc.gpsimd.tensor_copy(out=KE[1][64:128, o:o + n], in_=st[64:128, 0:n]), [st], [KE[1]])
        load_w(stage, ovl, ovl[:].rearrange("p c j -> p (c j)"), ovl_d, [128, NCC * NSEL])
        load_w(stage, cbm, cbm[:], cbm_d, [128, 1024])
        T.barrier()
    xtb = [mk(pb, "xtb%d" % i, [128, 1024]) for i in range(2)]
    sqjb = mk(pb, "sqjb", [128, 1024], BF16)
    ssb = mk(pb, "ssb", [128, 1])
    rsb = mk(pb, "rsb", [128, 1])
    xnb = mk(pb, "xnb", [128, 1024], BF16)
    xnTb = mk(pb, "xnTb", [128, 8, 128], BF16)
    csb = [mk(pb, "csb%d" % i, [128, 64]) for i in range(2)]
    qsq = mk(pb, "qsq", [128, 512])
    ss8 = mk(pb, "ss8", [128, 8])
    rs8 = mk(pb, "rs8", [128, 8])
    qn = mk(pb, "qn", [128, 8, 64])
    qrt = [mk(pb, "qrt%d" % i, [128, 8, 32]) for i in range(4)]
    qr = mk(pb, "qr", [128, 8, 64], BF16)
    qT2 = [mk(pb, "qT%d" % i, [128, 4, 128], BF16) for i in range(2)]
    sg2 = [mk(pb, "sg%d" % i, [128, 24]) for i in range(2)]
    cmk = [mk(pb, "cmk%d" % i, [128, NCPAD], BF16) for i in range(2)]
    itab = [mk(pb, "itab%d" % i, [128, NSEL]) for i in range(2)]
    sc2 = [mk(pb, "sc%d" % i, [128, NCPAD]) for i in range(2)]
    mx2 = [mk(pb, "mx%d" % i, [128, 1]) for i in range(2)]
    se2 = [mk(pb, "se%d" % i, [128, 1]) for i in range(2)]
    pc2_ = [mk(pb, "pc%d" % i, [128, NCPAD], BF16) for i in range(2)]
    pcT2 = [mk(pb, "pcT%d" % i, [128, NCC, 128], BF16) for i in range(2)]
    ocmp2 = [mk(pb, "ocmp%d" % i, [128, 8, 64]) for i in range(2)]
    oslc = mk(pb, "oslc", [128, 8, 64])
    owin = mk(pb, "owin", [128, 8, 64])
    imp2 = mk(pb, "imp2", [128, NSEL])
    imp3 = mk(pb, "imp3", [128, NSEL])
    m8 = mk(pb, "m8", [128, 16])
    selb = mk(pb, "selb", [128, NSEL])
    negp = mk(pb, "negp", [128, NSEL])
    bqb = mk(pb, "bqb", [128, 2, 128], BF16)
    V(lambda: nc.vector.memset(bqb[:], 0.0), [], [bqb])
    NW = 2 if NSEL > 64 else 1
    QB2 = [[[mk(pb, "QB%d%d%d" % (i, g, w), [128, 4, 128], BF16) for w in range(NW)] for g in range(2)] for i in range(2)]
    pTs = [mk(pb, "pTs%d" % i, [128, 512], BF16) for i in range(4)]
    den = mk(pb, "den", [128, 4])
    osb = [mk(pb, "osb%d" % i, [128, 260]) for i in range(2)]
    onb = mk(pb, "onb", [128, 512], BF16)
    gt = [mk(pb, "gt%d" % i, [128, 8, 64]) for i in range(2)]
    onTs = mk(pb, "onTs", [128, 512], BF16)
    psT, psQ, psG, psSC, psS0, psS1, psOS, psOW = PS
    psST = (psS0, psS1, psOW)
    cnt_ = {"st": 0, "pt": 0, "os": 0}

    def b_front(oi):
        t = QLO + oi
        p_ = oi % 2
        xt, cs, qT, sg, ocmp, QB = xtb[p_], csb[p_], qT2[p_], sg2[p_], ocmp2[p_], QB2[p_]
        D([(xt[:], x_d[t * 128:(t + 1) * 128, :])], w=[xt])
        D([(cs[:, 0:32], cos_d[t * 128:(t + 1) * 128, :]), (cs[:, 32:64], sin_d[t * 128:(t + 1) * 128, :])], w=[cs])
        D([(cmk[p_][:], cmask_d[oi])], w=[cmk[p_]])
        D([(itab[p_][:], imptab_d[oi])], w=[itab[p_]])
        A(lambda: nc.scalar.activation(out=sqjb[:], in_=xt[:], func=AF.Square, accum_out=ssb[:]), [xt], [sqjb, ssb])
        yield
        rstd_pow(rsb, ssb, 1024)
        yield
        A(lambda: nc.scalar.activation(out=xnb[:], in_=xt[:], func=AF.Copy, scale=rsb[:]), [xt, rsb], [xnb])
        yield
        pv = bfv(psT, 1024)
        for c in range(8):
            P(lambda: nc.tensor.transpose(out=pv[:, c * 128:(c + 1) * 128], in_=xnb[:, c * 128:(c + 1) * 128], identity=identb[:]),
              [xnb, identb], [psT], sig=(c == 7))
        yield
        V(lambda: nc.vector.tensor_copy(out=xnTb[:].rearrange("p c t -> p (c t)"), in_=pv), [psT], [xnTb])
        yield
        for c in range(8):
            P(lambda: nc.tensor.matmul(out=psQ[:, 0:512], lhsT=xnTb[:, c, :], rhs=wB[:, c, 0:512], start=(c == 0), stop=(c == 7)),
              [xnTb, wB], [psQ], sig=(c == 7))
        for c in range(8):
            P(lambda: nc.tensor.matmul(out=psG[:, 0:24], lhsT=xnTb[:, c, :], rhs=wB[:, c, 512:536], start=(c == 0), stop=(c == 7)),
              [xnTb, wB], [psG], sig=(c == 7))
        yield
        A(lambda: nc.scalar.activation(out=sg[:], in_=psG[:, 0:24], func=AF.Exp, scale=-1.0), [psG], [sg])
        A(lambda: nc.scalar.activation(out=qsq[:], in_=psQ[:, 0:512], func=AF.Square), [psQ], [qsq])
        yield
        V(lambda: nc.vector.tensor_scalar(out=sg[:], in0=sg[:], scalar1=1.0, scalar2=None, op0=ALU.add), [sg], [sg])
        V(lambda: nc.vector.reciprocal(out=sg[:], in_=sg[:]), [sg], [sg])
        V(lambda: nc.vector.tensor_reduce(out=ss8[:], in_=qsq[:].rearrange("p (a d) -> p a d", a=8), axis=AX.X, op=ALU.add), [qsq], [ss8])
        yield
        rstd_pow(rs8, ss8, 64)
        yield
        V(lambda: nc.vector.tensor_tensor(out=qn[:], in0=psQ[:, 0:512].rearrange("p (a d) -> p a d", a=8),
                                          in1=rs8[:, :].unsqueeze(2).broadcast_to([128, 8, 64]), op=ALU.mult), [psQ, rs8], [qn])
        V(lambda: nc.vector.tensor_tensor(out=qn[:], in0=qn[:], in1=tabA[:, TA_QNW:TA_QNW + 512].rearrange("p (a d) -> p a d", a=8),
                                          op=ALU.mult), [qn, tabA], [qn])
        yield
        cosb = cs[:, 0:32].unsqueeze(1).broadcast_to([128, 8, 32])
        sinb = cs[:, 32:64].unsqueeze(1).broadcast_to([128, 8, 32])
        V(lambda: nc.vector.tensor_tensor(out=qrt[0][:], in0=qn[:, :, 0:32], in1=cosb, op=ALU.mult), [qn, cs], [qrt[0]])
        G(lambda: nc.gpsimd.tensor_tensor(out=qrt[1][:], in0=qn[:, :, 32:64], in1=sinb, op=ALU.mult), [qn, cs], [qrt[1]])
        V(lambda: nc.vector.tensor_tensor(out=qrt[2][:], in0=qn[:, :, 32:64], in1=cosb, op=ALU.mult), [qn, cs], [qrt[2]])
        G(lambda: nc.gpsimd.tensor_tensor(out=qrt[3][:], in0=qn[:, :, 0:32], in1=sinb, op=ALU.mult), [qn, cs], [qrt[3]])
        yield
        qro = qr[:, :, :].rearrange("p (h g) d -> p g h d", g=2)
        v4 = lambda tb_: tb_[:, :, :].rearrange("p (g h) d -> p g h d", g=2)
        V(lambda: nc.vector.tensor_tensor(out=qro[:, :, :, 0:32], in0=v4(qrt[0]), in1=v4(qrt[1]), op=ALU.subtract), [qrt[0], qrt[1]], [qr])
        G(lambda: nc.gpsimd.tensor_tensor(out=qro[:, :, :, 32:64], in0=v4(qrt[2]), in1=v4(qrt[3]), op=ALU.add), [qrt[2], qrt[3]], [qr])
        yield
        pq = bfv(psT, 512)
        for hg in range(4):
            P(lambda: nc.tensor.transpose(out=pq[:, hg * 128:(hg + 1) * 128], in_=qr[:, 2 * hg:2 * hg + 2, :].rearrange("p a d -> p (a d)"),
                                          identity=identb[:]), [qr, identb], [psT], sig=(hg == 3))
        pq1 = bfv(psSC, 512)
        for hg in range(4):
            P(lambda: nc.tensor.transpose(out=pq1[0:64, hg * 128:(hg + 1) * 128], in_=qr[:, 2 * hg + 1, :], identity=identb[:]),
              [qr, identb], [psSC], sig=(hg == 3))
        yield
        V(lambda: nc.vector.tensor_copy(out=qT[:].rearrange("p h t -> p (h t)"), in_=pq), [psT], [qT])
        for w in range(NW):
            V(lambda: nc.vector.tensor_copy(out=QB[1][w][0:64].rearrange("p h t -> p (h t)"), in_=pq1[0:64, :]), [psSC], [QB[1][w]])
        yield
        for w in range(NW):
            G(lambda: nc.gpsimd.tensor_copy(out=QB[0][w][0:64], in_=qT[0:64]), [qT], [QB[0][w]])
        psOC = psQ
        cm = cmk[p_]

        def cmp_head(head, par):
            g, hg = head // 4, head % 4
            rows = slice(g * 64, g * 64 + 64)
            psc = (psSC, psT)[par]
            sc, mx, se, pc, pcT = sc2[par], mx2[par], se2[par], pc2_[par], pcT2[par]
            P(lambda: nc.tensor.matmul(out=psc[:, 0:NCPAD], lhsT=qT[rows, hg, :], rhs=kcmpT[rows, 0:NCPAD], start=True, stop=False),
              [qT, kcmpT], [psc], sig=False, rg=g * 64)
            P(lambda: nc.tensor.matmul(out=psc[:, 0:NCPAD], lhsT=identb[:], rhs=cm[:], start=False, stop=True), [identb, cm], [psc])
            yield
            V(lambda: nc.vector.reduce_max(out=mx[:], in_=psc[:, 0:NCPAD], axis=AX.X), [psc], [mx])
            V(lambda: nc.vector.tensor_scalar(out=mx[:], in0=mx[:], scalar1=-0.125, scalar2=None, op0=ALU.mult), [mx], [mx])
            yield
            A(lambda: nc.scalar.activation(out=sc[:], in_=psc[:, 0:NCPAD], func=AF.Exp, scale=0.125, bias=mx[:], accum_out=se[:]), [psc, mx], [sc, se])
            yield
            V(lambda: nc.vector.reciprocal(out=se[:], in_=se[:]), [se], [se])
            V(lambda: nc.vector.tensor_scalar(out=pc[:], in0=sc[:], scalar1=se[:], scalar2=rvt[:, oi:oi + 1], op0=ALU.mult, op1=ALU.mult),
              [sc, se, rvt], [pc])
            yield
            pp = bfv(psc, NCC * 128)
            for cc in range(NCC):
                P(lambda: nc.tensor.transpose(out=pp[:, cc * 128:(cc + 1) * 128], in_=pc[:, cc * 128:(cc + 1) * 128], identity=identb[:]),
                  [pc, identb], [psc], sig=(cc == NCC - 1))
            yield
            V(lambda: nc.vector.tensor_copy(out=pcT[:].rearrange("p c t -> p (c t)"), in_=pp), [psc], [pcT])
            yield
            for cc in range(NCC):
                P(lambda: nc.tensor.matmul(out=psOC[:, head * 64:(head + 1) * 64], lhsT=pcT[:, cc, :], rhs=vcmp[:, cc, g, :],
                                           start=(cc == 0), stop=(cc == NCC - 1)), [pcT, vcmp], [psOC], sig=(cc == NCC - 1))
            for cc in range(NCC):
                first = (hg == 0 and cc == 0)
                last = (hg == 3 and cc == NCC - 1)
                P(lambda: nc.tensor.matmul(out=psG[:, 256 + g * NSEL:256 + (g + 1) * NSEL], lhsT=pcT[:, cc, :], rhs=ovl[:, cc, :],
                                           start=first, stop=last, skip_group_check=True), [pcT, ovl], [psG], sig=(cc == NCC - 1))
            yield

        for pair in range(4):
            ga = cmp_head(2 * pair, 0)
            gb = cmp_head(2 * pair + 1, 1)
            for _ in ga:
                next(gb, None)
                yield
        V(lambda: nc.vector.tensor_copy(out=ocmp[:].rearrange("p h d -> p (h d)"), in_=psOC[:, 0:512]), [psOC], [ocmp])
        yield
        for g in range(2):
            V(lambda: nc.vector.tensor_tensor(out=imp2[:], in0=psG[:, 256 + g * NSEL:256 + (g + 1) * NSEL], in1=itab[p_][:], op=ALU.add),
              [psG, itab[p_]], [imp2])
            V(lambda: nc.vector.max(out=m8[:, 0:8], in_=imp2[:]), [imp2], [m8])
            yield
            V(lambda: nc.vector.match_replace(out=imp3[:], in_to_replace=m8[:, 0:8], in_values=imp2[:], imm_value=-3.0e38), [imp2, m8], [imp3])
            V(lambda: nc.vector.max(out=m8[:, 8:16], in_=imp3[:]), [imp3], [m8])
            yield
            V(lambda: nc.vector.tensor_scalar(out=selb[:], in0=imp2[:], scalar1=m8[:, 15:16], scalar2=None, op0=ALU.is_ge), [imp2, m8], [selb])
            V(lambda: nc.vector.tensor_scalar(out=selb[:], in0=selb[:], scalar1=-1.0, scalar2=-NEGB, op0=ALU.add, op1=ALU.mult), [selb], [selb])
            yield
            V(lambda: nc.vector.tensor_scalar(out=negp[:], in0=itab[p_][:], scalar1=0.0, scalar2=NEGB, op0=ALU.min, op1=ALU.max),
              [itab[p_]], [negp])
            V(lambda: nc.vector.tensor_tensor(out=bqb[:, 0, 0:NSEL], in0=selb[:], in1=negp[:], op=ALU.add), [selb, negp], [bqb])
            yield
            nlo = min(64, NSEL)
            V(lambda: nc.vector.tensor_copy(out=bqb[:, 1, 64:64 + nlo], in_=bqb[:, 0, 0:nlo]), [bqb], [bqb])
            if NSEL > 64:
                V(lambda: nc.vector.tensor_copy(out=bqb[:, 1, 0:NSEL - 64], in_=bqb[:, 0, 64:NSEL]), [bqb], [bqb])
            yield
            pb_ = bfv(psT, 256)
            for w in range(NW):
                P(lambda: nc.tensor.transpose(out=pb_[:, w * 128:(w + 1) * 128], in_=bqb[:, 1 - w, :], identity=identb[:]), [bqb, identb], [psT],
                  sig=(w == NW - 1))
            yield
            for w in range(NW):
                V(lambda: nc.vector.tensor_copy(out=QB[g][w][64:128], in_=pb_[64:128, w * 128:(w + 1) * 128].unsqueeze(1).broadcast_to([64, 4, 128])),
                  [psT], [QB[g][w]])
            yield

    def b_loops(oi):
        t = QLO + oi
        p_ = oi % 2
        qT, sg, ocmp, QB = qT2[p_], sg2[p_], ocmp2[p_], QB2[p_]
        qTf = qT[:, :, :].rearrange("p h t -> p (h t)")
        k0 = max(0, t - 4)
        its = []
        for g in range(2):
            its += [("s", g, kt) for kt in range(t + 1)]
            its += [("w", g, kt) for kt in range(k0, t + 1)]
        slots = {}

        def front(i):
            kind, g, kt = its[i]
            rows = slice(g * 64, g * 64 + 64)
            ps = psST[cnt_["st"] % 3]
            cnt_["st"] += 1
            pT = pTs[cnt_["pt"] % 4]
            cnt_["pt"] += 1
            slots[i] = (ps, pT)
            ksl = slice(kt * 128, (kt + 1) * 128)
            if kind == "s":
                qb = QB[g][kt // 32]
                P(lambda: nc.tensor.matmul(out=ps[:, :], lhsT=KE[g][:, ksl], rhs=qb[:, :, :].rearrange("p h t -> p (h t)"), start=True, stop=(kt != t)),
                  [KE[g], qb], [ps], sig=(kt != t))
                if kt == t:
                    P(lambda: nc.tensor.matmul(out=ps[:, :], lhsT=identb[:], rhs=cbm[:, 0:512], start=False, stop=True), [identb, cbm], [ps])
            else:
                edge = (kt == t) or (kt == t - 4)
                P(lambda: nc.tensor.matmul(out=ps[:, :], lhsT=kwT[rows, ksl], rhs=qTf[rows, :], start=True, stop=(not edge)), [kwT, qT], [ps],
                  sig=(not edge), rg=g * 64)
                if kt == t:
                    P(lambda: nc.tensor.matmul(out=ps[:, :], lhsT=identb[:], rhs=cbm[:, 0:512], start=False, stop=True), [identb, cbm], [ps])
                elif kt == t - 4:
                    P(lambda: nc.tensor.matmul(out=ps[:, :], lhsT=identb[:], rhs=cbm[:, 512:1024], start=False, stop=True), [identb, cbm], [ps])

        def back(i):
            kind, g, kt = its[i]
            ps, pT = slots.pop(i)
            if kind == "s":
                A(lambda: nc.scalar.activation(out=pT[:], in_=ps[:, :], func=AF.Exp, scale=0.125), [ps], [pT])
                pso, vst, first, last = psOS, vsA, (kt == 0), (kt == t)
            else:
                A(lambda: nc.scalar.activation(out=pT[:], in_=ps[:, :], func=AF.Exp, scale=0.125, bias=kbias[:, kt:kt + 1]), [ps, kbias], [pT])
                pso, vst, first, last = psOS, vwA, (kt == k0), (kt == t)
            for hg in range(4):
                P(lambda: nc.tensor.matmul(out=pso[:, hg * 65:(hg + 1) * 65], lhsT=pT[:, hg * 128:(hg + 1) * 128], rhs=vst[:, kt, g, :],
                                           start=(first and hg == 0), stop=last, skip_group_check=True), [pT, vst], [pso], sig=(hg == 3 and last))
            if last:
                ob = osb[cnt_["os"] % 2]
                cnt_["os"] += 1
                A(lambda: nc.scalar.copy(out=ob[:], in_=pso[:, 0:260]), [pso], [ob])
                ov_ = ob[:, :].rearrange("p (h e) -> p h e", h=4)
                dst = oslc if kind == "s" else owin
                V(lambda: nc.vector.tensor_scalar(out=den[:], in0=ov_[:, :, 64], scalar1=1e-30, scalar2=None, op0=ALU.max), [ob], [den])
                V(lambda: nc.vector.reciprocal(out=den[:], in_=den[:]), [den], [den])
                V(lambda: nc.vector.tensor_tensor(out=dst[:, g * 4:(g + 1) * 4, :], in0=ov_[:, :, 0:64],
                                                  in1=den[:, :].unsqueeze(2).broadcast_to([128, 4, 64]), op=ALU.mult), [ob, den], [dst])

        DEPTH = 2
        for i in range(min(DEPTH, len(its))):
            front(i)
        for i in range(len(its)):
            if i + DEPTH < len(its):
                front(i + DEPTH)
            back(i)
            yield
        sgv = sg[:, :].rearrange("p (h k) -> p h k", k=3)
        V(lambda: nc.vector.tensor_tensor(out=gt[0][:], in0=ocmp[:], in1=sgv[:, :, 0:1].broadcast_to([128, 8, 64]), op=ALU.mult), [ocmp, sg], [gt[0]])
        G(lambda: nc.gpsimd.tensor_tensor(out=gt[1][:], in0=oslc[:], in1=sgv[:, :, 1:2].broadcast_to([128, 8, 64]), op=ALU.mult), [oslc, sg], [gt[1]])
        V(lambda: nc.vector.tensor_tensor(out=gt[0][:], in0=gt[0][:], in1=gt[1][:], op=ALU.add), [gt[0], gt[1]], [gt[0]])
        G(lambda: nc.gpsimd.tensor_tensor(out=gt[1][:], in0=owin[:], in1=sgv[:, :, 2:3].broadcast_to([128, 8, 64]), op=ALU.mult), [owin, sg], [gt[1]])
        V(lambda: nc.vector.tensor_tensor(out=onb[:], in0=gt[0][:].rearrange("p h d -> p (h d)"), in1=gt[1][:].rearrange("p h d -> p (h d)"),
                                          op=ALU.add), [gt[0], gt[1]], [onb])
        yield
        po = bfv(psS0, 512)
        for c in range(4):
            P(lambda: nc.tensor.transpose(out=po[:, c * 128:(c + 1) * 128], in_=onb[:, c * 128:(c + 1) * 128], identity=identb[:]),
              [onb, identb], [psS0], sig=(c == 3))
        yield
        V(lambda: nc.vector.tensor_copy(out=onTs[:], in_=po), [psS0], [onTs])
        D([(onT_d[oi], onTs[:])], r=[onTs], w=[onT_db[oi]])
        yield

    for _ in b_front(0):
        pass
    for oi in range(NOWN):
        lp = b_loops(oi)
        nf = b_front(oi + 1) if oi + 1 < NOWN else None
        L_ = 2 * (QLO + oi) + 12 + 3
        F_ = 60
        for i_, _ in enumerate(lp):
            nadv = ((i_ + 1) * F_) // L_ - (i_ * F_) // L_
            for _k in range(nadv):
                if nf is not None:
                    try:
                        next(nf)
                    except StopIteration:
                        nf = None
        if nf is not None:
            for _ in nf:
                pass
    T.barrier()
    if debug == 2:
        o, ob = dout("d_onT", [NOWN, 128, 512], BF16)
        D([(o[i], onT_d[i]) for i in range(NOWN)], r=onT_db, w=[ob])
        T.barrier()
        return nc, T, dbg
    pb.close()
    kvs.close()

    pc1 = ExitStack()
    wC = mk(pc1, "wC", [128, 8, 2048], BF16)
    wum = mk(pc1, "wum", [128, 4, 1024], BF16)
    wun = mk(pc1, "wun", [128, 4, 1024], BF16)
    wo = mk(pc1, "wo", [128, 8, 1024], BF16)
    mgbf = mk(pc1, "mgbf", [1, 2048])
    mgbb = mk(pc1, "mgbb", [1, 2048], BF16)
    D([(mgbf[:], mgb_d)], w=[mgbf])
    V(lambda: nc.vector.tensor_copy(out=mgbb[:], in_=mgbf[:]), [mgbf], [mgbb])
    with ExitStack() as stg:
        stage = [[mk(stg, "stgD%d" % i, [128, 2048]) for i in range(2)], 0]
        w_in_v = w_in_d.rearrange("(c p) n -> p c n", p=128)
        so = IN_OFF["gm"][0]
        for o in range(0, 2048, 256):
            load_w(stage, wC, wC[:, :, o:o + 256], w_in_v[:, :, so + o:so + o + 256], [128, 8, 256],
                   scale_ap=n1w[:, :].unsqueeze(2).broadcast_to([128, 8, 256]), scale_tb=n1w)
        for (wt, wd, kc_) in ((wum, wupm_d, 4), (wun, wupn_d, 4), (wo, wout_d, 8)):
            wv = wd.rearrange("(c p) n -> p c n", p=128)
            for o in range(0, 1024, 2048 // kc_):
                n = 2048 // kc_
                load_w(stage, wt, wt[:, :, o:o + n], wv[:, :, o:o + n], [128, kc_, n])
        T.barrier()
    xtc = [mk(pc1, "xtc%d" % i, [128, 1024]) for i in range(2)]
    sqjc = mk(pc1, "sqjc", [128, 1024], BF16)
    ssc = [mk(pc1, "ssc%d" % i, [128, 1]) for i in range(2)]
    rsc = [mk(pc1, "rsc%d" % i, [128, 1]) for i in range(2)]
    xnc = [mk(pc1, "xnc%d" % i, [128, 1024], BF16) for i in range(2)]
    xnTc = [mk(pc1, "xnTc%d" % i, [128, 8, 128], BF16) for i in range(2)]
    hmTl = [mk(pc1, "hmTl%d" % i, [128, 4, 128], BF16) for i in range(2)]
    onTl = [mk(pc1, "onTl%d" % i, [128, 4, 128], BF16) for i in range(2)]
    sgm = [mk(pc1, "sgm%d" % i, [128, 2048]) for i in range(2)]
    y1 = [mk(pc1, "y1_%d" % i, [128, 1024]) for i in range(2)]
    y2 = [mk(pc1, "y2_%d" % i, [128, 1024]) for i in range(2)]
    yb = [mk(pc1, "yb%d" % i, [128, 1024], BF16) for i in range(2)]
    yT = [mk(pc1, "yT%d" % i, [128, 8, 128], BF16) for i in range(2)]
    xmo = [mk(pc1, "xmo%d" % i, [128, 1024]) for i in range(2)]
    rot = {"g": 0, "u": 0, "o": 0}

    def c1_tile(oi):
        t = QLO + oi
        p_ = oi % 2
        xt = xtc[p_]
        psTp = (PS[0], PS[7])[p_]
        D([(xt[:], x_d[t * 128:(t + 1) * 128, :])], w=[xt])
        D([(hmTl[p_][:].rearrange("p c t -> p (c t)"), hmT_d[oi])], r=[hmT_db[oi]], w=[hmTl[p_]])
        D([(onTl[p_][:].rearrange("p c t -> p (c t)"), onT_d[oi])], r=[onT_db[oi]], w=[onTl[p_]])
        A(lambda: nc.scalar.activation(out=sqjc[:], in_=xt[:], func=AF.Square, accum_out=ssc[p_][:]), [xt], [sqjc, ssc[p_]])
        rstd_pow(rsc[p_], ssc[p_], 1024)
        yield
        A(lambda: nc.scalar.activation(out=xnc[p_][:], in_=xt[:], func=AF.Copy, scale=rsc[p_][:]), [xt, rsc[p_]], [xnc[p_]])
        yield
        pv = bfv(psTp, 1024)
        for c in range(8):
            P(lambda: nc.tensor.transpose(out=pv[:, c * 128:(c + 1) * 128], in_=xnc[p_][:, c * 128:(c + 1) * 128], identity=identb[:]),
              [xnc[p_], identb], [psTp], sig=(c == 7))
        yield
        V(lambda: nc.vector.tensor_copy(out=xnTc[p_][:].rearrange("p c t -> p (c t)"), in_=pv), [psTp], [xnTc[p_]])
        yield
        for nb in range(4):
            ps = PS[1 + rot["g"] % 2]
            rot["g"] += 1
            for c in range(8):
                P(lambda: nc.tensor.matmul(out=ps[:, :], lhsT=xnTc[p_][:, c, :], rhs=wC[:, c, nb * 512:(nb + 1) * 512], start=(c == 0), stop=False),
                  [xnTc[p_], wC], [ps], sig=False)
            P(lambda: nc.tensor.matmul(out=ps[:, :], lhsT=onesb[0:1, :], rhs=mgbb[0:1, nb * 512:(nb + 1) * 512], start=False, stop=True),
              [onesb, mgbb], [ps])
            A(lambda: nc.scalar.activation(out=sgm[p_][:, nb * 512:(nb + 1) * 512], in_=ps[:, :], func=AF.Sigmoid), [ps], [sgm[p_]])
            if nb % 2 == 1:
                yield
        for br, (src, wt) in enumerate(((hmTl[p_], wum), (onTl[p_], wun))):
            for nb in range(2):
                ps = PS[3 + rot["u"] % 2]
                rot["u"] += 1
                for c in range(4):
                    P(lambda: nc.tensor.matmul(out=ps[:, :], lhsT=src[:, c, :], rhs=wt[:, c, nb * 512:(nb + 1) * 512], start=(c == 0), stop=(c == 3)),
                      [src, wt], [ps], sig=(c == 3))
                yy = y1[p_] if br == 0 else y2[p_]
                V(lambda: nc.vector.tensor_tensor(out=yy[:, nb * 512:(nb + 1) * 512], in0=ps[:, :],
                                                  in1=sgm[p_][:, br * 1024 + nb * 512:br * 1024 + (nb + 1) * 512], op=ALU.mult), [ps, sgm[p_]], [yy])
            yield
        G(lambda: nc.gpsimd.tensor_tensor(out=yb[p_][:], in0=y1[p_][:], in1=y2[p_][:], op=ALU.add), [y1[p_], y2[p_]], [yb[p_]])
        yield
        py = bfv(psTp, 1024)
        for c in range(8):
            P(lambda: nc.tensor.transpose(out=py[:, c * 128:(c + 1) * 128], in_=yb[p_][:, c * 128:(c + 1) * 128], identity=identb[:]),
              [yb[p_], identb], [psTp], sig=(c == 7))
        yield
        A(lambda: nc.scalar.copy(out=yT[p_][:].rearrange("p c t -> p (c t)"), in_=py), [psTp], [yT[p_]])
        yield
        xm = xmo[p_]
        for nb in range(2):
            ps = PS[5 + rot["o"] % 2]
            rot["o"] += 1
            for c in range(8):
                P(lambda: nc.tensor.matmul(out=ps[:, :], lhsT=yT[p_][:, c, :], rhs=wo[:, c, nb * 512:(nb + 1) * 512], start=(c == 0), stop=(c == 7)),
                  [yT[p_], wo], [ps], sig=(c == 7))
            V(lambda: nc.vector.tensor_tensor(out=xm[:, nb * 512:(nb + 1) * 512], in0=ps[:, :], in1=xt[:, nb * 512:(nb + 1) * 512], op=ALU.add),
              [ps, xt], [xm])
        D([(out_d[oi], xm[:])], r=[xm], w=[xmid_db[oi]])
        yield

    pipeline(range(NOWN), c1_tile, 6)
    T.barrier()
    if debug == 3:
        T.barrier()
        return nc, T, dbg
    pc1.close()

    pc2 = ExitStack()
    fup = mk(pc2, "fup", [128, 8, 2 * D_FF], BF16)
    fdn = mk(pc2, "fdn", [128, 22, 1024], BF16)
    fcw = mk(pc2, "fcw", [128, 22, 3])
    fcb = mk(pc2, "fcb", [128, 22])
    D([(fcw[:].rearrange("p a b -> p (a b)"), fcw_d)], w=[fcw])
    D([(fcb[:], fcb_d)], w=[fcb])
    with ExitStack() as stg:
        stage = [[mk(stg, "stgE%d" % i, [128, 2048]) for i in range(2)], 0]
        fup_v = fup_d.rearrange("(c p) n -> p c n", p=128)
        for o in range(0, 2 * D_FF, 256):
            load_w(stage, fup, fup[:, :, o:o + 256], fup_v[:, :, o:o + 256], [128, 8, 256],
                   scale_ap=n2w[:, :].unsqueeze(2).broadcast_to([128, 8, 256]), scale_tb=n2w)
        fdn_v = fdn_d.rearrange("(c p) n -> p c n", p=128)
        for c0 in range(0, 22, 2):
            load_w(stage, fdn, fdn[:, c0:c0 + 2, :], fdn_v[:, c0:c0 + 2, :], [128, 2, 1024], eng="pool")
        T.barrier()
    xmh = mk(pc2, "xmh", [128, 1024])
    xmd = [mk(pc2, "xmd%d" % i, [128, 1024]) for i in range(2)]
    sse = mk(pc2, "sse", [128, 1])
    rse_ = mk(pc2, "rse", [128, 1])
    xne = mk(pc2, "xne", [128, 1024], BF16)
    TS = 4
    h2T2 = [mk(pc2, "h2T%d" % i, [128, 8, TS * 128], BF16) for i in range(2)]
    cstate = mk(pc2, "cstate", [128, 22, 2])
    V(lambda: nc.vector.memset(cstate[:], 0.0), [], [cstate])
    ab = [mk(pc2, "ab%d" % i, [128, 2 + TS * 128]) for i in range(2)]
    acc = [mk(pc2, "acc%d" % i, [128, TS * 128]) for i in range(2)]
    gl = [mk(pc2, "gl%d" % i, [128, TS * 128]) for i in range(2)]
    uT = mk(pc2, "uT", [128, 22, TS * 128], BF16)
    groups = []
    o0 = 0
    if RESET_TILE is not None:
        groups.append([0])
        o0 = 1
    for a_ in range(o0, NOWN, TS):
        groups.append(list(range(a_, min(NOWN, a_ + TS))))
    cnt2 = {"x": 0, "c": 0}

    def c2_head(gi):
        grp = groups[gi]
        h2T = h2T2[gi % 2]
        for j, oi in enumerate(grp):
            D([(xmh[:], out_d[oi])], r=[xmid_db[oi]], w=[xmh])
            A(lambda: nc.scalar.activation(out=xne[:], in_=xmh[:], func=AF.Square, accum_out=sse[:]), [xmh], [xne, sse])
            yield
            rstd_pow(rse_, sse, 1024)
            yield
            A(lambda: nc.scalar.activation(out=xne[:], in_=xmh[:], func=AF.Copy, scale=rse_[:]), [xmh, rse_], [xne])
            yield
            pv = bfv(psT, 1024)
            for c in range(8):
                P(lambda: nc.tensor.transpose(out=pv[:, c * 128:(c + 1) * 128], in_=xne[:, c * 128:(c + 1) * 128], identity=identb[:]),
                  [xne, identb], [psT], sig=(c == 7))
            A(lambda: nc.scalar.copy(out=h2T[:, :, j * 128:(j + 1) * 128], in_=pv.rearrange("p (c t) -> p c t", c=8)), [psT], [h2T])
            yield

    def c2_body(gi):
        grp = groups[gi]
        h2T = h2T2[gi % 2]
        nt = len(grp)
        W = nt * 128
        for cch in range(22):
            ci = cnt2["c"]
            cnt2["c"] += 1
            psa = PS[1 + ci % 2]
            psv = PS[3 + ci % 2]
            ab_ = ab[ci % 2]
            acc_ = acc[ci % 2]
            gl_ = gl[ci % 2]
            for c in range(8):
                P(lambda: nc.tensor.matmul(out=psa[:, 0:W], lhsT=fup[:, c, cch * 128:(cch + 1) * 128], rhs=h2T[:, c, 0:W],
                                           start=(c == 0), stop=(c == 7)), [fup, h2T], [psa], sig=(c == 7))
            for c in range(8):
                P(lambda: nc.tensor.matmul(out=psv[:, 0:W], lhsT=fup[:, c, D_FF + cch * 128:D_FF + (cch + 1) * 128], rhs=h2T[:, c, 0:W],
                                           start=(c == 0), stop=(c == 7)), [fup, h2T], [psv], sig=(c == 7))
            G(lambda: nc.gpsimd.tensor_copy(out=ab_[:, 0:2], in_=cstate[:, cch, :]), [cstate], [ab_])
            A(lambda: nc.scalar.copy(out=ab_[:, 2:2 + W], in_=psa[:, 0:W]), [psa], [ab_])
            V(lambda: nc.vector.tensor_scalar(out=acc_[:, 0:W], in0=ab_[:, 0:W], scalar1=fcw[:, cch, 0:1], scalar2=None, op0=ALU.mult),
              [ab_, fcw], [acc_])
            for k in (1, 2):
                V(lambda: nc.vector.scalar_tensor_tensor(out=acc_[:, 0:W], in0=ab_[:, k:k + W], scalar=fcw[:, cch, k:k + 1],
                                                         in1=acc_[:, 0:W], op0=ALU.mult, op1=ALU.add), [ab_, fcw, acc_], [acc_])
            G(lambda: nc.gpsimd.tensor_copy(out=cstate[:, cch, :], in_=ab_[:, W:W + 2]), [ab_], [cstate])
            A(lambda: nc.scalar.activation(out=gl_[:, 0:W], in_=acc_[:, 0:W], func=AF.Gelu, bias=fcb[:, cch:cch + 1]), [acc_, fcb], [gl_])
            V(lambda: nc.vector.tensor_tensor(out=uT[:, cch, 0:W], in0=psv[:, 0:W], in1=gl_[:, 0:W], op=ALU.mult), [psv, gl_], [uT])
            yield
        if RESET_TILE is not None and grp == [0]:
            G(lambda: nc.gpsimd.tensor_scalar(out=cstate[:], in0=cstate[:], scalar1=flag, scalar2=None, op0=ALU.mult), [cstate, cst], [cstate])
        for j, oi in enumerate(grp):
            xm = xmd[cnt2["x"] % 2]
            cnt2["x"] += 1
            D([(xm[:], out_d[oi])], r=[xmid_db[oi]], w=[xm])
            for nb in range(2):
                ps = PS[5 + nb]
                for cch in range(22):
                    P(lambda: nc.tensor.matmul(out=ps[:, :], lhsT=uT[:, cch, j * 128:(j + 1) * 128], rhs=fdn[:, cch, nb * 512:(nb + 1) * 512],
                                               start=(cch == 0), stop=(cch == 21)), [uT, fdn], [ps], sig=(cch == 21))
                V(lambda: nc.vector.tensor_tensor(out=xm[:, nb * 512:(nb + 1) * 512], in0=ps[:, :], in1=xm[:, nb * 512:(nb + 1) * 512], op=ALU.add),
                  [ps, xm], [xm])
            D([(out_d[oi], xm[:])], r=[xm], w=[xmid_db[oi]])
            yield

    for _ in c2_head(0):
        pass
    for gi in range(len(groups)):
        nh = c2_head(gi + 1) if gi + 1 < len(groups) else None
        for _ in c2_body(gi):
            if nh is not None and next(nh, "end") == "end":
                nh = None
        if nh is not None:
            for _ in nh:
                pass
    T.barrier()
    pc2.close()
    return nc, T, dbg
    return nc, T, dbg


def prep_core(inp, b, h, NT, QLO, padded):
    S = NT * 128
    f32 = np.float32
    g = lambda k: np.asarray(inp[k], dtype=f32)[0]
    xb = np.asarray(inp["x"], dtype=f32)[b]
    if padded:
        half = S // 2
        if h == 0:
            x = np.concatenate([np.zeros((half, 1024), f32), xb[0:half]], 0)
            pos = np.arange(S) - half
        else:
            x = xb[0:S]
            pos = np.arange(S)
    else:
        x = xb[0:S]
        pos = np.arange(S)
    padlen = int((pos < 0).sum())
    NOWN = NT - QLO
    NCMP = 8 * NT - 1
    NCC = (8 * NT + 127) // 128
    NCPAD = NCC * 128
    NSEL = 2 * NT
    d = {"x": np.ascontiguousarray(x)}
    posc = np.maximum(pos, 0).astype(f32)
    inv = (f32(10000.0) ** (-np.arange(0, 64, 2, dtype=f32) / f32(64))).astype(f32)
    ang = (posc[:, None] * inv[None, :]).astype(f32)
    d["cos"] = np.cos(ang).astype(f32)
    d["sin"] = np.sin(ang).astype(f32)
    d["w_in"] = g("w_in")
    d["n1w"] = np.ascontiguousarray(g("norm1_w").reshape(8, 128).T)
    d["n2w"] = np.ascontiguousarray(g("norm2_w").reshape(8, 128).T)
    tabA = np.zeros((TA_N,), f32)
    tabA[TA_KNW:TA_KNW + 384] = np.concatenate([np.tile(g("kcmp_norm_w"), 2), np.tile(g("kslc_norm_w"), 2), np.tile(g("kwin_norm_w"), 2)])
    tabA[TA_QNW:TA_QNW + 512] = np.tile(g("q_norm_w"), 8)
    tabA[TA_BIG:TA_BIG + 4] = g("m_igate_b")
    tabA[TA_BFG:TA_BFG + 4] = g("m_fgate_b")
    tabA[TA_MONW:TA_MONW + 512] = g("m_out_norm_w").reshape(-1)
    d["tabA"] = np.ascontiguousarray(np.tile(tabA[None, :], (128, 1)))
    mcw = g("m_conv_w")
    mcb = g("m_conv_b")
    ch_of = [256 + np.arange(128), 384 + np.arange(128), np.arange(128), 128 + np.arange(128)]
    cw = np.zeros((128, 4, 4), f32)
    cb = np.zeros((128, 4), f32)
    for c in range(4):
        cw[:, c, :] = mcw[:, ch_of[c]].T
        cb[:, c] = mcb[ch_of[c]]
    d["cw"] = cw.reshape(128, 16)
    d["cb"] = cb
    fw = g("ffn_conv_w")
    d["fcw"] = np.ascontiguousarray(fw.reshape(3, 22, 128).transpose(2, 1, 0).reshape(128, 66))
    d["fcb"] = np.ascontiguousarray(g("ffn_conv_b").reshape(22, 128).T)
    d["mgb"] = g("merge_gate_b").reshape(1, 2048)
    kpe, vpe = g("cmp_k_pe"), g("cmp_v_pe")
    d["pe2"] = np.ascontiguousarray(np.concatenate([kpe, kpe, vpe, vpe], 1))
    p = np.arange(128)
    cst = np.zeros((128, 128 * 3 + 2 + 64 + 1), f32)
    cst[:, 0:128] = np.eye(128)
    cst[:, 128:256] = ((p[:, None] // 64 == p[None, :] // 64) & (p[:, None] <= p[None, :]))
    cst[:, 256:384] = 1.0
    cst[:, 384] = p < 64
    cst[:, 385] = p >= 64
    cst[:, 386:450] = (p[:, None] % 64) <= np.arange(64)[None, :]
    cst[:, 450] = 0.0 if (padded and h == 0) else 1.0
    d["cst"] = cst
    n = np.arange(NCPAD)
    j = np.arange(NSEL)
    ov = ((n[:, None] * 16 <= j[None, :] * 64 + 63) & (n[:, None] * 16 + 31 >= j[None, :] * 64) & (n[:, None] < NCMP)).astype(f32)
    d["ovl"] = np.ascontiguousarray(ov.reshape(NCC, 128, NSEL).transpose(1, 0, 2).reshape(128, NCC * NSEL))
    kk = np.arange(S)
    d["ewin"] = ((kk[None, :] // 64) == (64 * (kk[None, :] // 4096) + np.arange(64)[:, None])).astype(f32)
    diag = np.where(p[:, None] > p[None, :], NEGB, 0.0).astype(f32)
    anti = np.where(p[:, None] <= p[None, :], NEGB, 0.0).astype(f32)
    d["cbm"] = np.ascontiguousarray(np.concatenate([np.tile(diag, (1, 4)), np.tile(anti, (1, 4))], 1))
    tpos = pos[QLO * 128:].reshape(NOWN, 128)
    cend_real = 16 * n + 31 - padlen
    cvalid = (16 * n >= padlen) & (n < NCMP)
    d["cmask"] = np.where(cvalid[None, None, :] & (cend_real[None, None, :] <= tpos[:, :, None]), 0.0, NEGB * 8).astype(ml_dtypes.bfloat16)
    jr = j - padlen // 64
    cur = tpos // 64
    forced = (jr[None, None, :] == 0) | (jr[None, None, :] == cur[:, :, None]) | (jr[None, None, :] == cur[:, :, None] - 1)
    bad = (jr[None, None, :] > cur[:, :, None]) | (jr[None, None, :] < 0)
    d["imptab"] = np.where(bad, -1e30, np.where(forced, 1e4, 0.0)).astype(f32)
    d["rv"] = np.ascontiguousarray((tpos >= 31).astype(f32).T)
    d["kb"] = np.ascontiguousarray(np.where(pos.reshape(NT, 128).T < 0, NEGB, 0.0).astype(f32))
    for k in ("cmp_k_w1", "cmp_k_w2", "cmp_v_w1", "cmp_v_w2", "w_up_m", "w_up_n", "w_out", "ffn_w_up", "ffn_w_down"):
        d[k] = g(k)
    return d


NT_FULL, QLO_FULL, RESET_FULL = 64, 31, 32
_CACHE = {}


def kernel(**inputs):
    B = int(np.asarray(inputs["x"]).shape[0])
    if "nc" not in _CACHE:
        _CACHE["nc"] = build(NT_FULL, QLO_FULL, RESET_FULL)[0]
    nc = _CACHE["nc"]
    in_maps = []
    for b in range(B):
        for h in range(2):
            in_maps.append(prep_core(inputs, b, h, NT_FULL, QLO_FULL, True))
    res = run_bass_kernel_spmd(nc, in_maps, core_ids=list(range(2 * B)))
    out = np.zeros((B, NT_FULL * 128, 1024), np.float32)
    half = NT_FULL * 64
    for b in range(B):
        for h in range(2):
            o = np.asarray(res.results[b * 2 + h]["out"], dtype=np.float32).reshape(-1, 1024)
            out[b, h * half:(h + 1) * half] = o[128:128 + half]
    return out
```

```python
import math
from contextlib import ExitStack
import numpy as np
import ml_dtypes
import concourse.bass as bass
import concourse.mybir as mybir
from concourse.bass_utils import run_bass_kernel_spmd

F32 = mybir.dt.float32
BF16 = mybir.dt.bfloat16
AF = mybir.ActivationFunctionType
ALU = mybir.AluOpType
AX = mybir.AxisListType
EPS = 1e-6
NEGB = -30000.0


class Buf:
    __slots__ = ("name", "w", "r", "dsem", "dcnt", "excl", "rg")

    def __init__(self, name, excl=False):
        self.name = name
        self.rg = None
        self.excl = excl
        self.w = None
        self.r = {}
        self.dsem = None
        self.dcnt = 0


class Trk:
    ENG = ("pe", "act", "dve", "pool", "sp")

    def __init__(self, nc):
        self.nc = nc
        self.e = {"pe": nc.tensor, "act": nc.scalar, "dve": nc.vector,
                  "pool": nc.gpsimd, "sp": nc.sync}
        self.sems = {}
        self.cnt = {}
        for k in ("pe", "act", "dve", "pool"):
            self.sems[k] = nc.alloc_semaphore("s_" + k)
            self.cnt[k] = 0
        self.known = {k: {} for k in self.ENG}
        self.nd = 0
        self.dbufs = []
        self.ninstr = 0

    def _wait(self, eng, deps):
        kn = self.known[eng]
        best = {}
        for (s, v) in deps:
            if s == eng and v > self.cnt[eng]:
                continue
            if kn.get(s, 0) < v and best.get(s, 0) < v:
                best[s] = v
        for s, v in best.items():
            self.e[eng].wait_ge(self.sems[s], v)
            kn[s] = v

    @staticmethod
    def _deps(reads, writes):
        deps = []
        for b in reads:
            if b.w is not None:
                deps.append(b.w)
        for b in writes:
            if b.w is not None:
                deps.append(b.w)
            deps.extend(b.r.items())
        return deps

    def op(self, eng, fn, reads=(), writes=(), sig=True, rg="f"):
        ex = [b for b in reads if b.excl]
        if ex:
            reads = [b for b in reads if not b.excl]
            writes = list(writes) + ex
        deps = self._deps(reads, writes)
        if eng == "pe":
            drop = set()
            for b in writes:
                if b.w is not None and b.w[0] == "pe" and not ({b.rg, rg} == {0, 64}):
                    drop.add(b.w)
            keep = set()
            for b in writes:
                if b.w is not None and b.w[0] == "pe" and ({b.rg, rg} == {0, 64}):
                    keep.add(b.w)
                for it in b.r.items():
                    if it[0] == "pe":
                        keep.add(it)
            for b in reads:
                if b.w is not None and b.w[0] == "pe":
                    keep.add(b.w)
            deps = [d_ for d_ in deps if not (d_ in drop and d_ not in keep)]
            for b in writes:
                b.rg = rg
        self._wait(eng, deps)
        ins = fn()
        self.ninstr += 1
        if sig:
            self.cnt[eng] += 1
            ins.then_inc(self.sems[eng], 1)
            v = self.cnt[eng]
        else:
            v = self.cnt[eng] + 1
        for b in reads:
            b.r[eng] = v
        for b in writes:
            b.w = (eng, v)
            b.r = {}
        return ins

    def dma(self, q, pairs, reads=(), writes=(), sembuf=None):
        sb = sembuf if sembuf is not None else (writes[0] if writes else reads[0])
        if sb.dsem is None:
            key = "d%d" % self.nd
            self.nd += 1
            sb.dsem = key
            self.sems[key] = self.nc.alloc_semaphore(key)
            self.dbufs.append(sb)
        deps = self._deps(reads, writes)
        if sb.dcnt > 0:
            deps.append((sb.dsem, sb.dcnt))
        self._wait(q, deps)
        for (o, i) in pairs:
            self.e[q].dma_start(out=o, in_=i).then_inc(self.sems[sb.dsem], 16)
            sb.dcnt += 16
            self.ninstr += 1
        ev = (sb.dsem, sb.dcnt)
        for b in reads:
            b.r[ev[0]] = ev[1]
        for b in writes:
            b.w = ev
            b.r = {}
        return ev

    def barrier(self):
        deps = [(k, self.cnt[k]) for k in ("pe", "act", "dve", "pool") if self.cnt[k] > 0]
        deps += [(b.dsem, b.dcnt) for b in self.dbufs if b.dcnt > 0]
        for eng in self.ENG:
            self._wait(eng, deps)


def pipeline(items, body, skew, maxact=2):
    it = iter(items)
    act = []
    done = False
    while True:
        if not done and len(act) < maxact and (not act or act[-1][1] >= skew):
            try:
                act.append([body(next(it)), 0])
            except StopIteration:
                done = True
        if not act:
            if done:
                break
            continue
        for a in list(act):
            try:
                next(a[0])
                a[1] += 1
            except StopIteration:
                act.remove(a)


class TB:
    def __init__(self, t, name, excl=False):
        self.t = t
        self.b = Buf(name, excl)

    def __getitem__(self, idx):
        return self.t[idx]


IN_OFF = {}
_o = 0
for _n, _s in (("mq", 256), ("mk", 256), ("mv", 512), ("mo", 512), ("mi", 4), ("mf", 4),
               ("nq", 512), ("kc", 128), ("vc", 128), ("ks", 128), ("vs", 128), ("kw", 128),
               ("vw", 128), ("ng", 24), ("gm", 1024), ("gn", 1024)):
    IN_OFF[_n] = (_o, _s)
    _o += _s
D_IN = _o
D_FF = 2816

WA = {}
_o = 0
for _n in ("mv", "kc", "ks", "kw", "mi", "mf", "vs", "vw", "mo", "mk", "mq", "vc"):
    WA[_n] = _o
    _o += IN_OFF[_n][1]
WA_N = _o
TA_KNW, TA_QNW, TA_BIG, TA_BFG, TA_MONW, TA_N = 0, 384, 896, 900, 904, 1416


def build(NT, QLO, RESET_TILE, debug=0):
    S = NT * 128
    NOWN = NT - QLO
    NCMP = 8 * NT - 1
    NCC = (8 * NT + 127) // 128
    NCPAD = NCC * 128
    NSEL = 2 * NT
    nc = bass.Bass("TRN2", target_bir_lowering=False)
    T = Trk(nc)

    def din(name, shape, dt=F32):
        return nc.dram_tensor(name, list(shape), dt, kind="ExternalInput").ap()

    x_d = din("x", [S, 1024])
    cos_d = din("cos", [S, 32])
    sin_d = din("sin", [S, 32])
    w_in_d = din("w_in", [1024, D_IN])
    n1w_d = din("n1w", [128, 8])
    n2w_d = din("n2w", [128, 8])
    tabA_d = din("tabA", [128, TA_N])
    cw_d = din("cw", [128, 16])
    cb_d = din("cb", [128, 4])
    fcw_d = din("fcw", [128, 66])
    fcb_d = din("fcb", [128, 22])
    mgb_d = din("mgb", [1, 2048])
    pe2_d = din("pe2", [32, 256])
    cst_d = din("cst", [128, 128 * 3 + 2 + 64 + 1])
    ovl_d = din("ovl", [128, NCC * NSEL])
    ewin_d = din("ewin", [64, S])
    cbm_d = din("cbm", [128, 1024])
    cmask_d = din("cmask", [NOWN, 128, NCPAD], BF16)
    imptab_d = din("imptab", [NOWN, 128, NSEL])
    kb_d = din("kb", [128, NT])
    rv_d = din("rv", [128, NOWN])
    w1k_d = din("cmp_k_w1", [2048, 256])
    w2k_d = din("cmp_k_w2", [256, 64])
    w1v_d = din("cmp_v_w1", [2048, 256])
    w2v_d = din("cmp_v_w2", [256, 64])
    wupm_d = din("w_up_m", [512, 1024])
    wupn_d = din("w_up_n", [512, 1024])
    wout_d = din("w_out", [1024, 1024])
    fup_d = din("ffn_w_up", [1024, 2 * D_FF])
    fdn_d = din("ffn_w_down", [D_FF, 1024])
    out_d = nc.dram_tensor("out", [NOWN, 128, 1024], F32, kind="ExternalOutput").ap()
    out_b = Buf("out")
    hmT_d = nc.dram_tensor("hmT_scr", [NOWN, 128, 512], BF16, kind="Internal").ap()
    onT_d = nc.dram_tensor("onT_scr", [NOWN, 128, 512], BF16, kind="Internal").ap()
    kcs_d = nc.dram_tensor("kcs_scr", [128, S], BF16, kind="Internal").ap()
    vcs_d = nc.dram_tensor("vcs_scr", [128, S], BF16, kind="Internal").ap()
    kvs_db = Buf("kvs_scr")
    hmT_db = [Buf("hmTd%d" % i) for i in range(NOWN)]
    onT_db = [Buf("onTd%d" % i) for i in range(NOWN)]
    xmid_db = [Buf("xmid%d" % i) for i in range(NOWN)]
    dbg = {}
    if debug:
        def dout(name, shape, dt=F32):
            dbg[name] = (nc.dram_tensor(name, list(shape), dt, kind="ExternalOutput").ap(), Buf(name))
            return dbg[name]

    def V(fn, r=(), w=()):
        return T.op("dve", fn, [a.b for a in r], [a.b for a in w])

    def A(fn, r=(), w=()):
        return T.op("act", fn, [a.b for a in r], [a.b for a in w])

    def G(fn, r=(), w=()):
        return T.op("pool", fn, [a.b for a in r], [a.b for a in w])

    def P(fn, r=(), w=(), sig=True, rg="f"):
        return T.op("pe", fn, [a.b for a in r], [a.b for a in w], sig, rg)

    def D(pairs, r=(), w=(), q="sp", sembuf=None):
        if sembuf is None:
            cand = [a for a in list(w) + list(r) if isinstance(a, TB)]
            sembuf = cand[0].b if cand else None
        return T.dma(q, pairs, [a if isinstance(a, Buf) else a.b for a in r],
                     [a if isinstance(a, Buf) else a.b for a in w], sembuf)

    main = ExitStack()

    def mk(es, name, shape, dt=F32):
        return TB(es.enter_context(nc.sbuf_tensor("sb_" + name, list(shape), dt)), name)

    PS = [TB(nc.alloc_psum_tensor("ps%d" % i, [128, 512], F32), "ps%d" % i, True) for i in range(8)]

    def bfv(ps, ncols):
        return ps[:, 0:ncols // 2].bitcast(BF16)

    cst = mk(main, "cst", [128, 128 * 3 + 2 + 64 + 1])
    D([(cst[:], cst_d)], w=[cst])
    identf = cst[:, 0:128]
    U2 = cst[:, 128:256]
    onesf = cst[:, 256:384]
    m01 = cst[:, 384:386]
    mask_st = cst[:, 386:450]
    flag = cst[:, 450:451]
    identb = mk(main, "identb", [128, 128], BF16)
    V(lambda: nc.vector.tensor_copy(out=identb[:], in_=identf), [cst], [identb])
    onesb = mk(main, "onesb", [128, 128], BF16)
    V(lambda: nc.vector.tensor_copy(out=onesb[:], in_=onesf), [cst], [onesb])
    tabA = mk(main, "tabA", [128, TA_N])
    D([(tabA[:], tabA_d)], w=[tabA])
    n1w = mk(main, "n1w", [128, 8])
    D([(n1w[:], n1w_d)], w=[n1w])
    n2w = mk(main, "n2w", [128, 8])
    D([(n2w[:], n2w_d)], w=[n2w])
    kbias = mk(main, "kbias", [128, NT])
    D([(kbias[:], kb_d)], w=[kbias])
    rvt = mk(main, "rvt", [128, NOWN])
    D([(rvt[:], rv_d)], w=[rvt])

    def load_w(es_stage, dst_tb, dst_ap, src_ap, shape, scale_ap=None, eng="dve", scale_tb=None):
        st = es_stage[0][es_stage[1] % len(es_stage[0])]
        es_stage[1] += 1
        if len(shape) == 2:
            sv = st[:, 0:shape[1]]
        else:
            sv = st[:, 0:shape[1] * shape[2]].rearrange("p (a n) -> p a n", a=shape[1])
        D([(sv, src_ap)], w=[st])
        if scale_ap is None:
            if eng == "dve":
                V(lambda: nc.vector.tensor_copy(out=dst_ap, in_=sv), [st], [dst_tb])
            elif eng == "act":
                A(lambda: nc.scalar.copy(out=dst_ap, in_=sv), [st], [dst_tb])
            else:
                G(lambda: nc.gpsimd.tensor_copy(out=dst_ap, in_=sv), [st], [dst_tb])
        else:
            V(lambda: nc.vector.tensor_tensor(out=dst_ap, in0=sv, in1=scale_ap, op=ALU.mult), [st, scale_tb], [dst_tb])

    nhalf = mk(main, "nhalf", [128, 8])
    G(lambda: nc.gpsimd.memset(nhalf[:], -0.5), [], [nhalf])

    def rstd_pow(rs, ss, n):
        w = ss.t.shape[1]
        G(lambda: nc.gpsimd.tensor_scalar(out=rs[:], in0=ss[:], scalar1=1.0 / n, scalar2=EPS, op0=ALU.mult, op1=ALU.add), [ss], [rs])
        G(lambda: nc.gpsimd.tensor_tensor(out=rs[:], in0=rs[:], in1=nhalf[:, 0:w], op=ALU.pow), [rs, nhalf], [rs])

    def rmsnorm_T(xt, tmp, ss, rs, xn, xnT, psT, evac="dve"):
        A(lambda: nc.scalar.activation(out=tmp[:], in_=xt[:], func=AF.Square, accum_out=ss[:]), [xt], [tmp, ss])
        rstd_pow(rs, ss, 1024)
        A(lambda: nc.scalar.activation(out=xn[:], in_=xt[:], func=AF.Copy, scale=rs[:]), [xt, rs], [xn])
        pv = bfv(psT, 1024)
        for c in range(8):
            P(lambda: nc.tensor.transpose(out=pv[:, c * 128:(c + 1) * 128], in_=xn[:, c * 128:(c + 1) * 128],
                                          identity=identb[:]), [xn, identb], [psT], sig=(c == 7))
        if evac == "dve":
            V(lambda: nc.vector.tensor_copy(out=xnT[:].rearrange("p c t -> p (c t)"), in_=pv), [psT], [xnT])
        else:
            A(lambda: nc.scalar.copy(out=xnT[:].rearrange("p c t -> p (c t)"), in_=pv), [psT], [xnT])

    kvs = ExitStack()
    KE = [mk(kvs, "KE%d" % g, [128, S], BF16) for g in range(2)]
    kwT = mk(kvs, "kwT", [128, S], BF16)
    vsA = mk(kvs, "vsA", [128, NT, 2, 65], BF16)
    vwA = mk(kvs, "vwA", [128, NT, 2, 65], BF16)
    kcmpT = mk(kvs, "kcmpT", [128, NCPAD], BF16)
    vcmp = mk(kvs, "vcmp", [128, NCC, 2, 64], BF16)
    G(lambda: nc.gpsimd.memset(vsA[:], 1.0), [], [vsA])
    G(lambda: nc.gpsimd.memset(vwA[:], 1.0), [], [vwA])
    G(lambda: nc.gpsimd.memset(kcmpT[:], 0.0), [], [kcmpT])
    G(lambda: nc.gpsimd.memset(vcmp[:], 0.0), [], [vcmp])

    pa0 = ExitStack()
    pa = ExitStack()
    wA = mk(pa, "wA", [128, 8, WA_N], BF16)
    with ExitStack() as stg:
        stage = [[mk(stg, "stgA%d" % i, [128, 2048]) for i in range(2)], 0]
        w_in_v = w_in_d.rearrange("(c p) n -> p c n", p=128)
        for name in ("mv", "kc", "ks", "kw", "mi", "mf", "vs", "vw", "mo", "mk", "mq", "vc"):
            so, sn = IN_OFF[name]
            do = WA[name]
            for o in range(0, sn, 256):
                n = min(256, sn - o)
                load_w(stage, wA, wA[:, :, do + o:do + o + n], w_in_v[:, :, so + o:so + o + n], [128, 8, n],
                       scale_ap=n1w[:, :].unsqueeze(2).broadcast_to([128, 8, n]), scale_tb=n1w)
        T.barrier()
    xt2 = [mk(pa, "xt%d" % i, [128, 1024]) for i in range(2)]
    two = lambda nm, shp, dt=F32: [mk(pa, "%s_%d" % (nm, i), shp, dt) for i in range(2)]
    three = lambda nm, shp, dt=F32: [mk(pa, "%s_%d" % (nm, i), shp, dt) for i in range(4)]
    ss1_2, rs1_2 = two("ss1", [128, 1]), two("rs1", [128, 1])
    xn_2 = two("xn", [128, 1024], BF16)
    xnT_2 = two("xnT", [128, 8, 128], BF16)
    cs2 = two("cs", [128, 64])
    ksq_2 = two("ksq", [128, 384])
    ss6_2, rs6_2 = two("ss6", [128, 6]), two("rs6", [128, 6])
    kn_2 = two("kn", [128, 6, 64])
    rt_2 = [two("rt%d" % i, [128, 6, 32]) for i in range(4)]
    kr_2 = two("kr", [128, 6, 64], BF16)
    convb_2 = two("convb", [128, 4, 131])
    cacc_2 = two("cacc", [128, 4, 128])
    cw = mk(pa, "cw", [128, 4, 4])
    cb = mk(pa, "cb", [128, 4])
    D([(cw[:].rearrange("p a b -> p (a b)"), cw_d)], w=[cw])
    D([(cb[:], cb_d)], w=[cb])
    zg_2, sp_2, ip_2 = two("zg", [128, 4]), two("sp", [128, 4]), two("ip", [128, 4])
    sp8_2, es_2, expg_2 = two("sp8", [128, 8]), two("es", [128, 4]), two("expg", [128, 8])
    kvst_2 = two("kvst", [128, 2, 128], BF16)
    kqT2 = three("kqT", [128, 4, 128], BF16)
    ktil2 = three("ktil", [128, 4, 64], BF16)
    vaug2 = three("vaug", [128, 4, 129], BF16)
    esm2 = three("esm", [128, 4, 64])
    eb82 = three("eb8", [128, 4])
    egp2 = three("egp", [128, 2, 2])
    sigo2 = three("sigo", [128, 512])
    for v_ in vaug2:
        G(lambda: nc.gpsimd.memset(v_[:], 1.0), [], [v_])
    for c_ in convb_2:
        V(lambda: nc.vector.memset(c_[:], 0.0), [], [c_])
    Cst = mk(pa, "Cst", [128, 2, 129])
    snap = [mk(pa, "snap%d" % i, [128, 2, 129], BF16) for i in range(8)]
    V(lambda: nc.vector.memset(Cst[:], 0.0), [], [Cst])
    V(lambda: nc.vector.memset(snap[0][:], 0.0), [], [snap[0]])
    PT = mk(pa, "PT", [128, 4, 64], BF16)
    d4 = [mk(pa, "d4_%d" % i, [128, 4]) for i in range(3)]
    hraw = mk(pa, "hraw", [128, 4, 128])
    ss4 = mk(pa, "ss4", [128, 4])
    rs4 = mk(pa, "rs4", [128, 4])
    hm = mk(pa, "hm", [128, 512], BF16)
    hmTs = mk(pa, "hmTs", [128, 512], BF16)
    lnc = math.log(0.125)
    psT, pA_, pB_, psS, psKV, psSTm, psO0, psO1 = PS
    prot = {"i": 0}

    def a_front(t):
        own = t >= QLO
        needq = t >= QLO - 1
        p_ = t % 2
        h_ = t % 4
        xt, cs = xt2[p_], cs2[p_]
        vaug, kqT, ktil, esm, eb8, egp, sigo = vaug2[h_], kqT2[h_], ktil2[h_], esm2[h_], eb82[h_], egp2[h_], sigo2[h_]
        ss1, rs1, xn, xnT, ksq, ss6, rs6, kn, kr = ss1_2[p_], rs1_2[p_], xn_2[p_], xnT_2[p_], ksq_2[p_], ss6_2[p_], rs6_2[p_], kn_2[p_], kr_2[p_]
        rt = [rt_2[i][p_] for i in range(4)]
        convb, convn, cacc = convb_2[p_], convb_2[1 - p_], cacc_2[p_]
        zg, sp, ip, sp8, es_, expg, kvst = zg_2[p_], sp_2[p_], ip_2[p_], sp8_2[p_], es_2[p_], expg_2[p_], kvst_2[p_]
        sqj = xn
        D([(xt[:], x_d[t * 128:(t + 1) * 128, :])], w=[xt])
        D([(cs[:, 0:32], cos_d[t * 128:(t + 1) * 128, :]), (cs[:, 32:64], sin_d[t * 128:(t + 1) * 128, :])], w=[cs])
        A(lambda: nc.scalar.activation(out=sqj[:], in_=xt[:], func=AF.Square, accum_out=ss1[:]), [xt], [sqj, ss1])
        yield
        rstd_pow(rs1, ss1, 1024)
        yield
        A(lambda: nc.scalar.activation(out=xn[:], in_=xt[:], func=AF.Copy, scale=rs1[:]), [xt, rs1], [xn])
        yield
        pv = bfv(psT, 1024)
        for c in range(8):
            P(lambda: nc.tensor.transpose(out=pv[:, c * 128:(c + 1) * 128], in_=xn[:, c * 128:(c + 1) * 128], identity=identb[:]),
              [xn, identb], [psT], sig=(c == 7))
        V(lambda: nc.vector.tensor_copy(out=xnT[:].rearrange("p c t -> p (c t)"), in_=pv), [psT], [xnT])
        yield

        def bank():
            prot["i"] += 1
            return (pA_, pB_)[prot["i"] % 2]

        def proj_tm(ps, col0, ncols, wcol):
            for c in range(8):
                P(lambda: nc.tensor.matmul(out=ps[:, col0:col0 + ncols], lhsT=xnT[:, c, :], rhs=wA[:, c, wcol:wcol + ncols],
                                           start=(c == 0), stop=(c == 7)), [xnT, wA], [ps], sig=(c == 7))

        def proj_fm(ps, col0, wcol):
            for c in range(8):
                P(lambda: nc.tensor.matmul(out=ps[:, col0:col0 + 128], lhsT=wA[:, c, wcol:wcol + 128], rhs=xnT[:, c, :],
                                           start=(c == 0), stop=(c == 7)), [xnT, wA], [ps], sig=(c == 7))

        psK = bank()
        proj_tm(psK, 0, 392, WA["kc"])
        A(lambda: nc.scalar.activation(out=ksq[:], in_=psK[:, 0:384], func=AF.Square), [psK], [ksq])
        V(lambda: nc.vector.tensor_copy(out=kn[:].rearrange("p a d -> p (a d)"), in_=psK[:, 0:384]), [psK], [kn])
        V(lambda: nc.vector.tensor_tensor(out=zg[:], in0=psK[:, 388:392], in1=tabA[:, TA_BFG:TA_BFG + 4], op=ALU.add), [psK, tabA], [zg])
        V(lambda: nc.vector.tensor_tensor(out=ip[:], in0=psK[:, 384:388], in1=tabA[:, TA_BIG:TA_BIG + 4], op=ALU.add), [psK, tabA], [ip])
        yield
        psMV = bank()
        proj_tm(psMV, 0, 512, WA["mv"])
        A(lambda: nc.scalar.copy(out=vaug[:, :, 0:128], in_=psMV[:, :].rearrange("p (h d) -> p h d", h=4)), [psMV], [vaug])
        yield
        psV = bank()
        proj_tm(psV, 0, 256, WA["vs"])
        proj_fm(psV, 256, WA["vc"])
        tsl = slice(t * 128, (t + 1) * 128)
        A(lambda: nc.scalar.copy(out=vsA[:, t, :, 0:64], in_=psV[:, 0:128].rearrange("p (g d) -> p g d", g=2)), [psV], [vsA])
        A(lambda: nc.scalar.copy(out=vwA[:, t, :, 0:64], in_=psV[:, 128:256].rearrange("p (g d) -> p g d", g=2)), [psV], [vwA])
        V(lambda: nc.vector.tensor_copy(out=kvst[:, 1, :], in_=psV[:, 256:384]), [psV], [kvst])
        yield
        psF = bank()
        proj_fm(psF, 0, WA["mk"])
        proj_fm(psF, 128, WA["mk"] + 128)
        if needq:
            proj_fm(psF, 256, WA["mq"])
            proj_fm(psF, 384, WA["mq"] + 128)
        nch = 4 if needq else 2
        A(lambda: nc.scalar.copy(out=convb[:, 0:nch, 3:131], in_=psF[:, 0:nch * 128].rearrange("p (a t) -> p a t", a=nch)), [psF], [convb])
        yield
        if own:
            psMO = bank()
            proj_tm(psMO, 0, 512, WA["mo"])
            A(lambda: nc.scalar.activation(out=sigo[:], in_=psMO[:, 0:512], func=AF.Sigmoid), [psMO], [sigo])
            yield
        def chain_keys():
            V(lambda: nc.vector.tensor_reduce(out=ss6[:], in_=ksq[:].rearrange("p (a d) -> p a d", a=6), axis=AX.X, op=ALU.add), [ksq], [ss6])
            yield
            rstd_pow(rs6, ss6, 64)
            yield
            V(lambda: nc.vector.tensor_tensor(out=kn[:], in0=kn[:], in1=rs6[:, :].unsqueeze(2).broadcast_to([128, 6, 64]), op=ALU.mult), [kn, rs6], [kn])
            V(lambda: nc.vector.tensor_tensor(out=kn[:], in0=kn[:], in1=tabA[:, TA_KNW:TA_KNW + 384].rearrange("p (a d) -> p a d", a=6),
                                              op=ALU.mult), [kn, tabA], [kn])
            yield
            cosb = cs[:, 0:32].unsqueeze(1).broadcast_to([128, 6, 32])
            sinb = cs[:, 32:64].unsqueeze(1).broadcast_to([128, 6, 32])
            V(lambda: nc.vector.tensor_tensor(out=rt[0][:], in0=kn[:, :, 0:32], in1=cosb, op=ALU.mult), [kn, cs], [rt[0]])
            G(lambda: nc.gpsimd.tensor_tensor(out=rt[1][:], in0=kn[:, :, 32:64], in1=sinb, op=ALU.mult), [kn, cs], [rt[1]])
            V(lambda: nc.vector.tensor_tensor(out=rt[2][:], in0=kn[:, :, 32:64], in1=cosb, op=ALU.mult), [kn, cs], [rt[2]])
            G(lambda: nc.gpsimd.tensor_tensor(out=rt[3][:], in0=kn[:, :, 0:32], in1=sinb, op=ALU.mult), [kn, cs], [rt[3]])
            yield
            V(lambda: nc.vector.tensor_tensor(out=kr[:, :, 0:32], in0=rt[0][:], in1=rt[1][:], op=ALU.subtract), [rt[0], rt[1]], [kr])
            G(lambda: nc.gpsimd.tensor_tensor(out=kr[:, :, 32:64], in0=rt[2][:], in1=rt[3][:], op=ALU.add), [rt[2], rt[3]], [kr])
            yield
            pk = bfv(psS, 512)
            P(lambda: nc.tensor.transpose(out=pk[:, 0:128], in_=kr[:, 0:2, :].rearrange("p a d -> p (a d)"), identity=identb[:]), [kr, identb], [psS], sig=False)
            P(lambda: nc.tensor.transpose(out=pk[:, 128:256], in_=kr[:, 4:6, :].rearrange("p a d -> p (a d)"), identity=identb[:]), [kr, identb], [psS], sig=False)
            for g in range(2):
                P(lambda: nc.tensor.transpose(out=pk[0:64, 256 + g * 128:256 + (g + 1) * 128], in_=kr[:, 2 + g, :], identity=identb[:]), [kr, identb], [psS],
                  sig=(g == 1))
            A(lambda: nc.scalar.copy(out=kvst[:, 0, :], in_=pk[:, 0:128]), [psS], [kvst])
            A(lambda: nc.scalar.copy(out=kwT[:, tsl], in_=pk[:, 128:256]), [psS], [kwT])
            for g in range(2):
                V(lambda: nc.vector.tensor_copy(out=KE[g][0:64, tsl], in_=pk[0:64, 256 + g * 128:256 + (g + 1) * 128]), [psS], [KE[g]])
            D([(kcs_d[:, tsl], kvst[:, 0, :]), (vcs_d[:, tsl], kvst[:, 1, :])], r=[kvst], w=[kvs_db])
            yield

        def chain_conv():
            for j in range(nch):
                V(lambda: nc.vector.tensor_scalar(out=cacc[:, j, :], in0=convb[:, j, 0:128], scalar1=cw[:, j, 0:1], scalar2=None, op0=ALU.mult),
                  [convb, cw], [cacc])
                for k in range(1, 4):
                    V(lambda: nc.vector.scalar_tensor_tensor(out=cacc[:, j, :], in0=convb[:, j, k:k + 128], scalar=cw[:, j, k:k + 1],
                                                             in1=cacc[:, j, :], op0=ALU.mult, op1=ALU.add), [convb, cw, cacc], [cacc])
                A(lambda: nc.scalar.activation(out=kqT[:, j, :], in_=cacc[:, j, :], func=AF.Silu, bias=cb[:, j:j + 1]), [cacc, cb], [kqT])
                yield
            G(lambda: nc.gpsimd.tensor_copy(out=convn[:, 0:nch, 0:3], in_=convb[:, 0:nch, 128:131]), [convb], [convn])
            yield

        def chain_gates():
            A(lambda: nc.scalar.activation(out=zg[:], in_=zg[:], func=AF.Exp, scale=-1.0), [zg], [zg])
            yield
            A(lambda: nc.scalar.activation(out=sp[:], in_=zg[:], func=AF.Ln, bias=1.0), [zg], [sp])
            yield
            V(lambda: nc.vector.tensor_scalar(out=sp8[:, 0:4], in0=sp[:], scalar1=m01[:, 0:1], scalar2=None, op0=ALU.mult), [sp, cst], [sp8])
            V(lambda: nc.vector.tensor_scalar(out=sp8[:, 4:8], in0=sp[:], scalar1=m01[:, 1:2], scalar2=None, op0=ALU.mult), [sp, cst], [sp8])
            yield

        chains = [chain_keys(), chain_conv(), chain_gates()]
        while chains:
            for g_ in list(chains):
                if next(g_, "end") == "end":
                    chains.remove(g_)
            yield
        P(lambda: nc.tensor.matmul(out=psS[:, 256:260], lhsT=U2, rhs=sp[:], start=True, stop=True), [cst, sp], [psS])
        P(lambda: nc.tensor.matmul(out=psS[:, 264:272], lhsT=onesf, rhs=sp8[:], start=True, stop=True), [cst, sp8], [psS])
        pkt = psS[:, 272:400].bitcast(BF16)
        for j in range(2):
            P(lambda: nc.tensor.transpose(out=pkt[:, j * 128:(j + 1) * 128], in_=kqT[:, j, :], identity=identb[:]), [kqT, identb], [psS], sig=(j == 1))
        V(lambda: nc.vector.tensor_tensor(out=es_[:], in0=psS[:, 256:260], in1=ip[:], op=ALU.add), [psS, ip], [es_])
        A(lambda: nc.scalar.activation(out=eb8[:], in_=psS[:, 256:260], func=AF.Exp, scale=-1.0, bias=lnc), [psS], [eb8])
        A(lambda: nc.scalar.activation(out=expg[:], in_=psS[:, 264:272], func=AF.Exp, scale=-1.0), [psS], [expg])
        A(lambda: nc.scalar.activation(out=es_[:], in_=es_[:], func=AF.Exp), [es_], [es_])
        egv = expg[:, :].rearrange("p (ch c par) -> p ch c par", ch=2, c=2)
        V(lambda: nc.vector.tensor_copy(out=egp[0:64, :, :], in_=egv[0:64, :, :, 0]), [expg], [egp])
        V(lambda: nc.vector.tensor_copy(out=egp[64:128, :, :], in_=egv[64:128, :, :, 1]), [expg], [egp])
        V(lambda: nc.vector.tensor_tensor(out=ktil[:], in0=pkt.rearrange("p (h d) -> p h d", h=4),
                                          in1=es_[:, :].unsqueeze(2).broadcast_to([128, 4, 64]), op=ALU.mult), [psS, es_], [ktil])
        if own:
            V(lambda: nc.vector.tensor_tensor(out=esm[:], in0=mask_st.unsqueeze(1).broadcast_to([128, 4, 64]),
                                              in1=es_[:, :].unsqueeze(2).broadcast_to([128, 4, 64]), op=ALU.mult), [cst, es_], [esm])
        yield

    def a_scan(t):
        h_ = t % 4
        vaug, ktil, egp = vaug2[h_], ktil2[h_], egp2[h_]
        if RESET_TILE is not None and t == RESET_TILE:
            sn = snap[(2 * t) % 8]
            V(lambda: nc.vector.tensor_scalar(out=Cst[:], in0=Cst[:], scalar1=flag, scalar2=None, op0=ALU.mult), [Cst, cst], [Cst])
            V(lambda: nc.vector.tensor_scalar(out=sn[:], in0=sn[:], scalar1=flag, scalar2=None, op0=ALU.mult), [sn, cst], [sn])
        for ch in range(2):
            rows = slice(ch * 64, ch * 64 + 64)
            kvv = psKV[:, 0:258].rearrange("p (c e) -> p c e", c=2)
            for h in range(4):
                c, par = h // 2, h % 2
                P(lambda: nc.tensor.matmul(out=kvv[par * 64:(par + 1) * 64, c, :], lhsT=ktil[rows, h, :], rhs=vaug[rows, h, :],
                                           start=True, stop=True), [ktil, vaug], [psKV], sig=(h == 3), rg=ch * 64)
            V(lambda: nc.vector.tensor_tensor(out=Cst[:], in0=kvv, in1=Cst[:], op=ALU.add), [psKV, Cst], [Cst])
            yield
            V(lambda: nc.vector.tensor_tensor(out=Cst[:], in0=Cst[:], in1=egp[:, ch, :].unsqueeze(2).broadcast_to([128, 2, 129]),
                                              op=ALU.mult), [Cst, egp], [Cst])
            yield
            nx = snap[(2 * t + ch + 1) % 8]
            A(lambda: nc.scalar.copy(out=nx[:], in_=Cst[:]), [Cst], [nx])
            yield

    def a_out(t):
        h_ = t % 4
        vaug, kqT, esm, eb8, sigo = vaug2[h_], kqT2[h_], esm2[h_], eb82[h_], sigo2[h_]
        psO = (psO0, psO1)
        for ch in range(2):
            rows = slice(ch * 64, ch * 64 + 64)
            csl = slice(ch * 64, ch * 64 + 64)
            Cbf = snap[(2 * t + ch) % 8]
            stp = psSTm[:, 0:256].rearrange("p (h s) -> p h s", h=4)
            for h in range(4):
                c, par = h // 2, h % 2
                prow = slice(par * 64, par * 64 + 64)
                P(lambda: nc.tensor.matmul(out=stp[rows, h, :], lhsT=kqT[prow, c, csl], rhs=kqT[prow, 2 + c, csl],
                                           start=True, stop=True), [kqT], [psSTm], sig=True, rg=par * 64)
            V(lambda: nc.vector.tensor_tensor(out=PT[rows], in0=stp[rows], in1=esm[rows], op=ALU.mult), [psSTm, esm], [PT])
            yield
            for h in range(4):
                c, par = h // 2, h % 2
                prow = slice(par * 64, par * 64 + 64)
                ov = psO[c][:, 0:258].rearrange("p (a e) -> p a e", a=2)
                P(lambda: nc.tensor.matmul(out=ov[rows, par, :], lhsT=kqT[prow, 2 + c, csl], rhs=Cbf[prow, c, :],
                                           start=True, stop=False), [kqT, Cbf], [psO[c]], sig=True, rg=par * 64)
                P(lambda: nc.tensor.matmul(out=ov[rows, par, :], lhsT=PT[rows, h, :], rhs=vaug[rows, h, :],
                                           start=False, stop=True), [PT, vaug], [psO[c]], sig=True, rg=ch * 64)
            yield
        oi = t - QLO
        for c in range(2):
            ov = psO[c][:, 0:258].rearrange("p (a e) -> p a e", a=2)
            V(lambda: nc.vector.tensor_tensor(out=d4[0][:, 2 * c:2 * c + 2], in0=ov[:, :, 128], in1=eb8[:, 2 * c:2 * c + 2], op=ALU.mult),
              [psO[c], eb8], [d4[0]])
        V(lambda: nc.vector.scalar_tensor_tensor(out=d4[1][:], in0=d4[0][:], scalar=-1.0, in1=d4[0][:], op0=ALU.mult, op1=ALU.max), [d4[0]], [d4[1]])
        V(lambda: nc.vector.tensor_scalar(out=d4[1][:], in0=d4[1][:], scalar1=1.0, scalar2=None, op0=ALU.max), [d4[1]], [d4[1]])
        V(lambda: nc.vector.reciprocal(out=d4[1][:], in_=d4[1][:]), [d4[1]], [d4[1]])
        V(lambda: nc.vector.tensor_tensor(out=d4[2][:], in0=d4[1][:], in1=eb8[:], op=ALU.mult), [d4[1], eb8], [d4[2]])
        for c in range(2):
            ov = psO[c][:, 0:258].rearrange("p (a e) -> p a e", a=2)
            V(lambda: nc.vector.tensor_tensor(out=hraw[:, 2 * c:2 * c + 2, :], in0=ov[:, :, 0:128],
                                              in1=d4[2][:, 2 * c:2 * c + 2].unsqueeze(2).broadcast_to([128, 2, 128]), op=ALU.mult),
              [psO[c], d4[2]], [hraw])
        yield
        for h in range(4):
            A(lambda: nc.scalar.activation(out=hm[:, h * 128:(h + 1) * 128], in_=hraw[:, h, :], func=AF.Square, accum_out=ss4[:, h:h + 1]),
              [hraw], [hm, ss4])
        yield
        rstd_pow(rs4, ss4, 128)
        yield
        V(lambda: nc.vector.tensor_tensor(out=hraw[:], in0=hraw[:], in1=rs4[:, :].unsqueeze(2).broadcast_to([128, 4, 128]), op=ALU.mult),
          [hraw, rs4], [hraw])
        G(lambda: nc.gpsimd.tensor_tensor(out=hraw[:], in0=hraw[:], in1=tabA[:, TA_MONW:TA_MONW + 512].rearrange("p (h d) -> p h d", h=4),
                                          op=ALU.mult), [hraw, tabA], [hraw])
        yield
        V(lambda: nc.vector.tensor_tensor(out=hm[:], in0=hraw[:].rearrange("p h d -> p (h d)"), in1=sigo[:], op=ALU.mult), [hraw, sigo], [hm])
        yield
        ph = bfv(psSTm, 512)
        for c in range(4):
            P(lambda: nc.tensor.transpose(out=ph[:, c * 128:(c + 1) * 128], in_=hm[:, c * 128:(c + 1) * 128], identity=identb[:]),
              [hm, identb], [psSTm], sig=(c == 3))
        A(lambda: nc.scalar.copy(out=hmTs[:], in_=ph), [psSTm], [hmTs])
        D([(hmT_d[oi], hmTs[:])], r=[hmTs], w=[hmT_db[oi]])
        yield

    fs = {"next": 0, "act": [], "done": -1}
    SKEW = 7

    def ftick(limit):
        if fs["next"] < NT and fs["next"] <= limit and len(fs["act"]) < 2 and (not fs["act"] or fs["act"][-1][2] >= SKEW):
            fs["act"].append([fs["next"], a_front(fs["next"]), 0])
            fs["next"] += 1
        for a in list(fs["act"]):
            if next(a[1], "end") == "end":
                fs["act"].remove(a)
                fs["done"] = max(fs["done"], a[0])
            else:
                a[2] += 1

    prev_out = None
    for t in range(NT):
        while fs["done"] < t:
            ftick(t + 1)
            if prev_out is not None and next(prev_out, "end") == "end":
                prev_out = None
        sc_ = a_scan(t)
        while sc_ is not None or prev_out is not None:
            if sc_ is not None and next(sc_, "end") == "end":
                sc_ = None
            if prev_out is not None and next(prev_out, "end") == "end":
                prev_out = None
            ftick(t + 2)
        prev_out = a_out(t) if t >= QLO else None
    if prev_out is not None:
        for _ in prev_out:
            ftick(NT)
    while fs["act"]:
        ftick(NT)
    T.barrier()
    pa.close()
    pa2 = ExitStack()
    kcT = mk(pa2, "kcT", [128, S], BF16)
    vcT = mk(pa2, "vcT", [128, S], BF16)
    D([(kcT[:], kcs_d)], r=[kvs_db], w=[kcT])
    D([(vcT[:], vcs_d)], r=[kvs_db], w=[vcT])
    w1 = [mk(pa2, "w1k", [128, 32, 256], BF16), mk(pa2, "w1v", [128, 32, 256], BF16)]
    w2 = [mk(pa2, "w2k", [128, 2, 64], BF16), mk(pa2, "w2v", [128, 2, 64], BF16)]
    pe2f = mk(pa2, "pe2f", [32, 256])
    pe2b = mk(pa2, "pe2b", [32, 256], BF16)
    peT = mk(pa2, "peT", [128, 2, 32], BF16)
    bias4 = mk(pa2, "bias4", [128, 4])
    gel = [[mk(pa2, "gel%d%d" % (kv, g), [128, 2, NCPAD], BF16) for g in range(2)] for kv in range(2)]
    with ExitStack() as stg:
        stage = [[mk(stg, "stgB%d" % i, [128, 2048]) for i in range(2)], 0]
        for kv, wd in enumerate((w1k_d, w1v_d)):
            wv = wd.rearrange("(l d) n -> d l n", d=64)
            for lq in range(4):
                st = stage[0][stage[1] % 2]
                stage[1] += 1
                sv = st[:, :].rearrange("p (a n) -> p a n", a=8)
                D([(sv[0:64], wv[:, lq * 8:(lq + 1) * 8, :]), (sv[64:128], wv[:, lq * 8:(lq + 1) * 8, :])], w=[st])
                V(lambda: nc.vector.tensor_copy(out=w1[kv][:, lq * 8:(lq + 1) * 8, :], in_=sv), [st], [w1[kv]])
        for kv, wd in enumerate((w2k_d, w2v_d)):
            load_w(stage, w2[kv], w2[kv][:], wd.rearrange("(c p) n -> p c n", p=128), [128, 2, 64])
        T.barrier()
    D([(pe2f[:], pe2_d)], w=[pe2f])
    V(lambda: nc.vector.tensor_copy(out=pe2b[:], in_=pe2f[:]), [pe2f], [pe2b])
    for kv in range(2):
        for g in range(2):
            G(lambda: nc.gpsimd.memset(gel[kv][g][:], 0.0), [], [gel[kv][g]])
    ppe = bfv(PS[4], 128)
    for kv in range(2):
        P(lambda: nc.tensor.transpose(out=ppe[:, kv * 32:(kv + 1) * 32], in_=pe2b[:, kv * 128:(kv + 1) * 128], identity=identb[0:32, 0:32]),
          [pe2b, identb], [PS[4]])
    V(lambda: nc.vector.tensor_copy(out=peT[:].rearrange("p a l -> p (a l)"), in_=ppe[:, 0:64]), [PS[4]], [peT])
    for kv in range(2):
        for hc in range(2):
            i4 = kv * 2 + hc
            for l in range(32):
                P(lambda: nc.tensor.matmul(out=PS[5][:, i4:i4 + 1], lhsT=w1[kv][0:64, l, hc * 128:(hc + 1) * 128], rhs=peT[0:64, kv, l:l + 1],
                                           start=(l == 0), stop=(l == 31)), [w1[kv], peT], [PS[5]], sig=(l == 31))
    V(lambda: nc.vector.tensor_copy(out=bias4[:], in_=PS[5][:, 0:4]), [PS[5]], [bias4])
    bi = 0
    for kv, src in enumerate((kcT, vcT)):
        srcv = src[:, :].rearrange("p (n s) -> p n s", s=16)
        for hc in range(2):
            pss = (PS[(2 * bi) % 4], PS[(2 * bi + 1) % 4])
            bi += 1
            for l in range(32):
                for g in range(2):
                    rows = slice(g * 64, g * 64 + 64)
                    P(lambda: nc.tensor.matmul(out=pss[g][:, 0:NCMP], lhsT=w1[kv][rows, l, hc * 128:(hc + 1) * 128],
                                               rhs=srcv[rows, l // 16:l // 16 + NCMP, l % 16], start=(l == 0), stop=(l == 31)),
                      [w1[kv], src], [pss[g]], sig=(l == 31), rg=g * 64)
            for g in range(2):
                A(lambda: nc.scalar.activation(out=gel[kv][g][:, hc, 0:NCMP], in_=pss[g][:, 0:NCMP], func=AF.Gelu,
                                               bias=bias4[:, kv * 2 + hc:kv * 2 + hc + 1]), [pss[g], bias4], [gel[kv][g]])
    for g in range(2):
        rows = slice(g * 64, g * 64 + 64)
        for hc in range(2):
            P(lambda: nc.tensor.matmul(out=PS[6][rows, 0:NCMP], lhsT=w2[0][:, hc, :], rhs=gel[0][g][:, hc, 0:NCMP],
                                       start=(hc == 0), stop=(hc == 1)), [w2[0], gel[0][g]], [PS[6]], sig=(hc == 1))
        V(lambda: nc.vector.tensor_copy(out=kcmpT[rows, 0:NCMP], in_=PS[6][rows, 0:NCMP]), [PS[6]], [kcmpT])
        for cc in range(NCC):
            for hc in range(2):
                P(lambda: nc.tensor.matmul(out=PS[7][:, (cc * 2 + g) * 64:(cc * 2 + g + 1) * 64], lhsT=gel[1][g][:, hc, cc * 128:(cc + 1) * 128],
                                           rhs=w2[1][:, hc, :], start=(hc == 0), stop=(hc == 1)), [w2[1], gel[1][g]], [PS[7]], sig=(hc == 1))
    A(lambda: nc.scalar.copy(out=vcmp[:].rearrange("p c g d -> p (c g d)"), in_=PS[7][:, 0:NCC * 128]), [PS[7]], [vcmp])
    T.barrier()
    if debug == 1:
        for nm, tb in (("d_kwT", kwT), ("d_kcT", kcT), ("d_vcT", vcT)):
            o, ob = dout(nm, [128, S], BF16)
            D([(o, tb[:])], r=[tb], w=[ob])
        o, ob = dout("d_vsA", [128, NT * 130], BF16)
        D([(o, vsA[:].rearrange("p t g d -> p (t g d)"))], r=[vsA], w=[ob])
        o, ob = dout("d_hmT", [NOWN, 128, 512], BF16)
        D([(o[i], hmT_d[i]) for i in range(NOWN)], r=hmT_db, w=[ob])
        o, ob = dout("d_kcmpT", [128, NCPAD], BF16)
        D([(o, kcmpT[:])], r=[kcmpT], w=[ob])
        o, ob = dout("d_vcmp", [128, NCC * 128], BF16)
        D([(o, vcmp[:].rearrange("p c g d -> p (c g d)"))], r=[vcmp], w=[ob])
        T.barrier()
        return nc, T, dbg
    pa2.close()
    pa0.close()

    pb = ExitStack()
    wB = mk(pb, "wB", [128, 8, 536], BF16)
    ovl = mk(pb, "ovl", [128, NCC, NSEL], BF16)
    cbm = mk(pb, "cbm", [128, 1024], BF16)
    with ExitStack() as stg:
        stage = [[mk(stg, "stgC%d" % i, [128, 2048]) for i in range(2)], 0]
        w_in_v = w_in_d.rearrange("(c p) n -> p c n", p=128)
        for (do, so, sn) in ((0, IN_OFF["nq"][0], 512), (512, IN_OFF["ng"][0], 24)):
            for o in range(0, sn, 256):
                n = min(256, sn - o)
                load_w(stage, wB, wB[:, :, do + o:do + o + n], w_in_v[:, :, so + o:so + o + n], [128, 8, n],
                       scale_ap=n1w[:, :].unsqueeze(2).broadcast_to([128, 8, n]), scale_tb=n1w)
        for o in range(0, S, 2048):
            n = min(2048, S - o)
            st = stage[0][stage[1] % 2]
            stage[1] += 1
            D([(st[64:128, 0:n], ewin_d[:, o:o + n])], w=[st])
            V(lambda: nc.vector.tensor_copy(out=KE[0][64:128, o:o + n], in_=st[64:128, 0:n]), [st], [KE[0]])
            G(lambda: nc.gpsimd.tensor_copy(out=KE[1][64:128, o:o + n], in_=st[64:128, 0:n]), [st], [KE[1]])
        load_w(stage, ovl, ovl[:].rearrange("p c j -> p (c j)"), ovl_d, [128, NCC * NSEL])
        load_w(stage, cbm, cbm[:], cbm_d, [128, 1024])
        T.barrier()
    xtb = [mk(pb, "xtb%d" % i, [128, 1024]) for i in range(2)]
    sqjb = mk(pb, "sqjb", [128, 1024], BF16)
    ssb = mk(pb, "ssb", [128, 1])
    rsb = mk(pb, "rsb", [128, 1])
    xnb = mk(pb, "xnb", [128, 1024], BF16)
    xnTb = mk(pb, "xnTb", [128, 8, 128], BF16)
    csb = [mk(pb, "csb%d" % i, [128, 64]) for i in range(2)]
    qsq = mk(pb, "qsq", [128, 512])
    ss8 = mk(pb, "ss8", [128, 8])
    rs8 = mk(pb, "rs8", [128, 8])
    qn = mk(pb, "qn", [128, 8, 64])
    qrt = [mk(pb, "qrt%d" % i, [128, 8, 32]) for i in range(4)]
    qr = mk(pb, "qr", [128, 8, 64], BF16)
    qT2 = [mk(pb, "qT%d" % i, [128, 4, 128], BF16) for i in range(2)]
    sg2 = [mk(pb, "sg%d" % i, [128, 24]) for i in range(2)]
    cmk = [mk(pb, "cmk%d" % i, [128, NCPAD], BF16) for i in range(2)]
    itab = [mk(pb, "itab%d" % i, [128, NSEL]) for i in range(2)]
    sc2 = [mk(pb, "sc%d" % i, [128, NCPAD]) for i in range(2)]
    mx2 = [mk(pb, "mx%d" % i, [128, 1]) for i in range(2)]
    se2 = [mk(pb, "se%d" % i, [128, 1]) for i in range(2)]
    pc2_ = [mk(pb, "pc%d" % i, [128, NCPAD], BF16) for i in range(2)]
    pcT2 = [mk(pb, "pcT%d" % i, [128, NCC, 128], BF16) for i in range(2)]
    ocmp2 = [mk(pb, "ocmp%d" % i, [128, 8, 64]) for i in range(2)]
    oslc = mk(pb, "oslc", [128, 8, 64])
    owin = mk(pb, "owin", [128, 8, 64])
    imp2 = mk(pb, "imp2", [128, NSEL])
    imp3 = mk(pb, "imp3", [128, NSEL])
    m8 = mk(pb, "m8", [128, 16])
    selb = mk(pb, "selb", [128, NSEL])
    negp = mk(pb, "negp", [128, NSEL])
    bqb = mk(pb, "bqb", [128, 2, 128], BF16)
    V(lambda: nc.vector.memset(bqb[:], 0.0), [], [bqb])
    NW = 2 if NSEL > 64 else 1
    QB2 = [[[mk(pb, "QB%d%d%d" % (i, g, w), [128, 4, 128], BF16) for w in range(NW)] for g in range(2)] for i in range(2)]
    pTs = [mk(pb, "pTs%d" % i, [128, 512], BF16) for i in range(4)]
    den = mk(pb, "den", [128, 4])
    osb = [mk(pb, "osb%d" % i, [128, 260]) for i in range(2)]
    onb = mk(pb, "onb", [128, 512], BF16)
    gt = [mk(pb, "gt%d" % i, [128, 8, 64]) for i in range(2)]
    onTs = mk(pb, "onTs", [128, 512], BF16)
    psT, psQ, psG, psSC, psS0, psS1, psOS, psOW = PS
    psST = (psS0, psS1, psOW)
    cnt_ = {"st": 0, "pt": 0, "os": 0}

    def b_front(oi):
        t = QLO + oi
        p_ = oi % 2
        xt, cs, qT, sg, ocmp, QB = xtb[p_], csb[p_], qT2[p_], sg2[p_], ocmp2[p_], QB2[p_]
        D([(xt[:], x_d[t * 128:(t + 1) * 128, :])], w=[xt])
        D([(cs[:, 0:32], cos_d[t * 128:(t + 1) * 128, :]), (cs[:, 32:64], sin_d[t * 128:(t + 1) * 128, :])], w=[cs])
        D([(cmk[p_][:], cmask_d[oi])], w=[cmk[p_]])
        D([(itab[p_][:], imptab_d[oi])], w=[itab[p_]])
        A(lambda: nc.scalar.activation(out=sqjb[:], in_=xt[:], func=AF.Square, accum_out=ssb[:]), [xt], [sqjb, ssb])
        yield
        rstd_pow(rsb, ssb, 1024)
        yield
        A(lambda: nc.scalar.activation(out=xnb[:], in_=xt[:], func=AF.Copy, scale=rsb[:]), [xt, rsb], [xnb])
        yield
        pv = bfv(psT, 1024)
        for c in range(8):
            P(lambda: nc.tensor.transpose(out=pv[:, c * 128:(c + 1) * 128], in_=xnb[:, c * 128:(c + 1) * 128], identity=identb[:]),
              [xnb, identb], [psT], sig=(c == 7))
        yield
        V(lambda: nc.vector.tensor_copy(out=xnTb[:].rearrange("p c t -> p (c t)"), in_=pv), [psT], [xnTb])
        yield
        for c in range(8):
            P(lambda: nc.tensor.matmul(out=psQ[:, 0:512], lhsT=xnTb[:, c, :], rhs=wB[:, c, 0:512], start=(c == 0), stop=(c == 7)),
              [xnTb, wB], [psQ], sig=(c == 7))
        for c in range(8):
            P(lambda: nc.tensor.matmul(out=psG[:, 0:24], lhsT=xnTb[:, c, :], rhs=wB[:, c, 512:536], start=(c == 0), stop=(c == 7)),
              [xnTb, wB], [psG], sig=(c == 7))
        yield
        A(lambda: nc.scalar.activation(out=sg[:], in_=psG[:, 0:24], func=AF.Exp, scale=-1.0), [psG], [sg])
        A(lambda: nc.scalar.activation(out=qsq[:], in_=psQ[:, 0:512], func=AF.Square), [psQ], [qsq])
        yield
        V(lambda: nc.vector.tensor_scalar(out=sg[:], in0=sg[:], scalar1=1.0, scalar2=None, op0=ALU.add), [sg], [sg])
        V(lambda: nc.vector.reciprocal(out=sg[:], in_=sg[:]), [sg], [sg])
        V(lambda: nc.vector.tensor_reduce(out=ss8[:], in_=qsq[:].rearrange("p (a d) -> p a d", a=8), axis=AX.X, op=ALU.add), [qsq], [ss8])
        yield
        rstd_pow(rs8, ss8, 64)
        yield
        V(lambda: nc.vector.tensor_tensor(out=qn[:], in0=psQ[:, 0:512].rearrange("p (a d) -> p a d", a=8),
                                          in1=rs8[:, :].unsqueeze(2).broadcast_to([128, 8, 64]), op=ALU.mult), [psQ, rs8], [qn])
        V(lambda: nc.vector.tensor_tensor(out=qn[:], in0=qn[:], in1=tabA[:, TA_QNW:TA_QNW + 512].rearrange("p (a d) -> p a d", a=8),
                                          op=ALU.mult), [qn, tabA], [qn])
        yield
        cosb = cs[:, 0:32].unsqueeze(1).broadcast_to([128, 8, 32])
        sinb = cs[:, 32:64].unsqueeze(1).broadcast_to([128, 8, 32])
        V(lambda: nc.vector.tensor_tensor(out=qrt[0][:], in0=qn[:, :, 0:32], in1=cosb, op=ALU.mult), [qn, cs], [qrt[0]])
        G(lambda: nc.gpsimd.tensor_tensor(out=qrt[1][:], in0=qn[:, :, 32:64], in1=sinb, op=ALU.mult), [qn, cs], [qrt[1]])
        V(lambda: nc.vector.tensor_tensor(out=qrt[2][:], in0=qn[:, :, 32:64], in1=cosb, op=ALU.mult), [qn, cs], [qrt[2]])
        G(lambda: nc.gpsimd.tensor_tensor(out=qrt[3][:], in0=qn[:, :, 0:32], in1=sinb, op=ALU.mult), [qn, cs], [qrt[3]])
        yield
        qro = qr[:, :, :].rearrange("p (h g) d -> p g h d", g=2)
        v4 = lambda tb_: tb_[:, :, :].rearrange("p (g h) d -> p g h d", g=2)
        V(lambda: nc.vector.tensor_tensor(out=qro[:, :, :, 0:32], in0=v4(qrt[0]), in1=v4(qrt[1]), op=ALU.subtract), [qrt[0], qrt[1]], [qr])
        G(lambda: nc.gpsimd.tensor_tensor(out=qro[:, :, :, 32:64], in0=v4(qrt[2]), in1=v4(qrt[3]), op=ALU.add), [qrt[2], qrt[3]], [qr])
        yield
        pq = bfv(psT, 512)
        for hg in range(4):
            P(lambda: nc.tensor.transpose(out=pq[:, hg * 128:(hg + 1) * 128], in_=qr[:, 2 * hg:2 * hg + 2, :].rearrange("p a d -> p (a d)"),
                                          identity=identb[:]), [qr, identb], [psT], sig=(hg == 3))
        pq1 = bfv(psSC, 512)
        for hg in range(4):
            P(lambda: nc.tensor.transpose(out=pq1[0:64, hg * 128:(hg + 1) * 128], in_=qr[:, 2 * hg + 1, :], identity=identb[:]),
              [qr, identb], [psSC], sig=(hg == 3))
        yield
        V(lambda: nc.vector.tensor_copy(out=qT[:].rearrange("p h t -> p (h t)"), in_=pq), [psT], [qT])
        for w in range(NW):
            V(lambda: nc.vector.tensor_copy(out=QB[1][w][0:64].rearrange("p h t -> p (h t)"), in_=pq1[0:64, :]), [psSC], [QB[1][w]])
        yield
        for w in range(NW):
            G(lambda: nc.gpsimd.tensor_copy(out=QB[0][w][0:64], in_=qT[0:64]), [qT], [QB[0][w]])
        psOC = psQ
        cm = cmk[p_]

        def cmp_head(head, par):
            g, hg = head // 4, head % 4
            rows = slice(g * 64, g * 64 + 64)
            psc = (psSC, psT)[par]
            sc, mx, se, pc, pcT = sc2[par], mx2[par], se2[par], pc2_[par], pcT2[par]
            P(lambda: nc.tensor.matmul(out=psc[:, 0:NCPAD], lhsT=qT[rows, hg, :], rhs=kcmpT[rows, 0:NCPAD], start=True, stop=False),
              [qT, kcmpT], [psc], sig=False, rg=g * 64)
            P(lambda: nc.tensor.matmul(out=psc[:, 0:NCPAD], lhsT=identb[:], rhs=cm[:], start=False, stop=True), [identb, cm], [psc])
            yield
            V(lambda: nc.vector.reduce_max(out=mx[:], in_=psc[:, 0:NCPAD], axis=AX.X), [psc], [mx])
            V(lambda: nc.vector.tensor_scalar(out=mx[:], in0=mx[:], scalar1=-0.125, scalar2=None, op0=ALU.mult), [mx], [mx])
            yield
            A(lambda: nc.scalar.activation(out=sc[:], in_=psc[:, 0:NCPAD], func=AF.Exp, scale=0.125, bias=mx[:], accum_out=se[:]), [psc, mx], [sc, se])
            yield
            V(lambda: nc.vector.reciprocal(out=se[:], in_=se[:]), [se], [se])
            V(lambda: nc.vector.tensor_scalar(out=pc[:], in0=sc[:], scalar1=se[:], scalar2=rvt[:, oi:oi + 1], op0=ALU.mult, op1=ALU.mult),
              [sc, se, rvt], [pc])
            yield
            pp = bfv(psc, NCC * 128)
            for cc in range(NCC):
                P(lambda: nc.tensor.transpose(out=pp[:, cc * 128:(cc + 1) * 128], in_=pc[:, cc * 128:(cc + 1) * 128], identity=identb[:]),
                  [pc, identb], [psc], sig=(cc == NCC - 1))
            yield
            V(lambda: nc.vector.tensor_copy(out=pcT[:].rearrange("p c t -> p (c t)"), in_=pp), [psc], [pcT])
            yield
            for cc in range(NCC):
                P(lambda: nc.tensor.matmul(out=psOC[:, head * 64:(head + 1) * 64], lhsT=pcT[:, cc, :], rhs=vcmp[:, cc, g, :],
                                           start=(cc == 0), stop=(cc == NCC - 1)), [pcT, vcmp], [psOC], sig=(cc == NCC - 1))
            for cc in range(NCC):
                first = (hg == 0 and cc == 0)
                last = (hg == 3 and cc == NCC - 1)
                P(lambda: nc.tensor.matmul(out=psG[:, 256 + g * NSEL:256 + (g + 1) * NSEL], lhsT=pcT[:, cc, :], rhs=ovl[:, cc, :],
                                           start=first, stop=last, skip_group_check=True), [pcT, ovl], [psG], sig=(cc == NCC - 1))
            yield

        for pair in range(4):
            ga = cmp_head(2 * pair, 0)
            gb = cmp_head(2 * pair + 1, 1)
            for _ in ga:
                next(gb, None)
                yield
        V(lambda: nc.vector.tensor_copy(out=ocmp[:].rearrange("p h d -> p (h d)"), in_=psOC[:, 0:512]), [psOC], [ocmp])
        yield
        for g in range(2):
            V(lambda: nc.vector.tensor_tensor(out=imp2[:], in0=psG[:, 256 + g * NSEL:256 + (g + 1) * NSEL], in1=itab[p_][:], op=ALU.add),
              [psG, itab[p_]], [imp2])
            V(lambda: nc.vector.max(out=m8[:, 0:8], in_=imp2[:]), [imp2], [m8])
            yield
            V(lambda: nc.vector.match_replace(out=imp3[:], in_to_replace=m8[:, 0:8], in_values=imp2[:], imm_value=-3.0e38), [imp2, m8], [imp3])
            V(lambda: nc.vector.max(out=m8[:, 8:16], in_=imp3[:]), [imp3], [m8])
            yield
            V(lambda: nc.vector.tensor_scalar(out=selb[:], in0=imp2[:], scalar1=m8[:, 15:16], scalar2=None, op0=ALU.is_ge), [imp2, m8], [selb])
            V(lambda: nc.vector.tensor_scalar(out=selb[:], in0=selb[:], scalar1=-1.0, scalar2=-NEGB, op0=ALU.add, op1=ALU.mult), [selb], [selb])
            yield
            V(lambda: nc.vector.tensor_scalar(out=negp[:], in0=itab[p_][:], scalar1=0.0, scalar2=NEGB, op0=ALU.min, op1=ALU.max),
              [itab[p_]], [negp])
            V(lambda: nc.vector.tensor_tensor(out=bqb[:, 0, 0:NSEL], in0=selb[:], in1=negp[:], op=ALU.add), [selb, negp], [bqb])
            yield
            nlo = min(64, NSEL)
            V(lambda: nc.vector.tensor_copy(out=bqb[:, 1, 64:64 + nlo], in_=bqb[:, 0, 0:nlo]), [bqb], [bqb])
            if NSEL > 64:
                V(lambda: nc.vector.tensor_copy(out=bqb[:, 1, 0:NSEL - 64], in_=bqb[:, 0, 64:NSEL]), [bqb], [bqb])
            yield
            pb_ = bfv(psT, 256)
            for w in range(NW):
                P(lambda: nc.tensor.transpose(out=pb_[:, w * 128:(w + 1) * 128], in_=bqb[:, 1 - w, :], identity=identb[:]), [bqb, identb], [psT],
                  sig=(w == NW - 1))
            yield
            for w in range(NW):
                V(lambda: nc.vector.tensor_copy(out=QB[g][w][64:128], in_=pb_[64:128, w * 128:(w + 1) * 128].unsqueeze(1).broadcast_to([64, 4, 128])),
                  [psT], [QB[g][w]])
            yield

    def b_loops(oi):
        t = QLO + oi
        p_ = oi % 2
        qT, sg, ocmp, QB = qT2[p_], sg2[p_], ocmp2[p_], QB2[p_]
        qTf = qT[:, :, :].rearrange("p h t -> p (h t)")
        k0 = max(0, t - 4)
        its = []
        for g in range(2):
            its += [("s", g, kt) for kt in range(t + 1)]
            its += [("w", g, kt) for kt in range(k0, t + 1)]
        slots = {}

        def front(i):
            kind, g, kt = its[i]
            rows = slice(g * 64, g * 64 + 64)
            ps = psST[cnt_["st"] % 3]
            cnt_["st"] += 1
            pT = pTs[cnt_["pt"] % 4]
            cnt_["pt"] += 1
            slots[i] = (ps, pT)
            ksl = slice(kt * 128, (kt + 1) * 128)
            if kind == "s":
                qb = QB[g][kt // 32]
                P(lambda: nc.tensor.matmul(out=ps[:, :], lhsT=KE[g][:, ksl], rhs=qb[:, :, :].rearrange("p h t -> p (h t)"), start=True, stop=(kt != t)),
                  [KE[g], qb], [ps], sig=(kt != t))
                if kt == t:
                    P(lambda: nc.tensor.matmul(out=ps[:, :], lhsT=identb[:], rhs=cbm[:, 0:512], start=False, stop=True), [identb, cbm], [ps])
            else:
                edge = (kt == t) or (kt == t - 4)
                P(lambda: nc.tensor.matmul(out=ps[:, :], lhsT=kwT[rows, ksl], rhs=qTf[rows, :], start=True, stop=(not edge)), [kwT, qT], [ps],
                  sig=(not edge), rg=g * 64)
                if kt == t:
                    P(lambda: nc.tensor.matmul(out=ps[:, :], lhsT=identb[:], rhs=cbm[:, 0:512], start=False, stop=True), [identb, cbm], [ps])
                elif kt == t - 4:
                    P(lambda: nc.tensor.matmul(out=ps[:, :], lhsT=identb[:], rhs=cbm[:, 512:1024], start=False, stop=True), [identb, cbm], [ps])

        def back(i):
            kind, g, kt = its[i]
            ps, pT = slots.pop(i)
            if kind == "s":
                A(lambda: nc.scalar.activation(out=pT[:], in_=ps[:, :], func=AF.Exp, scale=0.125), [ps], [pT])
                pso, vst, first, last = psOS, vsA, (kt == 0), (kt == t)
            else:
                A(lambda: nc.scalar.activation(out=pT[:], in_=ps[:, :], func=AF.Exp, scale=0.125, bias=kbias[:, kt:kt + 1]), [ps, kbias], [pT])
                pso, vst, first, last = psOS, vwA, (kt == k0), (kt == t)
            for hg in range(4):
                P(lambda: nc.tensor.matmul(out=pso[:, hg * 65:(hg + 1) * 65], lhsT=pT[:, hg * 128:(hg + 1) * 128], rhs=vst[:, kt, g, :],
                                           start=(first and hg == 0), stop=last, skip_group_check=True), [pT, vst], [pso], sig=(hg == 3 and last))
            if last:
                ob = osb[cnt_["os"] % 2]
                cnt_["os"] += 1
                A(lambda: nc.scalar.copy(out=ob[:], in_=pso[:, 0:260]), [pso], [ob])
                ov_ = ob[:, :].rearrange("p (h e) -> p h e", h=4)
                dst = oslc if kind == "s" else owin
                V(lambda: nc.vector.tensor_scalar(out=den[:], in0=ov_[:, :, 64], scalar1=1e-30, scalar2=None, op0=ALU.max), [ob], [den])
                V(lambda: nc.vector.reciprocal(out=den[:], in_=den[:]), [den], [den])
                V(lambda: nc.vector.tensor_tensor(out=dst[:, g * 4:(g + 1) * 4, :], in0=ov_[:, :, 0:64],
                                                  in1=den[:, :].unsqueeze(2).broadcast_to([128, 4, 64]), op=ALU.mult), [ob, den], [dst])

        DEPTH = 2
        for i in range(min(DEPTH, len(its))):
            front(i)
        for i in range(len(its)):
            if i + DEPTH < len(its):
                front(i + DEPTH)
            back(i)
            yield
        sgv = sg[:, :].rearrange("p (h k) -> p h k", k=3)
        V(lambda: nc.vector.tensor_tensor(out=gt[0][:], in0=ocmp[:], in1=sgv[:, :, 0:1].broadcast_to([128, 8, 64]), op=ALU.mult), [ocmp, sg], [gt[0]])
        G(lambda: nc.gpsimd.tensor_tensor(out=gt[1][:], in0=oslc[:], in1=sgv[:, :, 1:2].broadcast_to([128, 8, 64]), op=ALU.mult), [oslc, sg], [gt[1]])
        V(lambda: nc.vector.tensor_tensor(out=gt[0][:], in0=gt[0][:], in1=gt[1][:], op=ALU.add), [gt[0], gt[1]], [gt[0]])
        G(lambda: nc.gpsimd.tensor_tensor(out=gt[1][:], in0=owin[:], in1=sgv[:, :, 2:3].broadcast_to([128, 8, 64]), op=ALU.mult), [owin, sg], [gt[1]])
        V(lambda: nc.vector.tensor_tensor(out=onb[:], in0=gt[0][:].rearrange("p h d -> p (h d)"), in1=gt[1][:].rearrange("p h d -> p (h d)"),
                                          op=ALU.add), [gt[0], gt[1]], [onb])
        yield
        po = bfv(psS0, 512)
        for c in range(4):
            P(lambda: nc.tensor.transpose(out=po[:, c * 128:(c + 1) * 128], in_=onb[:, c * 128:(c + 1) * 128], identity=identb[:]),
              [onb, identb], [psS0], sig=(c == 3))
        yield
        V(lambda: nc.vector.tensor_copy(out=onTs[:], in_=po), [psS0], [onTs])
        D([(onT_d[oi], onTs[:])], r=[onTs], w=[onT_db[oi]])
        yield

    for _ in b_front(0):
        pass
    for oi in range(NOWN):
        lp = b_loops(oi)
        nf = b_front(oi + 1) if oi + 1 < NOWN else None
        L_ = 2 * (QLO + oi) + 12 + 3
        F_ = 70
        for i_, _ in enumerate(lp):
            nadv = ((i_ + 1) * F_) // L_ - (i_ * F_) // L_
            for _k in range(nadv):
                if nf is not None:
                    try:
                        next(nf)
                    except StopIteration:
                        nf = None
        if nf is not None:
            for _ in nf:
                pass
    T.barrier()
    if debug == 2:
        o, ob = dout("d_onT", [NOWN, 128, 512], BF16)
        D([(o[i], onT_d[i]) for i in range(NOWN)], r=onT_db, w=[ob])
        T.barrier()
        return nc, T, dbg
    pb.close()
    kvs.close()

    pc1 = ExitStack()
    wC = mk(pc1, "wC", [128, 8, 2048], BF16)
    wum = mk(pc1, "wum", [128, 4, 1024], BF16)
    wun = mk(pc1, "wun", [128, 4, 1024], BF16)
    wo = mk(pc1, "wo", [128, 8, 1024], BF16)
    mgbf = mk(pc1, "mgbf", [1, 2048])
    mgbb = mk(pc1, "mgbb", [1, 2048], BF16)
    D([(mgbf[:], mgb_d)], w=[mgbf])
    V(lambda: nc.vector.tensor_copy(out=mgbb[:], in_=mgbf[:]), [mgbf], [mgbb])
    with ExitStack() as stg:
        stage = [[mk(stg, "stgD%d" % i, [128, 2048]) for i in range(2)], 0]
        w_in_v = w_in_d.rearrange("(c p) n -> p c n", p=128)
        so = IN_OFF["gm"][0]
        for o in range(0, 2048, 256):
            load_w(stage, wC, wC[:, :, o:o + 256], w_in_v[:, :, so + o:so + o + 256], [128, 8, 256],
                   scale_ap=n1w[:, :].unsqueeze(2).broadcast_to([128, 8, 256]), scale_tb=n1w)
        for (wt, wd, kc_) in ((wum, wupm_d, 4), (wun, wupn_d, 4), (wo, wout_d, 8)):
            wv = wd.rearrange("(c p) n -> p c n", p=128)
            for o in range(0, 1024, 2048 // kc_):
                n = 2048 // kc_
                load_w(stage, wt, wt[:, :, o:o + n], wv[:, :, o:o + n], [128, kc_, n])
        T.barrier()
    xtc = [mk(pc1, "xtc%d" % i, [128, 1024]) for i in range(2)]
    sqjc = mk(pc1, "sqjc", [128, 1024], BF16)
    ssc = [mk(pc1, "ssc%d" % i, [128, 1]) for i in range(2)]
    rsc = [mk(pc1, "rsc%d" % i, [128, 1]) for i in range(2)]
    xnc = [mk(pc1, "xnc%d" % i, [128, 1024], BF16) for i in range(2)]
    xnTc = [mk(pc1, "xnTc%d" % i, [128, 8, 128], BF16) for i in range(2)]
    hmTl = [mk(pc1, "hmTl%d" % i, [128, 4, 128], BF16) for i in range(2)]
    onTl = [mk(pc1, "onTl%d" % i, [128, 4, 128], BF16) for i in range(2)]
    sgm = [mk(pc1, "sgm%d" % i, [128, 2048]) for i in range(2)]
    y1 = [mk(pc1, "y1_%d" % i, [128, 1024]) for i in range(2)]
    y2 = [mk(pc1, "y2_%d" % i, [128, 1024]) for i in range(2)]
    yb = [mk(pc1, "yb%d" % i, [128, 1024], BF16) for i in range(2)]
    yT = [mk(pc1, "yT%d" % i, [128, 8, 128], BF16) for i in range(2)]
    xmo = [mk(pc1, "xmo%d" % i, [128, 1024]) for i in range(2)]
    rot = {"g": 0, "u": 0, "o": 0}

    def c1_tile(oi):
        t = QLO + oi
        p_ = oi % 2
        xt = xtc[p_]
        psTp = (PS[0], PS[7])[p_]
        D([(xt[:], x_d[t * 128:(t + 1) * 128, :])], w=[xt])
        D([(hmTl[p_][:].rearrange("p c t -> p (c t)"), hmT_d[oi])], r=[hmT_db[oi]], w=[hmTl[p_]])
        D([(onTl[p_][:].rearrange("p c t -> p (c t)"), onT_d[oi])], r=[onT_db[oi]], w=[onTl[p_]])
        A(lambda: nc.scalar.activation(out=sqjc[:], in_=xt[:], func=AF.Square, accum_out=ssc[p_][:]), [xt], [sqjc, ssc[p_]])
        rstd_pow(rsc[p_], ssc[p_], 1024)
        yield
        A(lambda: nc.scalar.activation(out=xnc[p_][:], in_=xt[:], func=AF.Copy, scale=rsc[p_][:]), [xt, rsc[p_]], [xnc[p_]])
        yield
        pv = bfv(psTp, 1024)
        for c in range(8):
            P(lambda: nc.tensor.transpose(out=pv[:, c * 128:(c + 1) * 128], in_=xnc[p_][:, c * 128:(c + 1) * 128], identity=identb[:]),
              [xnc[p_], identb], [psTp], sig=(c == 7))
        yield
        V(lambda: nc.vector.tensor_copy(out=xnTc[p_][:].rearrange("p c t -> p (c t)"), in_=pv), [psTp], [xnTc[p_]])
        yield
        for nb in range(4):
            ps = PS[1 + rot["g"] % 2]
            rot["g"] += 1
            for c in range(8):
                P(lambda: nc.tensor.matmul(out=ps[:, :], lhsT=xnTc[p_][:, c, :], rhs=wC[:, c, nb * 512:(nb + 1) * 512], start=(c == 0), stop=False),
                  [xnTc[p_], wC], [ps], sig=False)
            P(lambda: nc.tensor.matmul(out=ps[:, :], lhsT=onesb[0:1, :], rhs=mgbb[0:1, nb * 512:(nb + 1) * 512], start=False, stop=True),
              [onesb, mgbb], [ps])
            A(lambda: nc.scalar.activation(out=sgm[p_][:, nb * 512:(nb + 1) * 512], in_=ps[:, :], func=AF.Sigmoid), [ps], [sgm[p_]])
            if nb % 2 == 1:
                yield
        for br, (src, wt) in enumerate(((hmTl[p_], wum), (onTl[p_], wun))):
            for nb in range(2):
                ps = PS[3 + rot["u"] % 2]
                rot["u"] += 1
                for c in range(4):
                    P(lambda: nc.tensor.matmul(out=ps[:, :], lhsT=src[:, c, :], rhs=wt[:, c, nb * 512:(nb + 1) * 512], start=(c == 0), stop=(c == 3)),
                      [src, wt], [ps], sig=(c == 3))
                yy = y1[p_] if br == 0 else y2[p_]
                V(lambda: nc.vector.tensor_tensor(out=yy[:, nb * 512:(nb + 1) * 512], in0=ps[:, :],
                                                  in1=sgm[p_][:, br * 1024 + nb * 512:br * 1024 + (nb + 1) * 512], op=ALU.mult), [ps, sgm[p_]], [yy])
            yield
        G(lambda: nc.gpsimd.tensor_tensor(out=yb[p_][:], in0=y1[p_][:], in1=y2[p_][:], op=ALU.add), [y1[p_], y2[p_]], [yb[p_]])
        yield
        py = bfv(psTp, 1024)
        for c in range(8):
            P(lambda: nc.tensor.transpose(out=py[:, c * 128:(c + 1) * 128], in_=yb[p_][:, c * 128:(c + 1) * 128], identity=identb[:]),
              [yb[p_], identb], [psTp], sig=(c == 7))
        yield
        A(lambda: nc.scalar.copy(out=yT[p_][:].rearrange("p c t -> p (c t)"), in_=py), [psTp], [yT[p_]])
        yield
        xm = xmo[p_]
        for nb in range(2):
            ps = PS[5 + rot["o"] % 2]
            rot["o"] += 1
            for c in range(8):
                P(lambda: nc.tensor.matmul(out=ps[:, :], lhsT=yT[p_][:, c, :], rhs=wo[:, c, nb * 512:(nb + 1) * 512], start=(c == 0), stop=(c == 7)),
                  [yT[p_], wo], [ps], sig=(c == 7))
            V(lambda: nc.vector.tensor_tensor(out=xm[:, nb * 512:(nb + 1) * 512], in0=ps[:, :], in1=xt[:, nb * 512:(nb + 1) * 512], op=ALU.add),
              [ps, xt], [xm])
        D([(out_d[oi], xm[:])], r=[xm], w=[xmid_db[oi]])
        yield

    pipeline(range(NOWN), c1_tile, 6)
    T.barrier()
    if debug == 3:
        T.barrier()
        return nc, T, dbg
    pc1.close()

    pc2 = ExitStack()
    fup = mk(pc2, "fup", [128, 8, 2 * D_FF], BF16)
    fdn = mk(pc2, "fdn", [128, 22, 1024], BF16)
    fcw = mk(pc2, "fcw", [128, 22, 3])
    fcb = mk(pc2, "fcb", [128, 22])
    D([(fcw[:].rearrange("p a b -> p (a b)"), fcw_d)], w=[fcw])
    D([(fcb[:], fcb_d)], w=[fcb])
    with ExitStack() as stg:
        stage = [[mk(stg, "stgE%d" % i, [128, 2048]) for i in range(2)], 0]
        fup_v = fup_d.rearrange("(c p) n -> p c n", p=128)
        for o in range(0, 2 * D_FF, 256):
            load_w(stage, fup, fup[:, :, o:o + 256], fup_v[:, :, o:o + 256], [128, 8, 256],
                   scale_ap=n2w[:, :].unsqueeze(2).broadcast_to([128, 8, 256]), scale_tb=n2w)
        fdn_v = fdn_d.rearrange("(c p) n -> p c n", p=128)
        for c0 in range(0, 22, 2):
            load_w(stage, fdn, fdn[:, c0:c0 + 2, :], fdn_v[:, c0:c0 + 2, :], [128, 2, 1024], eng="pool")
        T.barrier()
    xmh = mk(pc2, "xmh", [128, 1024])
    xmd = [mk(pc2, "xmd%d" % i, [128, 1024]) for i in range(2)]
    sse = mk(pc2, "sse", [128, 1])
    rse_ = mk(pc2, "rse", [128, 1])
    xne = mk(pc2, "xne", [128, 1024], BF16)
    TS = 4
    h2T2 = [mk(pc2, "h2T%d" % i, [128, 8, TS * 128], BF16) for i in range(2)]
    cstate = mk(pc2, "cstate", [128, 22, 2])
    V(lambda: nc.vector.memset(cstate[:], 0.0), [], [cstate])
    ab = [mk(pc2, "ab%d" % i, [128, 2 + TS * 128]) for i in range(2)]
    acc = [mk(pc2, "acc%d" % i, [128, TS * 128]) for i in range(2)]
    gl = [mk(pc2, "gl%d" % i, [128, TS * 128]) for i in range(2)]
    uT = mk(pc2, "uT", [128, 22, TS * 128], BF16)
    groups = []
    o0 = 0
    if RESET_TILE is not None:
        groups.append([0])
        o0 = 1
    for a_ in range(o0, NOWN, TS):
        groups.append(list(range(a_, min(NOWN, a_ + TS))))
    cnt2 = {"x": 0, "c": 0}

    def c2_head(gi):
        grp = groups[gi]
        h2T = h2T2[gi % 2]
        for j, oi in enumerate(grp):
            D([(xmh[:], out_d[oi])], r=[xmid_db[oi]], w=[xmh])
            A(lambda: nc.scalar.activation(out=xne[:], in_=xmh[:], func=AF.Square, accum_out=sse[:]), [xmh], [xne, sse])
            yield
            rstd_pow(rse_, sse, 1024)
            yield
            A(lambda: nc.scalar.activation(out=xne[:], in_=xmh[:], func=AF.Copy, scale=rse_[:]), [xmh, rse_], [xne])
            yield
            pv = bfv(psT, 1024)
            for c in range(8):
                P(lambda: nc.tensor.transpose(out=pv[:, c * 128:(c + 1) * 128], in_=xne[:, c * 128:(c + 1) * 128], identity=identb[:]),
                  [xne, identb], [psT], sig=(c == 7))
            A(lambda: nc.scalar.copy(out=h2T[:, :, j * 128:(j + 1) * 128], in_=pv.rearrange("p (c t) -> p c t", c=8)), [psT], [h2T])
            yield

    def c2_body(gi):
        grp = groups[gi]
        h2T = h2T2[gi % 2]
        nt = len(grp)
        W = nt * 128
        for cch in range(22):
            ci = cnt2["c"]
            cnt2["c"] += 1
            psa = PS[1 + ci % 2]
            psv = PS[3 + ci % 2]
            ab_ = ab[ci % 2]
            acc_ = acc[ci % 2]
            gl_ = gl[ci % 2]
            for c in range(8):
                P(lambda: nc.tensor.matmul(out=psa[:, 0:W], lhsT=fup[:, c, cch * 128:(cch + 1) * 128], rhs=h2T[:, c, 0:W],
                                           start=(c == 0), stop=(c == 7)), [fup, h2T], [psa], sig=(c == 7))
            for c in range(8):
                P(lambda: nc.tensor.matmul(out=psv[:, 0:W], lhsT=fup[:, c, D_FF + cch * 128:D_FF + (cch + 1) * 128], rhs=h2T[:, c, 0:W],
                                           start=(c == 0), stop=(c == 7)), [fup, h2T], [psv], sig=(c == 7))
            G(lambda: nc.gpsimd.tensor_copy(out=ab_[:, 0:2], in_=cstate[:, cch, :]), [cstate], [ab_])
            A(lambda: nc.scalar.copy(out=ab_[:, 2:2 + W], in_=psa[:, 0:W]), [psa], [ab_])
            V(lambda: nc.vector.tensor_scalar(out=acc_[:, 0:W], in0=ab_[:, 0:W], scalar1=fcw[:, cch, 0:1], scalar2=None, op0=ALU.mult),
              [ab_, fcw], [acc_])
            for k in (1, 2):
                V(lambda: nc.vector.scalar_tensor_tensor(out=acc_[:, 0:W], in0=ab_[:, k:k + W], scalar=fcw[:, cch, k:k + 1],
                                                         in1=acc_[:, 0:W], op0=ALU.mult, op1=ALU.add), [ab_, fcw, acc_], [acc_])
            G(lambda: nc.gpsimd.tensor_copy(out=cstate[:, cch, :], in_=ab_[:, W:W + 2]), [ab_], [cstate])
            A(lambda: nc.scalar.activation(out=gl_[:, 0:W], in_=acc_[:, 0:W], func=AF.Gelu, bias=fcb[:, cch:cch + 1]), [acc_, fcb], [gl_])
            V(lambda: nc.vector.tensor_tensor(out=uT[:, cch, 0:W], in0=psv[:, 0:W], in1=gl_[:, 0:W], op=ALU.mult), [psv, gl_], [uT])
            yield
        if RESET_TILE is not None and grp == [0]:
            G(lambda: nc.gpsimd.tensor_scalar(out=cstate[:], in0=cstate[:], scalar1=flag, scalar2=None, op0=ALU.mult), [cstate, cst], [cstate])
        for j, oi in enumerate(grp):
            xm = xmd[cnt2["x"] % 2]
            cnt2["x"] += 1
            D([(xm[:], out_d[oi])], r=[xmid_db[oi]], w=[xm])
            for nb in range(2):
                ps = PS[5 + nb]
                for cch in range(22):
                    P(lambda: nc.tensor.matmul(out=ps[:, :], lhsT=uT[:, cch, j * 128:(j + 1) * 128], rhs=fdn[:, cch, nb * 512:(nb + 1) * 512],
                                               start=(cch == 0), stop=(cch == 21)), [uT, fdn], [ps], sig=(cch == 21))
                V(lambda: nc.vector.tensor_tensor(out=xm[:, nb * 512:(nb + 1) * 512], in0=ps[:, :], in1=xm[:, nb * 512:(nb + 1) * 512], op=ALU.add),
                  [ps, xm], [xm])
            D([(out_d[oi], xm[:])], r=[xm], w=[xmid_db[oi]])
            yield

    for _ in c2_head(0):
        pass
    for gi in range(len(groups)):
        nh = c2_head(gi + 1) if gi + 1 < len(groups) else None
        for _ in c2_body(gi):
            if nh is not None and next(nh, "end") == "end":
                nh = None
        if nh is not None:
            for _ in nh:
                pass
    T.barrier()
    pc2.close()
    return nc, T, dbg
    return nc, T, dbg


def prep_core(inp, b, h, NT, QLO, padded):
    S = NT * 128
    f32 = np.float32
    g = lambda k: np.asarray(inp[k], dtype=f32)[0]
    xb = np.asarray(inp["x"], dtype=f32)[b]
    if padded:
        half = S // 2
        if h == 0:
            x = np.concatenate([np.zeros((half, 1024), f32), xb[0:half]], 0)
            pos = np.arange(S) - half
        else:
            x = xb[0:S]
            pos = np.arange(S)
    else:
        x = xb[0:S]
        pos = np.arange(S)
    padlen = int((pos < 0).sum())
    NOWN = NT - QLO
    NCMP = 8 * NT - 1
    NCC = (8 * NT + 127) // 128
    NCPAD = NCC * 128
    NSEL = 2 * NT
    d = {"x": np.ascontiguousarray(x)}
    posc = np.maximum(pos, 0).astype(f32)
    inv = (f32(10000.0) ** (-np.arange(0, 64, 2, dtype=f32) / f32(64))).astype(f32)
    ang = (posc[:, None] * inv[None, :]).astype(f32)
    d["cos"] = np.cos(ang).astype(f32)
    d["sin"] = np.sin(ang).astype(f32)
    d["w_in"] = g("w_in")
    d["n1w"] = np.ascontiguousarray(g("norm1_w").reshape(8, 128).T)
    d["n2w"] = np.ascontiguousarray(g("norm2_w").reshape(8, 128).T)
    tabA = np.zeros((TA_N,), f32)
    tabA[TA_KNW:TA_KNW + 384] = np.concatenate([np.tile(g("kcmp_norm_w"), 2), np.tile(g("kslc_norm_w"), 2), np.tile(g("kwin_norm_w"), 2)])
    tabA[TA_QNW:TA_QNW + 512] = np.tile(g("q_norm_w"), 8)
    tabA[TA_BIG:TA_BIG + 4] = g("m_igate_b")
    tabA[TA_BFG:TA_BFG + 4] = g("m_fgate_b")
    tabA[TA_MONW:TA_MONW + 512] = g("m_out_norm_w").reshape(-1)
    d["tabA"] = np.ascontiguousarray(np.tile(tabA[None, :], (128, 1)))
    mcw = g("m_conv_w")
    mcb = g("m_conv_b")
    ch_of = [256 + np.arange(128), 384 + np.arange(128), np.arange(128), 128 + np.arange(128)]
    cw = np.zeros((128, 4, 4), f32)
    cb = np.zeros((128, 4), f32)
    for c in range(4):
        cw[:, c, :] = mcw[:, ch_of[c]].T
        cb[:, c] = mcb[ch_of[c]]
    d["cw"] = cw.reshape(128, 16)
    d["cb"] = cb
    fw = g("ffn_conv_w")
    d["fcw"] = np.ascontiguousarray(fw.reshape(3, 22, 128).transpose(2, 1, 0).reshape(128, 66))
    d["fcb"] = np.ascontiguousarray(g("ffn_conv_b").reshape(22, 128).T)
    d["mgb"] = g("merge_gate_b").reshape(1, 2048)
    kpe, vpe = g("cmp_k_pe"), g("cmp_v_pe")
    d["pe2"] = np.ascontiguousarray(np.concatenate([kpe, kpe, vpe, vpe], 1))
    p = np.arange(128)
    cst = np.zeros((128, 128 * 3 + 2 + 64 + 1), f32)
    cst[:, 0:128] = np.eye(128)
    cst[:, 128:256] = ((p[:, None] // 64 == p[None, :] // 64) & (p[:, None] <= p[None, :]))
    cst[:, 256:384] = 1.0
    cst[:, 384] = p < 64
    cst[:, 385] = p >= 64
    cst[:, 386:450] = (p[:, None] % 64) <= np.arange(64)[None, :]
    cst[:, 450] = 0.0 if (padded and h == 0) else 1.0
    d["cst"] = cst
    n = np.arange(NCPAD)
    j = np.arange(NSEL)
    ov = ((n[:, None] * 16 <= j[None, :] * 64 + 63) & (n[:, None] * 16 + 31 >= j[None, :] * 64) & (n[:, None] < NCMP)).astype(f32)
    d["ovl"] = np.ascontiguousarray(ov.reshape(NCC, 128, NSEL).transpose(1, 0, 2).reshape(128, NCC * NSEL))
    kk = np.arange(S)
    d["ewin"] = ((kk[None, :] // 64) == (64 * (kk[None, :] // 4096) + np.arange(64)[:, None])).astype(f32)
    diag = np.where(p[:, None] > p[None, :], NEGB, 0.0).astype(f32)
    anti = np.where(p[:, None] <= p[None, :], NEGB, 0.0).astype(f32)
    d["cbm"] = np.ascontiguousarray(np.concatenate([np.tile(diag, (1, 4)), np.tile(anti, (1, 4))], 1))
    tpos = pos[QLO * 128:].reshape(NOWN, 128)
    cend_real = 16 * n + 31 - padlen
    cvalid = (16 * n >= padlen) & (n < NCMP)
    d["cmask"] = np.where(cvalid[None, None, :] & (cend_real[None, None, :] <= tpos[:, :, None]), 0.0, NEGB * 8).astype(ml_dtypes.bfloat16)
    jr = j - padlen // 64
    cur = tpos // 64
    forced = (jr[None, None, :] == 0) | (jr[None, None, :] == cur[:, :, None]) | (jr[None, None, :] == cur[:, :, None] - 1)
    bad = (jr[None, None, :] > cur[:, :, None]) | (jr[None, None, :] < 0)
    d["imptab"] = np.where(bad, -1e30, np.where(forced, 1e4, 0.0)).astype(f32)
    d["rv"] = np.ascontiguousarray((tpos >= 31).astype(f32).T)
    d["kb"] = np.ascontiguousarray(np.where(pos.reshape(NT, 128).T < 0, NEGB, 0.0).astype(f32))
    for k in ("cmp_k_w1", "cmp_k_w2", "cmp_v_w1", "cmp_v_w2", "w_up_m", "w_up_n", "w_out", "ffn_w_up", "ffn_w_down"):
        d[k] = g(k)
    return d


NT_FULL, QLO_FULL, RESET_FULL = 64, 31, 32
_CACHE = {}


def kernel(**inputs):
    B = int(np.asarray(inputs["x"]).shape[0])
    if "nc" not in _CACHE:
        _CACHE["nc"] = build(NT_FULL, QLO_FULL, RESET_FULL)[0]
    nc = _CACHE["nc"]
    in_maps = []
    for b in range(B):
        for h in range(2):
            in_maps.append(prep_core(inputs, b, h, NT_FULL, QLO_FULL, True))
    res = run_bass_kernel_spmd(nc, in_maps, core_ids=list(range(2 * B)))
    out = np.zeros((B, NT_FULL * 128, 1024), np.float32)
    half = NT_FULL * 64
    for b in range(B):
        for h in range(2):
            o = np.asarray(res.results[b * 2 + h]["out"], dtype=np.float32).reshape(-1, 1024)
            out[b, h * half:(h + 1) * half] = o[128:128 + half]
    return out
```

```python
import math
from contextlib import ExitStack
import numpy as np
import ml_dtypes
import concourse.bass as bass
import concourse.mybir as mybir
from concourse.bass_utils import run_bass_kernel_spmd

F32 = mybir.dt.float32
BF16 = mybir.dt.bfloat16
AF = mybir.ActivationFunctionType
ALU = mybir.AluOpType
AX = mybir.AxisListType
EPS = 1e-6
NEGB = -30000.0


class Buf:
    __slots__ = ("name", "w", "r", "dsem", "dcnt", "excl", "rg")

    def __init__(self, name, excl=False):
        self.name = name
        self.rg = None
        self.excl = excl
        self.w = None
        self.r = {}
        self.dsem = None
        self.dcnt = 0


class Trk:
    ENG = ("pe", "act", "dve", "pool", "sp")

    def __init__(self, nc):
        self.nc = nc
        self.e = {"pe": nc.tensor, "act": nc.scalar, "dve": nc.vector,
                  "pool": nc.gpsimd, "sp": nc.sync}
        self.sems = {}
        self.cnt = {}
        for k in ("pe", "act", "dve", "pool"):
            self.sems[k] = nc.alloc_semaphore("s_" + k)
            self.cnt[k] = 0
        self.known = {k: {} for k in self.ENG}
        self.nd = 0
        self.dbufs = []
        self.ninstr = 0

    def _wait(self, eng, deps):
        kn = self.known[eng]
        best = {}
        for (s, v) in deps:
            if s == eng and v > self.cnt[eng]:
                continue
            if kn.get(s, 0) < v and best.get(s, 0) < v:
                best[s] = v
        for s, v in best.items():
            self.e[eng].wait_ge(self.sems[s], v)
            kn[s] = v

    @staticmethod
    def _deps(reads, writes):
        deps = []
        for b in reads:
            if b.w is not None:
                deps.append(b.w)
        for b in writes:
            if b.w is not None:
                deps.append(b.w)
            deps.extend(b.r.items())
        return deps

    def op(self, eng, fn, reads=(), writes=(), sig=True, rg="f"):
        ex = [b for b in reads if b.excl]
        if ex:
            reads = [b for b in reads if not b.excl]
            writes = list(writes) + ex
        deps = self._deps(reads, writes)
        if eng == "pe":
            drop = set()
            for b in writes:
                if b.w is not None and b.w[0] == "pe" and not ({b.rg, rg} == {0, 64}):
                    drop.add(b.w)
            keep = set()
            for b in writes:
                if b.w is not None and b.w[0] == "pe" and ({b.rg, rg} == {0, 64}):
                    keep.add(b.w)
                for it in b.r.items():
                    if it[0] == "pe":
                        keep.add(it)
            for b in reads:
                if b.w is not None and b.w[0] == "pe":
                    keep.add(b.w)
            deps = [d_ for d_ in deps if not (d_ in drop and d_ not in keep)]
            for b in writes:
                b.rg = rg
        self._wait(eng, deps)
        ins = fn()
        self.ninstr += 1
        if sig:
            self.cnt[eng] += 1
            ins.then_inc(self.sems[eng], 1)
            v = self.cnt[eng]
        else:
            v = self.cnt[eng] + 1
        for b in reads:
            b.r[eng] = v
        for b in writes:
            b.w = (eng, v)
            b.r = {}
        return ins

    def dma(self, q, pairs, reads=(), writes=(), sembuf=None):
        sb = sembuf if sembuf is not None else (writes[0] if writes else reads[0])
        if sb.dsem is None:
            key = "d%d" % self.nd
            self.nd += 1
            sb.dsem = key
            self.sems[key] = self.nc.alloc_semaphore(key)
            self.dbufs.append(sb)
        deps = self._deps(reads, writes)
        if sb.dcnt > 0:
            deps.append((sb.dsem, sb.dcnt))
        self._wait(q, deps)
        for (o, i) in pairs:
            self.e[q].dma_start(out=o, in_=i).then_inc(self.sems[sb.dsem], 16)
            sb.dcnt += 16
            self.ninstr += 1
        ev = (sb.dsem, sb.dcnt)
        for b in reads:
            b.r[ev[0]] = ev[1]
        for b in writes:
            b.w = ev
            b.r = {}
        return ev

    def barrier(self):
        deps = [(k, self.cnt[k]) for k in ("pe", "act", "dve", "pool") if self.cnt[k] > 0]
        deps += [(b.dsem, b.dcnt) for b in self.dbufs if b.dcnt > 0]
        for eng in self.ENG:
            self._wait(eng, deps)


def pipeline(items, body, skew, maxact=2):
    it = iter(items)
    act = []
    done = False
    while True:
        if not done and len(act) < maxact and (not act or act[-1][1] >= skew):
            try:
                act.append([body(next(it)), 0])
            except StopIteration:
                done = True
        if not act:
            if done:
                break
            continue
        for a in list(act):
            try:
                next(a[0])
                a[1] += 1
            except StopIteration:
                act.remove(a)


class TB:
    def __init__(self, t, name, excl=False):
        self.t = t
        self.b = Buf(name, excl)

    def __getitem__(self, idx):
        return self.t[idx]


IN_OFF = {}
_o = 0
for _n, _s in (("mq", 256), ("mk", 256), ("mv", 512), ("mo", 512), ("mi", 4), ("mf", 4),
               ("nq", 512), ("kc", 128), ("vc", 128), ("ks", 128), ("vs", 128), ("kw", 128),
               ("vw", 128), ("ng", 24), ("gm", 1024), ("gn", 1024)):
    IN_OFF[_n] = (_o, _s)
    _o += _s
D_IN = _o
D_FF = 2816

WA = {}
_o = 0
for _n in ("mv", "kc", "ks", "kw", "mi", "mf", "vs", "vw", "mo", "mk", "mq", "vc"):
    WA[_n] = _o
    _o += IN_OFF[_n][1]
WA_N = _o
TA_KNW, TA_QNW, TA_BIG, TA_BFG, TA_MONW, TA_N = 0, 384, 896, 900, 904, 1416


def build(NT, QLO, RESET_TILE, debug=0):
    S = NT * 128
    NOWN = NT - QLO
    NCMP = 8 * NT - 1
    NCC = (8 * NT + 127) // 128
    NCPAD = NCC * 128
    NSEL = 2 * NT
    nc = bass.Bass("TRN2", target_bir_lowering=False)
    T = Trk(nc)

    def din(name, shape, dt=F32):
        return nc.dram_tensor(name, list(shape), dt, kind="ExternalInput").ap()

    x_d = din("x", [S, 1024])
    cos_d = din("cos", [S, 32])
    sin_d = din("sin", [S, 32])
    w_in_d = din("w_in", [1024, D_IN])
    n1w_d = din("n1w", [128, 8])
    n2w_d = din("n2w", [128, 8])
    tabA_d = din("tabA", [128, TA_N])
    cw_d = din("cw", [128, 16])
    cb_d = din("cb", [128, 4])
    fcw_d = din("fcw", [128, 66])
    fcb_d = din("fcb", [128, 22])
    mgb_d = din("mgb", [1, 2048])
    pe2_d = din("pe2", [32, 256])
    cst_d = din("cst", [128, 128 * 3 + 2 + 64 + 1])
    ovl_d = din("ovl", [128, NCC * NSEL])
    ewin_d = din("ewin", [64, S])
    cbm_d = din("cbm", [128, 1024])
    cmask_d = din("cmask", [NOWN, 128, NCPAD], BF16)
    imptab_d = din("imptab", [NOWN, 128, NSEL])
    kb_d = din("kb", [128, NT])
    rv_d = din("rv", [128, NOWN])
    w1k_d = din("cmp_k_w1", [2048, 256])
    w2k_d = din("cmp_k_w2", [256, 64])
    w1v_d = din("cmp_v_w1", [2048, 256])
    w2v_d = din("cmp_v_w2", [256, 64])
    wupm_d = din("w_up_m", [512, 1024])
    wupn_d = din("w_up_n", [512, 1024])
    wout_d = din("w_out", [1024, 1024])
    fup_d = din("ffn_w_up", [1024, 2 * D_FF])
    fdn_d = din("ffn_w_down", [D_FF, 1024])
    out_d = nc.dram_tensor("out", [NOWN, 128, 1024], F32, kind="ExternalOutput").ap()
    out_b = Buf("out")
    hmT_d = nc.dram_tensor("hmT_scr", [NOWN, 128, 512], BF16, kind="Internal").ap()
    onT_d = nc.dram_tensor("onT_scr", [NOWN, 128, 512], BF16, kind="Internal").ap()
    kcs_d = nc.dram_tensor("kcs_scr", [128, S], BF16, kind="Internal").ap()
    vcs_d = nc.dram_tensor("vcs_scr", [128, S], BF16, kind="Internal").ap()
    kvs_db = Buf("kvs_scr")
    hmT_db = [Buf("hmTd%d" % i) for i in range(NOWN)]
    onT_db = [Buf("onTd%d" % i) for i in range(NOWN)]
    xmid_db = [Buf("xmid%d" % i) for i in range(NOWN)]
    dbg = {}
    if debug:
        def dout(name, shape, dt=F32):
            dbg[name] = (nc.dram_tensor(name, list(shape), dt, kind="ExternalOutput").ap(), Buf(name))
            return dbg[name]

    def V(fn, r=(), w=()):
        return T.op("dve", fn, [a.b for a in r], [a.b for a in w])

    def A(fn, r=(), w=()):
        return T.op("act", fn, [a.b for a in r], [a.b for a in w])

    def G(fn, r=(), w=()):
        return T.op("pool", fn, [a.b for a in r], [a.b for a in w])

    def P(fn, r=(), w=(), sig=True, rg="f"):
        return T.op("pe", fn, [a.b for a in r], [a.b for a in w], sig, rg)

    def D(pairs, r=(), w=(), q="sp", sembuf=None):
        if sembuf is None:
            cand = [a for a in list(w) + list(r) if isinstance(a, TB)]
            sembuf = cand[0].b if cand else None
        return T.dma(q, pairs, [a if isinstance(a, Buf) else a.b for a in r],
                     [a if isinstance(a, Buf) else a.b for a in w], sembuf)

    main = ExitStack()

    def mk(es, name, shape, dt=F32):
        return TB(es.enter_context(nc.sbuf_tensor("sb_" + name, list(shape), dt)), name)

    PS = [TB(nc.alloc_psum_tensor("ps%d" % i, [128, 512], F32), "ps%d" % i, True) for i in range(8)]

    def bfv(ps, ncols):
        return ps[:, 0:ncols // 2].bitcast(BF16)

    cst = mk(main, "cst", [128, 128 * 3 + 2 + 64 + 1])
    D([(cst[:], cst_d)], w=[cst])
    identf = cst[:, 0:128]
    U2 = cst[:, 128:256]
    onesf = cst[:, 256:384]
    m01 = cst[:, 384:386]
    mask_st = cst[:, 386:450]
    flag = cst[:, 450:451]
    identb = mk(main, "identb", [128, 128], BF16)
    V(lambda: nc.vector.tensor_copy(out=identb[:], in_=identf), [cst], [identb])
    onesb = mk(main, "onesb", [128, 128], BF16)
    V(lambda: nc.vector.tensor_copy(out=onesb[:], in_=onesf), [cst], [onesb])
    tabA = mk(main, "tabA", [128, TA_N])
    D([(tabA[:], tabA_d)], w=[tabA])
    n1w = mk(main, "n1w", [128, 8])
    D([(n1w[:], n1w_d)], w=[n1w])
    n2w = mk(main, "n2w", [128, 8])
    D([(n2w[:], n2w_d)], w=[n2w])
    kbias = mk(main, "kbias", [128, NT])
    D([(kbias[:], kb_d)], w=[kbias])
    rvt = mk(main, "rvt", [128, NOWN])
    D([(rvt[:], rv_d)], w=[rvt])

    def load_w(es_stage, dst_tb, dst_ap, src_ap, shape, scale_ap=None, eng="dve", scale_tb=None):
        st = es_stage[0][es_stage[1] % len(es_stage[0])]
        es_stage[1] += 1
        if len(shape) == 2:
            sv = st[:, 0:shape[1]]
        else:
            sv = st[:, 0:shape[1] * shape[2]].rearrange("p (a n) -> p a n", a=shape[1])
        D([(sv, src_ap)], w=[st])
        if scale_ap is None:
            if eng == "dve":
                V(lambda: nc.vector.tensor_copy(out=dst_ap, in_=sv), [st], [dst_tb])
            elif eng == "act":
                A(lambda: nc.scalar.copy(out=dst_ap, in_=sv), [st], [dst_tb])
            else:
                G(lambda: nc.gpsimd.tensor_copy(out=dst_ap, in_=sv), [st], [dst_tb])
        else:
            V(lambda: nc.vector.tensor_tensor(out=dst_ap, in0=sv, in1=scale_ap, op=ALU.mult), [st, scale_tb], [dst_tb])

    nhalf = mk(main, "nhalf", [128, 8])
    G(lambda: nc.gpsimd.memset(nhalf[:], -0.5), [], [nhalf])

    def rstd_pow(rs, ss, n):
        w = ss.t.shape[1]
        G(lambda: nc.gpsimd.tensor_scalar(out=rs[:], in0=ss[:], scalar1=1.0 / n, scalar2=EPS, op0=ALU.mult, op1=ALU.add), [ss], [rs])
        G(lambda: nc.gpsimd.tensor_tensor(out=rs[:], in0=rs[:], in1=nhalf[:, 0:w], op=ALU.pow), [rs, nhalf], [rs])

    def rmsnorm_T(xt, tmp, ss, rs, xn, xnT, psT, evac="dve"):
        A(lambda: nc.scalar.activation(out=tmp[:], in_=xt[:], func=AF.Square, accum_out=ss[:]), [xt], [tmp, ss])
        rstd_pow(rs, ss, 1024)
        A(lambda: nc.scalar.activation(out=xn[:], in_=xt[:], func=AF.Copy, scale=rs[:]), [xt, rs], [xn])
        pv = bfv(psT, 1024)
        for c in range(8):
            P(lambda: nc.tensor.transpose(out=pv[:, c * 128:(c + 1) * 128], in_=xn[:, c * 128:(c + 1) * 128],
                                          identity=identb[:]), [xn, identb], [psT], sig=(c == 7))
        if evac == "dve":
            V(lambda: nc.vector.tensor_copy(out=xnT[:].rearrange("p c t -> p (c t)"), in_=pv), [psT], [xnT])
        else:
            A(lambda: nc.scalar.copy(out=xnT[:].rearrange("p c t -> p (c t)"), in_=pv), [psT], [xnT])

    kvs = ExitStack()
    KE = [mk(kvs, "KE%d" % g, [128, S], BF16) for g in range(2)]
    kwT = mk(kvs, "kwT", [128, S], BF16)
    vsA = mk(kvs, "vsA", [128, NT, 2, 65], BF16)
    vwA = mk(kvs, "vwA", [128, NT, 2, 65], BF16)
    kcmpT = mk(kvs, "kcmpT", [128, NCPAD], BF16)
    vcmp = mk(kvs, "vcmp", [128, NCC, 2, 64], BF16)
    G(lambda: nc.gpsimd.memset(vsA[:], 1.0), [], [vsA])
    G(lambda: nc.gpsimd.memset(vwA[:], 1.0), [], [vwA])
    G(lambda: nc.gpsimd.memset(kcmpT[:], 0.0), [], [kcmpT])
    G(lambda: nc.gpsimd.memset(vcmp[:], 0.0), [], [vcmp])

    pa0 = ExitStack()
    pa = ExitStack()
    wA = mk(pa, "wA", [128, 8, WA_N], BF16)
    with ExitStack() as stg:
        stage = [[mk(stg, "stgA%d" % i, [128, 2048]) for i in range(2)], 0]
        w_in_v = w_in_d.rearrange("(c p) n -> p c n", p=128)
        for name in ("mv", "kc", "ks", "kw", "mi", "mf", "vs", "vw", "mo", "mk", "mq", "vc"):
            so, sn = IN_OFF[name]
            do = WA[name]
            for o in range(0, sn, 256):
                n = min(256, sn - o)
                load_w(stage, wA, wA[:, :, do + o:do + o + n], w_in_v[:, :, so + o:so + o + n], [128, 8, n],
                       scale_ap=n1w[:, :].unsqueeze(2).broadcast_to([128, 8, n]), scale_tb=n1w)
        T.barrier()
    xt2 = [mk(pa, "xt%d" % i, [128, 1024]) for i in range(2)]
    two = lambda nm, shp, dt=F32: [mk(pa, "%s_%d" % (nm, i), shp, dt) for i in range(2)]
    three = lambda nm, shp, dt=F32: [mk(pa, "%s_%d" % (nm, i), shp, dt) for i in range(4)]
    ss1_2, rs1_2 = two("ss1", [128, 1]), two("rs1", [128, 1])
    xn_2 = two("xn", [128, 1024], BF16)
    xnT_2 = two("xnT", [128, 8, 128], BF16)
    cs2 = two("cs", [128, 64])
    ksq_2 = two("ksq", [128, 384])
    ss6_2, rs6_2 = two("ss6", [128, 6]), two("rs6", [128, 6])
    kn_2 = two("kn", [128, 6, 64])
    rt_2 = [two("rt%d" % i, [128, 6, 32]) for i in range(4)]
    kr_2 = two("kr", [128, 6, 64], BF16)
    convb_2 = two("convb", [128, 4, 131])
    cacc_2 = two("cacc", [128, 4, 128])
    cw = mk(pa, "cw", [128, 4, 4])
    cb = mk(pa, "cb", [128, 4])
    D([(cw[:].rearrange("p a b -> p (a b)"), cw_d)], w=[cw])
    D([(cb[:], cb_d)], w=[cb])
    zg_2, sp_2, ip_2 = two("zg", [128, 4]), two("sp", [128, 4]), two("ip", [128, 4])
    sp8_2, es_2, expg_2 = two("sp8", [128, 8]), two("es", [128, 4]), two("expg", [128, 8])
    kvst_2 = two("kvst", [128, 2, 128], BF16)
    kqT2 = three("kqT", [128, 4, 128], BF16)
    ktil2 = three("ktil", [128, 4, 64], BF16)
    vaug2 = three("vaug", [128, 4, 129], BF16)
    esm2 = three("esm", [128, 4, 64])
    eb82 = three("eb8", [128, 4])
    egp2 = three("egp", [128, 2, 2])
    sigo2 = three("sigo", [128, 512])
    for v_ in vaug2:
        G(lambda: nc.gpsimd.memset(v_[:], 1.0), [], [v_])
    for c_ in convb_2:
        V(lambda: nc.vector.memset(c_[:], 0.0), [], [c_])
    Cst = mk(pa, "Cst", [128, 2, 129])
    snap = [mk(pa, "snap%d" % i, [128, 2, 129], BF16) for i in range(8)]
    V(lambda: nc.vector.memset(Cst[:], 0.0), [], [Cst])
    V(lambda: nc.vector.memset(snap[0][:], 0.0), [], [snap[0]])
    PT = mk(pa, "PT", [128, 4, 64], BF16)
    d4 = [mk(pa, "d4_%d" % i, [128, 4]) for i in range(3)]
    hraw = mk(pa, "hraw", [128, 4, 128])
    ss4 = mk(pa, "ss4", [128, 4])
    rs4 = mk(pa, "rs4", [128, 4])
    hm = mk(pa, "hm", [128, 512], BF16)
    hmTs = mk(pa, "hmTs", [128, 512], BF16)
    lnc = math.log(0.125)
    psT, pA_, pB_, psS, psKV, psSTm, psO0, psO1 = PS
    prot = {"i": 0}

    def a_front(t):
        own = t >= QLO
        needq = t >= QLO - 1
        p_ = t % 2
        h_ = t % 4
        xt, cs = xt2[p_], cs2[p_]
        vaug, kqT, ktil, esm, eb8, egp, sigo = vaug2[h_], kqT2[h_], ktil2[h_], esm2[h_], eb82[h_], egp2[h_], sigo2[h_]
        ss1, rs1, xn, xnT, ksq, ss6, rs6, kn, kr = ss1_2[p_], rs1_2[p_], xn_2[p_], xnT_2[p_], ksq_2[p_], ss6_2[p_], rs6_2[p_], kn_2[p_], kr_2[p_]
        rt = [rt_2[i][p_] for i in range(4)]
        convb, convn, cacc = convb_2[p_], convb_2[1 - p_], cacc_2[p_]
        zg, sp, ip, sp8, es_, expg, kvst = zg_2[p_], sp_2[p_], ip_2[p_], sp8_2[p_], es_2[p_], expg_2[p_], kvst_2[p_]
        sqj = xn
        D([(xt[:], x_d[t * 128:(t + 1) * 128, :])], w=[xt])
        D([(cs[:, 0:32], cos_d[t * 128:(t + 1) * 128, :]), (cs[:, 32:64], sin_d[t * 128:(t + 1) * 128, :])], w=[cs])
        A(lambda: nc.scalar.activation(out=sqj[:], in_=xt[:], func=AF.Square, accum_out=ss1[:]), [xt], [sqj, ss1])
        yield
        rstd_pow(rs1, ss1, 1024)
        yield
        A(lambda: nc.scalar.activation(out=xn[:], in_=xt[:], func=AF.Copy, scale=rs1[:]), [xt, rs1], [xn])
        yield
        pv = bfv(psT, 1024)
        for c in range(8):
            P(lambda: nc.tensor.transpose(out=pv[:, c * 128:(c + 1) * 128], in_=xn[:, c * 128:(c + 1) * 128], identity=identb[:]),
              [xn, identb], [psT], sig=(c == 7))
        V(lambda: nc.vector.tensor_copy(out=xnT[:].rearrange("p c t -> p (c t)"), in_=pv), [psT], [xnT])
        yield

        def bank():
            prot["i"] += 1
            return (pA_, pB_)[prot["i"] % 2]

        def proj_tm(ps, col0, ncols, wcol):
            for c in range(8):
                P(lambda: nc.tensor.matmul(out=ps[:, col0:col0 + ncols], lhsT=xnT[:, c, :], rhs=wA[:, c, wcol:wcol + ncols],
                                           start=(c == 0), stop=(c == 7)), [xnT, wA], [ps], sig=(c == 7))

        def proj_fm(ps, col0, wcol):
            for c in range(8):
                P(lambda: nc.tensor.matmul(out=ps[:, col0:col0 + 128], lhsT=wA[:, c, wcol:wcol + 128], rhs=xnT[:, c, :],
                                           start=(c == 0), stop=(c == 7)), [xnT, wA], [ps], sig=(c == 7))

        psK = bank()
        proj_tm(psK, 0, 392, WA["kc"])
        A(lambda: nc.scalar.activation(out=ksq[:], in_=psK[:, 0:384], func=AF.Square), [psK], [ksq])
        V(lambda: nc.vector.tensor_copy(out=kn[:].rearrange("p a d -> p (a d)"), in_=psK[:, 0:384]), [psK], [kn])
        V(lambda: nc.vector.tensor_tensor(out=zg[:], in0=psK[:, 388:392], in1=tabA[:, TA_BFG:TA_BFG + 4], op=ALU.add), [psK, tabA], [zg])
        V(lambda: nc.vector.tensor_tensor(out=ip[:], in0=psK[:, 384:388], in1=tabA[:, TA_BIG:TA_BIG + 4], op=ALU.add), [psK, tabA], [ip])
        yield
        psMV = bank()
        proj_tm(psMV, 0, 512, WA["mv"])
        A(lambda: nc.scalar.copy(out=vaug[:, :, 0:128], in_=psMV[:, :].rearrange("p (h d) -> p h d", h=4)), [psMV], [vaug])
        yield
        psV = bank()
        proj_tm(psV, 0, 256, WA["vs"])
        proj_fm(psV, 256, WA["vc"])
        tsl = slice(t * 128, (t + 1) * 128)
        A(lambda: nc.scalar.copy(out=vsA[:, t, :, 0:64], in_=psV[:, 0:128].rearrange("p (g d) -> p g d", g=2)), [psV], [vsA])
        A(lambda: nc.scalar.copy(out=vwA[:, t, :, 0:64], in_=psV[:, 128:256].rearrange("p (g d) -> p g d", g=2)), [psV], [vwA])
        V(lambda: nc.vector.tensor_copy(out=kvst[:, 1, :], in_=psV[:, 256:384]), [psV], [kvst])
        yield
        psF = bank()
        proj_fm(psF, 0, WA["mk"])
        proj_fm(psF, 128, WA["mk"] + 128)
        if needq:
            proj_fm(psF, 256, WA["mq"])
            proj_fm(psF, 384, WA["mq"] + 128)
        nch = 4 if needq else 2
        A(lambda: nc.scalar.copy(out=convb[:, 0:nch, 3:131], in_=psF[:, 0:nch * 128].rearrange("p (a t) -> p a t", a=nch)), [psF], [convb])
        yield
        if own:
            psMO = bank()
            proj_tm(psMO, 0, 512, WA["mo"])
            A(lambda: nc.scalar.activation(out=sigo[:], in_=psMO[:, 0:512], func=AF.Sigmoid), [psMO], [sigo])
            yield
        def chain_keys():
            V(lambda: nc.vector.tensor_reduce(out=ss6[:], in_=ksq[:].rearrange("p (a d) -> p a d", a=6), axis=AX.X, op=ALU.add), [ksq], [ss6])
            yield
            rstd_pow(rs6, ss6, 64)
            yield
            V(lambda: nc.vector.tensor_tensor(out=kn[:], in0=kn[:], in1=rs6[:, :].unsqueeze(2).broadcast_to([128, 6, 64]), op=ALU.mult), [kn, rs6], [kn])
            V(lambda: nc.vector.tensor_tensor(out=kn[:], in0=kn[:], in1=tabA[:, TA_KNW:TA_KNW + 384].rearrange("p (a d) -> p a d", a=6),
                                              op=ALU.mult), [kn, tabA], [kn])
            yield
            cosb = cs[:, 0:32].unsqueeze(1).broadcast_to([128, 6, 32])
            sinb = cs[:, 32:64].unsqueeze(1).broadcast_to([128, 6, 32])
            V(lambda: nc.vector.tensor_tensor(out=rt[0][:], in0=kn[:, :, 0:32], in1=cosb, op=ALU.mult), [kn, cs], [rt[0]])
            G(lambda: nc.gpsimd.tensor_tensor(out=rt[1][:], in0=kn[:, :, 32:64], in1=sinb, op=ALU.mult), [kn, cs], [rt[1]])
            V(lambda: nc.vector.tensor_tensor(out=rt[2][:], in0=kn[:, :, 32:64], in1=cosb, op=ALU.mult), [kn, cs], [rt[2]])
            G(lambda: nc.gpsimd.tensor_tensor(out=rt[3][:], in0=kn[:, :, 0:32], in1=sinb, op=ALU.mult), [kn, cs], [rt[3]])
            yield
            V(lambda: nc.vector.tensor_tensor(out=kr[:, :, 0:32], in0=rt[0][:], in1=rt[1][:], op=ALU.subtract), [rt[0], rt[1]], [kr])
            G(lambda: nc.gpsimd.tensor_tensor(out=kr[:, :, 32:64], in0=rt[2][:], in1=rt[3][:], op=ALU.add), [rt[2], rt[3]], [kr])
            yield
            pk = bfv(psS, 512)
            P(lambda: nc.tensor.transpose(out=pk[:, 0:128], in_=kr[:, 0:2, :].rearrange("p a d -> p (a d)"), identity=identb[:]), [kr, identb], [psS], sig=False)
            P(lambda: nc.tensor.transpose(out=pk[:, 128:256], in_=kr[:, 4:6, :].rearrange("p a d -> p (a d)"), identity=identb[:]), [kr, identb], [psS], sig=False)
            for g in range(2):
                P(lambda: nc.tensor.transpose(out=pk[0:64, 256 + g * 128:256 + (g + 1) * 128], in_=kr[:, 2 + g, :], identity=identb[:]), [kr, identb], [psS],
                  sig=(g == 1))
            A(lambda: nc.scalar.copy(out=kvst[:, 0, :], in_=pk[:, 0:128]), [psS], [kvst])
            A(lambda: nc.scalar.copy(out=kwT[:, tsl], in_=pk[:, 128:256]), [psS], [kwT])
            for g in range(2):
                V(lambda: nc.vector.tensor_copy(out=KE[g][0:64, tsl], in_=pk[0:64, 256 + g * 128:256 + (g + 1) * 128]), [psS], [KE[g]])
            D([(kcs_d[:, tsl], kvst[:, 0, :]), (vcs_d[:, tsl], kvst[:, 1, :])], r=[kvst], w=[kvs_db])
            yield

        def chain_conv():
            for j in range(nch):
                V(lambda: nc.vector.tensor_scalar(out=cacc[:, j, :], in0=convb[:, j, 0:128], scalar1=cw[:, j, 0:1], scalar2=None, op0=ALU.mult),
                  [convb, cw], [cacc])
                for k in range(1, 4):
                    V(lambda: nc.vector.scalar_tensor_tensor(out=cacc[:, j, :], in0=convb[:, j, k:k + 128], scalar=cw[:, j, k:k + 1],
                                                             in1=cacc[:, j, :], op0=ALU.mult, op1=ALU.add), [convb, cw, cacc], [cacc])
                A(lambda: nc.scalar.activation(out=kqT[:, j, :], in_=cacc[:, j, :], func=AF.Silu, bias=cb[:, j:j + 1]), [cacc, cb], [kqT])
                yield
            G(lambda: nc.gpsimd.tensor_copy(out=convn[:, 0:nch, 0:3], in_=convb[:, 0:nch, 128:131]), [convb], [convn])
            yield

        def chain_gates():
            A(lambda: nc.scalar.activation(out=zg[:], in_=zg[:], func=AF.Exp, scale=-1.0), [zg], [zg])
            yield
            A(lambda: nc.scalar.activation(out=sp[:], in_=zg[:], func=AF.Ln, bias=1.0), [zg], [sp])
            yield
            V(lambda: nc.vector.tensor_scalar(out=sp8[:, 0:4], in0=sp[:], scalar1=m01[:, 0:1], scalar2=None, op0=ALU.mult), [sp, cst], [sp8])
            V(lambda: nc.vector.tensor_scalar(out=sp8[:, 4:8], in0=sp[:], scalar1=m01[:, 1:2], scalar2=None, op0=ALU.mult), [sp, cst], [sp8])
            yield

        chains = [chain_keys(), chain_conv(), chain_gates()]
        while chains:
            for g_ in list(chains):
                if next(g_, "end") == "end":
                    chains.remove(g_)
            yield
        P(lambda: nc.tensor.matmul(out=psS[:, 256:260], lhsT=U2, rhs=sp[:], start=True, stop=True), [cst, sp], [psS])
        P(lambda: nc.tensor.matmul(out=psS[:, 264:272], lhsT=onesf, rhs=sp8[:], start=True, stop=True), [cst, sp8], [psS])
        pkt = psS[:, 272:400].bitcast(BF16)
        for j in range(2):
            P(lambda: nc.tensor.transpose(out=pkt[:, j * 128:(j + 1) * 128], in_=kqT[:, j, :], identity=identb[:]), [kqT, identb], [psS], sig=(j == 1))
        V(lambda: nc.vector.tensor_tensor(out=es_[:], in0=psS[:, 256:260], in1=ip[:], op=ALU.add), [psS, ip], [es_])
        A(lambda: nc.scalar.activation(out=eb8[:], in_=psS[:, 256:260], func=AF.Exp, scale=-1.0, bias=lnc), [psS], [eb8])
        A(lambda: nc.scalar.activation(out=expg[:], in_=psS[:, 264:272], func=AF.Exp, scale=-1.0), [psS], [expg])
        A(lambda: nc.scalar.activation(out=es_[:], in_=es_[:], func=AF.Exp), [es_], [es_])
        egv = expg[:, :].rearrange("p (ch c par) -> p ch c par", ch=2, c=2)
        V(lambda: nc.vector.tensor_copy(out=egp[0:64, :, :], in_=egv[0:64, :, :, 0]), [expg], [egp])
        V(lambda: nc.vector.tensor_copy(out=egp[64:128, :, :], in_=egv[64:128, :, :, 1]), [expg], [egp])
        V(lambda: nc.vector.tensor_tensor(out=ktil[:], in0=pkt.rearrange("p (h d) -> p h d", h=4),
                                          in1=es_[:, :].unsqueeze(2).broadcast_to([128, 4, 64]), op=ALU.mult), [psS, es_], [ktil])
        if own:
            V(lambda: nc.vector.tensor_tensor(out=esm[:], in0=mask_st.unsqueeze(1).broadcast_to([128, 4, 64]),
                                              in1=es_[:, :].unsqueeze(2).broadcast_to([128, 4, 64]), op=ALU.mult), [cst, es_], [esm])
        yield

    def a_scan(t):
        h_ = t % 4
        vaug, ktil, egp = vaug2[h_], ktil2[h_], egp2[h_]
        if RESET_TILE is not None and t == RESET_TILE:
            sn = snap[(2 * t) % 8]
            V(lambda: nc.vector.tensor_scalar(out=Cst[:], in0=Cst[:], scalar1=flag, scalar2=None, op0=ALU.mult), [Cst, cst], [Cst])
            V(lambda: nc.vector.tensor_scalar(out=sn[:], in0=sn[:], scalar1=flag, scalar2=None, op0=ALU.mult), [sn, cst], [sn])
        for ch in range(2):
            rows = slice(ch * 64, ch * 64 + 64)
            kvv = psKV[:, 0:258].rearrange("p (c e) -> p c e", c=2)
            for h in range(4):
                c, par = h // 2, h % 2
                P(lambda: nc.tensor.matmul(out=kvv[par * 64:(par + 1) * 64, c, :], lhsT=ktil[rows, h, :], rhs=vaug[rows, h, :],
                                           start=True, stop=True), [ktil, vaug], [psKV], sig=(h == 3), rg=ch * 64)
            V(lambda: nc.vector.tensor_tensor(out=Cst[:], in0=kvv, in1=Cst[:], op=ALU.add), [psKV, Cst], [Cst])
            yield
            V(lambda: nc.vector.tensor_tensor(out=Cst[:], in0=Cst[:], in1=egp[:, ch, :].unsqueeze(2).broadcast_to([128, 2, 129]),
                                              op=ALU.mult), [Cst, egp], [Cst])
            yield
            nx = snap[(2 * t + ch + 1) % 8]
            A(lambda: nc.scalar.copy(out=nx[:], in_=Cst[:]), [Cst], [nx])
            yield

    def a_out(t):
        h_ = t % 4
        vaug, kqT, esm, eb8, sigo = vaug2[h_], kqT2[h_], esm2[h_], eb82[h_], sigo2[h_]
        psO = (psO0, psO1)
        for ch in range(2):
            rows = slice(ch * 64, ch * 64 + 64)
            csl = slice(ch * 64, ch * 64 + 64)
            Cbf = snap[(2 * t + ch) % 8]
            stp = psSTm[:, 0:256].rearrange("p (h s) -> p h s", h=4)
            for h in range(4):
                c, par = h // 2, h % 2
                prow = slice(par * 64, par * 64 + 64)
                P(lambda: nc.tensor.matmul(out=stp[rows, h, :], lhsT=kqT[prow, c, csl], rhs=kqT[prow, 2 + c, csl],
                                           start=True, stop=True), [kqT], [psSTm], sig=True, rg=par * 64)
            V(lambda: nc.vector.tensor_tensor(out=PT[rows], in0=stp[rows], in1=esm[rows], op=ALU.mult), [psSTm, esm], [PT])
            yield
            for h in range(4):
                c, par = h // 2, h % 2
                prow = slice(par * 64, par * 64 + 64)
                ov = psO[c][:, 0:258].rearrange("p (a e) -> p a e", a=2)
                P(lambda: nc.tensor.matmul(out=ov[rows, par, :], lhsT=kqT[prow, 2 + c, csl], rhs=Cbf[prow, c, :],
                                           start=True, stop=False), [kqT, Cbf], [psO[c]], sig=True, rg=par * 64)
                P(lambda: nc.tensor.matmul(out=ov[rows, par, :], lhsT=PT[rows, h, :], rhs=vaug[rows, h, :],
                                           start=False, stop=True), [PT, vaug], [psO[c]], sig=True, rg=ch * 64)
            yield
        oi = t - QLO
        for c in range(2):
            ov = psO[c][:, 0:258].rearrange("p (a e) -> p a e", a=2)
            V(lambda: nc.vector.tensor_tensor(out=d4[0][:, 2 * c:2 * c + 2], in0=ov[:, :, 128], in1=eb8[:, 2 * c:2 * c + 2], op=ALU.mult),
              [psO[c], eb8], [d4[0]])
        V(lambda: nc.vector.scalar_tensor_tensor(out=d4[1][:], in0=d4[0][:], scalar=-1.0, in1=d4[0][:], op0=ALU.mult, op1=ALU.max), [d4[0]], [d4[1]])
        V(lambda: nc.vector.tensor_scalar(out=d4[1][:], in0=d4[1][:], scalar1=1.0, scalar2=None, op0=ALU.max), [d4[1]], [d4[1]])
        V(lambda: nc.vector.reciprocal(out=d4[1][:], in_=d4[1][:]), [d4[1]], [d4[1]])
        V(lambda: nc.vector.tensor_tensor(out=d4[2][:], in0=d4[1][:], in1=eb8[:], op=ALU.mult), [d4[1], eb8], [d4[2]])
        for c in range(2):
            ov = psO[c][:, 0:258].rearrange("p (a e) -> p a e", a=2)
            V(lambda: nc.vector.tensor_tensor(out=hraw[:, 2 * c:2 * c + 2, :], in0=ov[:, :, 0:128],
                                              in1=d4[2][:, 2 * c:2 * c + 2].unsqueeze(2).broadcast_to([128, 2, 128]), op=ALU.mult),
              [psO[c], d4[2]], [hraw])
        yield
        for h in range(4):
            A(lambda: nc.scalar.activation(out=hm[:, h * 128:(h + 1) * 128], in_=hraw[:, h, :], func=AF.Square, accum_out=ss4[:, h:h + 1]),
              [hraw], [hm, ss4])
        yield
        rstd_pow(rs4, ss4, 128)
        yield
        V(lambda: nc.vector.tensor_tensor(out=hraw[:], in0=hraw[:], in1=rs4[:, :].unsqueeze(2).broadcast_to([128, 4, 128]), op=ALU.mult),
          [hraw, rs4], [hraw])
        G(lambda: nc.gpsimd.tensor_tensor(out=hraw[:], in0=hraw[:], in1=tabA[:, TA_MONW:TA_MONW + 512].rearrange("p (h d) -> p h d", h=4),
                                          op=ALU.mult), [hraw, tabA], [hraw])
        yield
        V(lambda: nc.vector.tensor_tensor(out=hm[:], in0=hraw[:].rearrange("p h d -> p (h d)"), in1=sigo[:], op=ALU.mult), [hraw, sigo], [hm])
        yield
        ph = bfv(psSTm, 512)
        for c in range(4):
            P(lambda: nc.tensor.transpose(out=ph[:, c * 128:(c + 1) * 128], in_=hm[:, c * 128:(c + 1) * 128], identity=identb[:]),
              [hm, identb], [psSTm], sig=(c == 3))
        A(lambda: nc.scalar.copy(out=hmTs[:], in_=ph), [psSTm], [hmTs])
        D([(hmT_d[oi], hmTs[:])], r=[hmTs], w=[hmT_db[oi]])
        yield

    fs = {"next": 0, "act": [], "done": -1}
    SKEW = 7

    def ftick(limit):
        if fs["next"] < NT and fs["next"] <= limit and len(fs["act"]) < 2 and (not fs["act"] or fs["act"][-1][2] >= SKEW):
            fs["act"].append([fs["next"], a_front(fs["next"]), 0])
            fs["next"] += 1
        for a in list(fs["act"]):
            if next(a[1], "end") == "end":
                fs["act"].remove(a)
                fs["done"] = max(fs["done"], a[0])
            else:
                a[2] += 1

    prev_out = None
    for t in range(NT):
        while fs["done"] < t:
            ftick(t + 1)
            if prev_out is not None and next(prev_out, "end") == "end":
                prev_out = None
        sc_ = a_scan(t)
        while sc_ is not None or prev_out is not None:
            if sc_ is not None and next(sc_, "end") == "end":
                sc_ = None
            if prev_out is not None and next(prev_out, "end") == "end":
                prev_out = None
            ftick(t + 2)
        prev_out = a_out(t) if t >= QLO else None
    if prev_out is not None:
        for _ in prev_out:
            ftick(NT)
    while fs["act"]:
        ftick(NT)
    T.barrier()
    pa.close()
    pa2 = ExitStack()
    kcT = mk(pa2, "kcT", [128, S], BF16)
    vcT = mk(pa2, "vcT", [128, S], BF16)
    D([(kcT[:], kcs_d)], r=[kvs_db], w=[kcT])
    D([(vcT[:], vcs_d)], r=[kvs_db], w=[vcT])
    w1 = [mk(pa2, "w1k", [128, 32, 256], BF16), mk(pa2, "w1v", [128, 32, 256], BF16)]
    w2 = [mk(pa2, "w2k", [128, 2, 64], BF16), mk(pa2, "w2v", [128, 2, 64], BF16)]
    pe2f = mk(pa2, "pe2f", [32, 256])
    pe2b = mk(pa2, "pe2b", [32, 256], BF16)
    peT = mk(pa2, "peT", [128, 2, 32], BF16)
    bias4 = mk(pa2, "bias4", [128, 4])
    gel = [[mk(pa2, "gel%d%d" % (kv, g), [128, 2, NCPAD], BF16) for g in range(2)] for kv in range(2)]
    with ExitStack() as stg:
        stage = [[mk(stg, "stgB%d" % i, [128, 2048]) for i in range(2)], 0]
        for kv, wd in enumerate((w1k_d, w1v_d)):
            wv = wd.rearrange("(l d) n -> d l n", d=64)
            for lq in range(4):
                st = stage[0][stage[1] % 2]
                stage[1] += 1
                sv = st[:, :].rearrange("p (a n) -> p a n", a=8)
                D([(sv[0:64], wv[:, lq * 8:(lq + 1) * 8, :]), (sv[64:128], wv[:, lq * 8:(lq + 1) * 8, :])], w=[st])
                V(lambda: nc.vector.tensor_copy(out=w1[kv][:, lq * 8:(lq + 1) * 8, :], in_=sv), [st], [w1[kv]])
        for kv, wd in enumerate((w2k_d, w2v_d)):
            load_w(stage, w2[kv], w2[kv][:], wd.rearrange("(c p) n -> p c n", p=128), [128, 2, 64])
        T.barrier()
    D([(pe2f[:], pe2_d)], w=[pe2f])
    V(lambda: nc.vector.tensor_copy(out=pe2b[:], in_=pe2f[:]), [pe2f], [pe2b])
    for kv in range(2):
        for g in range(2):
            G(lambda: nc.gpsimd.memset(gel[kv][g][:], 0.0), [], [gel[kv][g]])
    ppe = bfv(PS[4], 128)
    for kv in range(2):
        P(lambda: nc.tensor.transpose(out=ppe[:, kv * 32:(kv + 1) * 32], in_=pe2b[:, kv * 128:(kv + 1) * 128], identity=identb[0:32, 0:32]),
          [pe2b, identb], [PS[4]])
    V(lambda: nc.vector.tensor_copy(out=peT[:].rearrange("p a l -> p (a l)"), in_=ppe[:, 0:64]), [PS[4]], [peT])
    for kv in range(2):
        for hc in range(2):
            i4 = kv * 2 + hc
            for l in range(32):
                P(lambda: nc.tensor.matmul(out=PS[5][:, i4:i4 + 1], lhsT=w1[kv][0:64, l, hc * 128:(hc + 1) * 128], rhs=peT[0:64, kv, l:l + 1],
                                           start=(l == 0), stop=(l == 31)), [w1[kv], peT], [PS[5]], sig=(l == 31))
    V(lambda: nc.vector.tensor_copy(out=bias4[:], in_=PS[5][:, 0:4]), [PS[5]], [bias4])
    bi = 0
    for kv, src in enumerate((kcT, vcT)):
        srcv = src[:, :].rearrange("p (n s) -> p n s", s=16)
        for hc in range(2):
            pss = (PS[(2 * bi) % 4], PS[(2 * bi + 1) % 4])
            bi += 1
            for l in range(32):
                for g in range(2):
                    rows = slice(g * 64, g * 64 + 64)
                    P(lambda: nc.tensor.matmul(out=pss[g][:, 0:NCMP], lhsT=w1[kv][rows, l, hc * 128:(hc + 1) * 128],
                                               rhs=srcv[rows, l // 16:l // 16 + NCMP, l % 16], start=(l == 0), stop=(l == 31)),
                      [w1[kv], src], [pss[g]], sig=(l == 31), rg=g * 64)
            for g in range(2):
                A(lambda: nc.scalar.activation(out=gel[kv][g][:, hc, 0:NCMP], in_=pss[g][:, 0:NCMP], func=AF.Gelu,
                                               bias=bias4[:, kv * 2 + hc:kv * 2 + hc + 1]), [pss[g], bias4], [gel[kv][g]])
    for g in range(2):
        rows = slice(g * 64, g * 64 + 64)
        for hc in range(2):
            P(lambda: nc.tensor.matmul(out=PS[6][rows, 0:NCMP], lhsT=w2[0][:, hc, :], rhs=gel[0][g][:, hc, 0:NCMP],
                                       start=(hc == 0), stop=(hc == 1)), [w2[0], gel[0][g]], [PS[6]], sig=(hc == 1))
        V(lambda: nc.vector.tensor_copy(out=kcmpT[rows, 0:NCMP], in_=PS[6][rows, 0:NCMP]), [PS[6]], [kcmpT])
        for cc in range(NCC):
            for hc in range(2):
                P(lambda: nc.tensor.matmul(out=PS[7][:, (cc * 2 + g) * 64:(cc * 2 + g + 1) * 64], lhsT=gel[1][g][:, hc, cc * 128:(cc + 1) * 128],
                                           rhs=w2[1][:, hc, :], start=(hc == 0), stop=(hc == 1)), [w2[1], gel[1][g]], [PS[7]], sig=(hc == 1))
    A(lambda: nc.scalar.copy(out=vcmp[:].rearrange("p c g d -> p (c g d)"), in_=PS[7][:, 0:NCC * 128]), [PS[7]], [vcmp])
    T.barrier()
    if debug == 1:
        for nm, tb in (("d_kwT", kwT), ("d_kcT", kcT), ("d_vcT", vcT)):
            o, ob = dout(nm, [128, S], BF16)
            D([(o, tb[:])], r=[tb], w=[ob])
        o, ob = dout("d_vsA", [128, NT * 130], BF16)
        D([(o, vsA[:].rearrange("p t g d -> p (t g d)"))], r=[vsA], w=[ob])
        o, ob = dout("d_hmT", [NOWN, 128, 512], BF16)
        D([(o[i], hmT_d[i]) for i in range(NOWN)], r=hmT_db, w=[ob])
        o, ob = dout("d_kcmpT", [128, NCPAD], BF16)
        D([(o, kcmpT[:])], r=[kcmpT], w=[ob])
        o, ob = dout("d_vcmp", [128, NCC * 128], BF16)
        D([(o, vcmp[:].rearrange("p c g d -> p (c g d)"))], r=[vcmp], w=[ob])
        T.barrier()
        return nc, T, dbg
    pa2.close()
    pa0.close()

    pb = ExitStack()
    wB = mk(pb, "wB", [128, 8, 536], BF16)
    ovl = mk(pb, "ovl", [128, NCC, NSEL], BF16)
    cbm = mk(pb, "cbm", [128, 1024], BF16)
    with ExitStack() as stg:
        stage = [[mk(stg, "stgC%d" % i, [128, 2048]) for i in range(2)], 0]
        w_in_v = w_in_d.rearrange("(c p) n -> p c n", p=128)
        for (do, so, sn) in ((0, IN_OFF["nq"][0], 512), (512, IN_OFF["ng"][0], 24)):
            for o in range(0, sn, 256):
                n = min(256, sn - o)
                load_w(stage, wB, wB[:, :, do + o:do + o + n], w_in_v[:, :, so + o:so + o + n], [128, 8, n],
                       scale_ap=n1w[:, :].unsqueeze(2).broadcast_to([128, 8, n]), scale_tb=n1w)
        for o in range(0, S, 2048):
            n = min(2048, S - o)
            st = stage[0][stage[1] % 2]
            stage[1] += 1
            D([(st[64:128, 0:n], ewin_d[:, o:o + n])], w=[st])
            V(lambda: nc.vector.tensor_copy(out=KE[0][64:128, o:o + n], in_=st[64:128, 0:n]), [st], [KE[0]])
            G(lambda: nc.gpsimd.tensor_copy(out=KE[1][64:128, o:o + n], in_=st[64:128, 0:n]), [st], [KE[1]])
        load_w(stage, ovl, ovl[:].rearrange("p c j -> p (c j)"), ovl_d, [128, NCC * NSEL])
        load_w(stage, cbm, cbm[:], cbm_d, [128, 1024])
        T.barrier()
    xtb = [mk(pb, "xtb%d" % i, [128, 1024]) for i in range(2)]
    sqjb = mk(pb, "sqjb", [128, 1024], BF16)
    ssb = mk(pb, "ssb", [128, 1])
    rsb = mk(pb, "rsb", [128, 1])
    xnb = mk(pb, "xnb", [128, 1024], BF16)
    xnTb = mk(pb, "xnTb", [128, 8, 128], BF16)
    csb = [mk(pb, "csb%d" % i, [128, 64]) for i in range(2)]
    qsq = mk(pb, "qsq", [128, 512])
    ss8 = mk(pb, "ss8", [128, 8])
    rs8 = mk(pb, "rs8", [128, 8])
    qn = mk(pb, "qn", [128, 8, 64])
    qrt = [mk(pb, "qrt%d" % i, [128, 8, 32]) for i in range(4)]
    qr = mk(pb, "qr", [128, 8, 64], BF16)
    qT2 = [mk(pb, "qT%d" % i, [128, 4, 128], BF16) for i in range(2)]
    sg2 = [mk(pb, "sg%d" % i, [128, 24]) for i in range(2)]
    cmk = [mk(pb, "cmk%d" % i, [128, NCPAD], BF16) for i in range(2)]
    itab = [mk(pb, "itab%d" % i, [128, NSEL]) for i in range(2)]
    sc2 = [mk(pb, "sc%d" % i, [128, NCPAD]) for i in range(2)]
    mx2 = [mk(pb, "mx%d" % i, [128, 1]) for i in range(2)]
    se2 = [mk(pb, "se%d" % i, [128, 1]) for i in range(2)]
    pc2_ = [mk(pb, "pc%d" % i, [128, NCPAD], BF16) for i in range(2)]
    pcT2 = [mk(pb, "pcT%d" % i, [128, NCC, 128], BF16) for i in range(2)]
    ocmp2 = [mk(pb, "ocmp%d" % i, [128, 8, 64]) for i in range(2)]
    oslc = mk(pb, "oslc", [128, 8, 64])
    owin = mk(pb, "owin", [128, 8, 64])
    imp2 = mk(pb, "imp2", [128, NSEL])
    imp3 = mk(pb, "imp3", [128, NSEL])
    m8 = mk(pb, "m8", [128, 16])
    selb = mk(pb, "selb", [128, NSEL])
    negp = mk(pb, "negp", [128, NSEL])
    bqb = mk(pb, "bqb", [128, 2, 128], BF16)
    V(lambda: nc.vector.memset(bqb[:], 0.0), [], [bqb])
    NW = 2 if NSEL > 64 else 1
    QB2 = [[[mk(pb, "QB%d%d%d" % (i, g, w), [128, 4, 128], BF16) for w in range(NW)] for g in range(2)] for i in range(2)]
    pTs = [mk(pb, "pTs%d" % i, [128, 512], BF16) for i in range(4)]
    den = mk(pb, "den", [128, 4])
    osb = [mk(pb, "osb%d" % i, [128, 260]) for i in range(2)]
    onb = mk(pb, "onb", [128, 512], BF16)
    gt = [mk(pb, "gt%d" % i, [128, 8, 64]) for i in range(2)]
    onTs = mk(pb, "onTs", [128, 512], BF16)
    psT, psQ, psG, psSC, psS0, psS1, psOS, psOW = PS
    psST = (psS0, psS1, psOW)
    cnt_ = {"st": 0, "pt": 0, "os": 0}

    def b_front(oi):
        t = QLO + oi
        p_ = oi % 2
        xt, cs, qT, sg, ocmp, QB = xtb[p_], csb[p_], qT2[p_], sg2[p_], ocmp2[p_], QB2[p_]
        D([(xt[:], x_d[t * 128:(t + 1) * 128, :])], w=[xt])
        D([(cs[:, 0:32], cos_d[t * 128:(t + 1) * 128, :]), (cs[:, 32:64], sin_d[t * 128:(t + 1) * 128, :])], w=[cs])
        D([(cmk[p_][:], cmask_d[oi])], w=[cmk[p_]])
        D([(itab[p_][:], imptab_d[oi])], w=[itab[p_]])
        A(lambda: nc.scalar.activation(out=sqjb[:], in_=xt[:], func=AF.Square, accum_out=ssb[:]), [xt], [sqjb, ssb])
        yield
        rstd_pow(rsb, ssb, 1024)
        yield
        A(lambda: nc.scalar.activation(out=xnb[:], in_=xt[:], func=AF.Copy, scale=rsb[:]), [xt, rsb], [xnb])
        yield
        pv = bfv(psT, 1024)
        for c in range(8):
            P(lambda: nc.tensor.transpose(out=pv[:, c * 128:(c + 1) * 128], in_=xnb[:, c * 128:(c + 1) * 128], identity=identb[:]),
              [xnb, identb], [psT], sig=(c == 7))
        yield
        V(lambda: nc.vector.tensor_copy(out=xnTb[:].rearrange("p c t -> p (c t)"), in_=pv), [psT], [xnTb])
        yield
        for c in range(8):
            P(lambda: nc.tensor.matmul(out=psQ[:, 0:512], lhsT=xnTb[:, c, :], rhs=wB[:, c, 0:512], start=(c == 0), stop=(c == 7)),
              [xnTb, wB], [psQ], sig=(c == 7))
        for c in range(8):
            P(lambda: nc.tensor.matmul(out=psG[:, 0:24], lhsT=xnTb[:, c, :], rhs=wB[:, c, 512:536], start=(c == 0), stop=(c == 7)),
              [xnTb, wB], [psG], sig=(c == 7))
        yield
        A(lambda: nc.scalar.activation(out=sg[:], in_=psG[:, 0:24], func=AF.Exp, scale=-1.0), [psG], [sg])
        A(lambda: nc.scalar.activation(out=qsq[:], in_=psQ[:, 0:512], func=AF.Square), [psQ], [qsq])
        yield
        V(lambda: nc.vector.tensor_scalar(out=sg[:], in0=sg[:], scalar1=1.0, scalar2=None, op0=ALU.add), [sg], [sg])
        V(lambda: nc.vector.reciprocal(out=sg[:], in_=sg[:]), [sg], [sg])
        V(lambda: nc.vector.tensor_reduce(out=ss8[:], in_=qsq[:].rearrange("p (a d) -> p a d", a=8), axis=AX.X, op=ALU.add), [qsq], [ss8])
        yield
        rstd_pow(rs8, ss8, 64)
        yield
        V(lambda: nc.vector.tensor_tensor(out=qn[:], in0=psQ[:, 0:512].rearrange("p (a d) -> p a d", a=8),
                                          in1=rs8[:, :].unsqueeze(2).broadcast_to([128, 8, 64]), op=ALU.mult), [psQ, rs8], [qn])
        V(lambda: nc.vector.tensor_tensor(out=qn[:], in0=qn[:], in1=tabA[:, TA_QNW:TA_QNW + 512].rearrange("p (a d) -> p a d", a=8),
                                          op=ALU.mult), [qn, tabA], [qn])
        yield
        cosb = cs[:, 0:32].unsqueeze(1).broadcast_to([128, 8, 32])
        sinb = cs[:, 32:64].unsqueeze(1).broadcast_to([128, 8, 32])
        V(lambda: nc.vector.tensor_tensor(out=qrt[0][:], in0=qn[:, :, 0:32], in1=cosb, op=ALU.mult), [qn, cs], [qrt[0]])
        G(lambda: nc.gpsimd.tensor_tensor(out=qrt[1][:], in0=qn[:, :, 32:64], in1=sinb, op=ALU.mult), [qn, cs], [qrt[1]])
        V(lambda: nc.vector.tensor_tensor(out=qrt[2][:], in0=qn[:, :, 32:64], in1=cosb, op=ALU.mult), [qn, cs], [qrt[2]])
        G(lambda: nc.gpsimd.tensor_tensor(out=qrt[3][:], in0=qn[:, :, 0:32], in1=sinb, op=ALU.mult), [qn, cs], [qrt[3]])
        yield
        qro = qr[:, :, :].rearrange("p (h g) d -> p g h d", g=2)
        v4 = lambda tb_: tb_[:, :, :].rearrange("p (g h) d -> p g h d", g=2)
        V(lambda: nc.vector.tensor_tensor(out=qro[:, :, :, 0:32], in0=v4(qrt[0]), in1=v4(qrt[1]), op=ALU.subtract), [qrt[0], qrt[1]], [qr])
        G(lambda: nc.gpsimd.tensor_tensor(out=qro[:, :, :, 32:64], in0=v4(qrt[2]), in1=v4(qrt[3]), op=ALU.add), [qrt[2], qrt[3]], [qr])
        yield
        pq = bfv(psT, 512)
        for hg in range(4):
            P(lambda: nc.tensor.transpose(out=pq[:, hg * 128:(hg + 1) * 128], in_=qr[:, 2 * hg:2 * hg + 2, :].rearrange("p a d -> p (a d)"),
                                          identity=identb[:]), [qr, identb], [psT], sig=(hg == 3))
        pq1 = bfv(psSC, 512)
        for hg in range(4):
            P(lambda: nc.tensor.transpose(out=pq1[0:64, hg * 128:(hg + 1) * 128], in_=qr[:, 2 * hg + 1, :], identity=identb[:]),
              [qr, identb], [psSC], sig=(hg == 3))
        yield
        V(lambda: nc.vector.tensor_copy(out=qT[:].rearrange("p h t -> p (h t)"), in_=pq), [psT], [qT])
        for w in range(NW):
            V(lambda: nc.vector.tensor_copy(out=QB[1][w][0:64].rearrange("p h t -> p (h t)"), in_=pq1[0:64, :]), [psSC], [QB[1][w]])
        yield
        for w in range(NW):
            G(lambda: nc.gpsimd.tensor_copy(out=QB[0][w][0:64], in_=qT[0:64]), [qT], [QB[0][w]])
        psOC = psQ
        cm = cmk[p_]

        def cmp_head(head, par):
            g, hg = head // 4, head % 4
            rows = slice(g * 64, g * 64 + 64)
            psc = (psSC, psT)[par]
            sc, mx, se, pc, pcT = sc2[par], mx2[par], se2[par], pc2_[par], pcT2[par]
            P(lambda: nc.tensor.matmul(out=psc[:, 0:NCPAD], lhsT=qT[rows, hg, :], rhs=kcmpT[rows, 0:NCPAD], start=True, stop=False),
              [qT, kcmpT], [psc], sig=False, rg=g * 64)
            P(lambda: nc.tensor.matmul(out=psc[:, 0:NCPAD], lhsT=identb[:], rhs=cm[:], start=False, stop=True), [identb, cm], [psc])
            yield
            V(lambda: nc.vector.reduce_max(out=mx[:], in_=psc[:, 0:NCPAD], axis=AX.X), [psc], [mx])
            V(lambda: nc.vector.tensor_scalar(out=mx[:], in0=mx[:], scalar1=-0.125, scalar2=None, op0=ALU.mult), [mx], [mx])
            yield
            A(lambda: nc.scalar.activation(out=sc[:], in_=psc[:, 0:NCPAD], func=AF.Exp, scale=0.125, bias=mx[:], accum_out=se[:]), [psc, mx], [sc, se])
            yield
            V(lambda: nc.vector.reciprocal(out=se[:], in_=se[:]), [se], [se])
            V(lambda: nc.vector.tensor_scalar(out=pc[:], in0=sc[:], scalar1=se[:], scalar2=rvt[:, oi:oi + 1], op0=ALU.mult, op1=ALU.mult),
              [sc, se, rvt], [pc])
            yield
            pp = bfv(psc, NCC * 128)
            for cc in range(NCC):
                P(lambda: nc.tensor.transpose(out=pp[:, cc * 128:(cc + 1) * 128], in_=pc[:, cc * 128:(cc + 1) * 128], identity=identb[:]),
                  [pc, identb], [psc], sig=(cc == NCC - 1))
            yield
            V(lambda: nc.vector.tensor_copy(out=pcT[:].rearrange("p c t -> p (c t)"), in_=pp), [psc], [pcT])
            yield
            for cc in range(NCC):
                P(lambda: nc.tensor.matmul(out=psOC[:, head * 64:(head + 1) * 64], lhsT=pcT[:, cc, :], rhs=vcmp[:, cc, g, :],
                                           start=(cc == 0), stop=(cc == NCC - 1)), [pcT, vcmp], [psOC], sig=(cc == NCC - 1))
            for cc in range(NCC):
                first = (hg == 0 and cc == 0)
                last = (hg == 3 and cc == NCC - 1)
                P(lambda: nc.tensor.matmul(out=psG[:, 256 + g * NSEL:256 + (g + 1) * NSEL], lhsT=pcT[:, cc, :], rhs=ovl[:, cc, :],
                                           start=first, stop=last, skip_group_check=True), [pcT, ovl], [psG], sig=(cc == NCC - 1))
            yield

        for pair in range(4):
            ga = cmp_head(2 * pair, 0)
            gb = cmp_head(2 * pair + 1, 1)
            for _ in ga:
                next(gb, None)
                yield
        V(lambda: nc.vector.tensor_copy(out=ocmp[:].rearrange("p h d -> p (h d)"), in_=psOC[:, 0:512]), [psOC], [ocmp])
        yield
        for g in range(2):
            V(lambda: nc.vector.tensor_tensor(out=imp2[:], in0=psG[:, 256 + g * NSEL:256 + (g + 1) * NSEL], in1=itab[p_][:], op=ALU.add),
              [psG, itab[p_]], [imp2])
            V(lambda: nc.vector.max(out=m8[:, 0:8], in_=imp2[:]), [imp2], [m8])
            yield
            V(lambda: nc.vector.match_replace(out=imp3[:], in_to_replace=m8[:, 0:8], in_values=imp2[:], imm_value=-3.0e38), [imp2, m8], [imp3])
            V(lambda: nc.vector.max(out=m8[:, 8:16], in_=imp3[:]), [imp3], [m8])
            yield
            V(lambda: nc.vector.tensor_scalar(out=selb[:], in0=imp2[:], scalar1=m8[:, 15:16], scalar2=None, op0=ALU.is_ge), [imp2, m8], [selb])
            V(lambda: nc.vector.tensor_scalar(out=selb[:], in0=selb[:], scalar1=-1.0, scalar2=-NEGB, op0=ALU.add, op1=ALU.mult), [selb], [selb])
            yield
            V(lambda: nc.vector.tensor_scalar(out=negp[:], in0=itab[p_][:], scalar1=0.0, scalar2=NEGB, op0=ALU.min, op1=ALU.max),
              [itab[p_]], [negp])
            V(lambda: nc.vector.tensor_tensor(out=bqb[:, 0, 0:NSEL], in0=selb[:], in1=negp[:], op=ALU.add), [selb, negp], [bqb])
            yield
            nlo = min(64, NSEL)
            V(lambda: nc.vector.tensor_copy(out=bqb[:, 1, 64:64 + nlo], in_=bqb[:, 0, 0:nlo]), [bqb], [bqb])
            if NSEL > 64:
                V(lambda: nc.vector.tensor_copy(out=bqb[:, 1, 0:NSEL - 64], in_=bqb[:, 0, 64:NSEL]), [bqb], [bqb])
            yield
            pb_ = bfv(psT, 256)
            for w in range(NW):
                P(lambda: nc.tensor.transpose(out=pb_[:, w * 128:(w + 1) * 128], in_=bqb[:, 1 - w, :], identity=identb[:]), [bqb, identb], [psT],
                  sig=(w == NW - 1))
            yield
            for w in range(NW):
                V(lambda: nc.vector.tensor_copy(out=QB[g][w][64:128], in_=pb_[64:128, w * 128:(w + 1) * 128].unsqueeze(1).broadcast_to([64, 4, 128])),
                  [psT], [QB[g][w]])
            yield

    def b_loops(oi):
        t = QLO + oi
        p_ = oi % 2
        qT, sg, ocmp, QB = qT2[p_], sg2[p_], ocmp2[p_], QB2[p_]
        qTf = qT[:, :, :].rearrange("p h t -> p (h t)")
        k0 = max(0, t - 4)
        its = []
        for g in range(2):
            its += [("s", g, kt) for kt in range(t + 1)]
            its += [("w", g, kt) for kt in range(k0, t + 1)]
        slots = {}

        def front(i):
            kind, g, kt = its[i]
            rows = slice(g * 64, g * 64 + 64)
            ps = psST[cnt_["st"] % 3]
            cnt_["st"] += 1
            pT = pTs[cnt_["pt"] % 4]
            cnt_["pt"] += 1
            slots[i] = (ps, pT)
            ksl = slice(kt * 128, (kt + 1) * 128)
            if kind == "s":
                qb = QB[g][kt // 32]
                P(lambda: nc.tensor.matmul(out=ps[:, :], lhsT=KE[g][:, ksl], rhs=qb[:, :, :].rearrange("p h t -> p (h t)"), start=True, stop=(kt != t)),
                  [KE[g], qb], [ps], sig=(kt != t))
                if kt == t:
                    P(lambda: nc.tensor.matmul(out=ps[:, :], lhsT=identb[:], rhs=cbm[:, 0:512], start=False, stop=True), [identb, cbm], [ps])
            else:
                edge = (kt == t) or (kt == t - 4)
                P(lambda: nc.tensor.matmul(out=ps[:, :], lhsT=kwT[rows, ksl], rhs=qTf[rows, :], start=True, stop=(not edge)), [kwT, qT], [ps],
                  sig=(not edge), rg=g * 64)
                if kt == t:
                    P(lambda: nc.tensor.matmul(out=ps[:, :], lhsT=identb[:], rhs=cbm[:, 0:512], start=False, stop=True), [identb, cbm], [ps])
                elif kt == t - 4:
                    P(lambda: nc.tensor.matmul(out=ps[:, :], lhsT=identb[:], rhs=cbm[:, 512:1024], start=False, stop=True), [identb, cbm], [ps])

        def back(i):
            kind, g, kt = its[i]
            ps, pT = slots.pop(i)
            if kind == "s":
                A(lambda: nc.scalar.activation(out=pT[:], in_=ps[:, :], func=AF.Exp, scale=0.125), [ps], [pT])
                pso, vst, first, last = psOS, vsA, (kt == 0), (kt == t)
            else:
                A(lambda: nc.scalar.activation(out=pT[:], in_=ps[:, :], func=AF.Exp, scale=0.125, bias=kbias[:, kt:kt + 1]), [ps, kbias], [pT])
                pso, vst, first, last = psOS, vwA, (kt == k0), (kt == t)
            for hg in range(4):
                P(lambda: nc.tensor.matmul(out=pso[:, hg * 65:(hg + 1) * 65], lhsT=pT[:, hg * 128:(hg + 1) * 128], rhs=vst[:, kt, g, :],
                                           start=(first and hg == 0), stop=last, skip_group_check=True), [pT, vst], [pso], sig=(hg == 3 and last))
            if last:
                ob = osb[cnt_["os"] % 2]
                cnt_["os"] += 1
                A(lambda: nc.scalar.copy(out=ob[:], in_=pso[:, 0:260]), [pso], [ob])
                ov_ = ob[:, :].rearrange("p (h e) -> p h e", h=4)
                dst = oslc if kind == "s" else owin
                V(lambda: nc.vector.tensor_scalar(out=den[:], in0=ov_[:, :, 64], scalar1=1e-30, scalar2=None, op0=ALU.max), [ob], [den])
                V(lambda: nc.vector.reciprocal(out=den[:], in_=den[:]), [den], [den])
                V(lambda: nc.vector.tensor_tensor(out=dst[:, g * 4:(g + 1) * 4, :], in0=ov_[:, :, 0:64],
                                                  in1=den[:, :].unsqueeze(2).broadcast_to([128, 4, 64]), op=ALU.mult), [ob, den], [dst])

        DEPTH = 2
        for i in range(min(DEPTH, len(its))):
            front(i)
        for i in range(len(its)):
            if i + DEPTH < len(its):
                front(i + DEPTH)
            back(i)
            yield
        sgv = sg[:, :].rearrange("p (h k) -> p h k", k=3)
        V(lambda: nc.vector.tensor_tensor(out=gt[0][:], in0=ocmp[:], in1=sgv[:, :, 0:1].broadcast_to([128, 8, 64]), op=ALU.mult), [ocmp, sg], [gt[0]])
        G(lambda: nc.gpsimd.tensor_tensor(out=gt[1][:], in0=oslc[:], in1=sgv[:, :, 1:2].broadcast_to([128, 8, 64]), op=ALU.mult), [oslc, sg], [gt[1]])
        V(lambda: nc.vector.tensor_tensor(out=gt[0][:], in0=gt[0][:], in1=gt[1][:], op=ALU.add), [gt[0], gt[1]], [gt[0]])
        G(lambda: nc.gpsimd.tensor_tensor(out=gt[1][:], in0=owin[:], in1=sgv[:, :, 2:3].broadcast_to([128, 8, 64]), op=ALU.mult), [owin, sg], [gt[1]])
        V(lambda: nc.vector.tensor_tensor(out=onb[:], in0=gt[0][:].rearrange("p h d -> p (h d)"), in1=gt[1][:].rearrange("p h d -> p (h d)"),
                                          op=ALU.add), [gt[0], gt[1]], [onb])
        yield
        po = bfv(psS0, 512)
        for c in range(4):
            P(lambda: nc.tensor.transpose(out=po[:, c * 128:(c + 1) * 128], in_=onb[:, c * 128:(c + 1) * 128], identity=identb[:]),
              [onb, identb], [psS0], sig=(c == 3))
        yield
        V(lambda: nc.vector.tensor_copy(out=onTs[:], in_=po), [psS0], [onTs])
        D([(onT_d[oi], onTs[:])], r=[onTs], w=[onT_db[oi]])
        yield

    for _ in b_front(0):
        pass
    for oi in range(NOWN):
        lp = b_loops(oi)
        nf = b_front(oi + 1) if oi + 1 < NOWN else None
        L_ = 2 * (QLO + oi) + 12 + 3
        F_ = 52
        for i_, _ in enumerate(lp):
            nadv = ((i_ + 1) * F_) // L_ - (i_ * F_) // L_
            for _k in range(nadv):
                if nf is not None:
                    try:
                        next(nf)
                    except StopIteration:
                        nf = None
        if nf is not None:
            for _ in nf:
                pass
    T.barrier()
    if debug == 2:
        o, ob = dout("d_onT", [NOWN, 128, 512], BF16)
        D([(o[i], onT_d[i]) for i in range(NOWN)], r=onT_db, w=[ob])
        T.barrier()
        return nc, T, dbg
    pb.close()
    kvs.close()

    pc1 = ExitStack()
    wC = mk(pc1, "wC", [128, 8, 2048], BF16)
    wum = mk(pc1, "wum", [128, 4, 1024], BF16)
    wun = mk(pc1, "wun", [128, 4, 1024], BF16)
    wo = mk(pc1, "wo", [128, 8, 1024], BF16)
    mgbf = mk(pc1, "mgbf", [1, 2048])
    mgbb = mk(pc1, "mgbb", [1, 2048], BF16)
    D([(mgbf[:], mgb_d)], w=[mgbf])
    V(lambda: nc.vector.tensor_copy(out=mgbb[:], in_=mgbf[:]), [mgbf], [mgbb])
    with ExitStack() as stg:
        stage = [[mk(stg, "stgD%d" % i, [128, 2048]) for i in range(2)], 0]
        w_in_v = w_in_d.rearrange("(c p) n -> p c n", p=128)
        so = IN_OFF["gm"][0]
        for o in range(0, 2048, 256):
            load_w(stage, wC, wC[:, :, o:o + 256], w_in_v[:, :, so + o:so + o + 256], [128, 8, 256],
                   scale_ap=n1w[:, :].unsqueeze(2).broadcast_to([128, 8, 256]), scale_tb=n1w)
        for (wt, wd, kc_) in ((wum, wupm_d, 4), (wun, wupn_d, 4), (wo, wout_d, 8)):
            wv = wd.rearrange("(c p) n -> p c n", p=128)
            for o in range(0, 1024, 2048 // kc_):
                n = 2048 // kc_
                load_w(stage, wt, wt[:, :, o:o + n], wv[:, :, o:o + n], [128, kc_, n])
        T.barrier()
    xtc = [mk(pc1, "xtc%d" % i, [128, 1024]) for i in range(2)]
    sqjc = mk(pc1, "sqjc", [128, 1024], BF16)
    ssc = [mk(pc1, "ssc%d" % i, [128, 1]) for i in range(2)]
    rsc = [mk(pc1, "rsc%d" % i, [128, 1]) for i in range(2)]
    xnc = [mk(pc1, "xnc%d" % i, [128, 1024], BF16) for i in range(2)]
    xnTc = [mk(pc1, "xnTc%d" % i, [128, 8, 128], BF16) for i in range(2)]
    hmTl = [mk(pc1, "hmTl%d" % i, [128, 4, 128], BF16) for i in range(2)]
    onTl = [mk(pc1, "onTl%d" % i, [128, 4, 128], BF16) for i in range(2)]
    sgm = [mk(pc1, "sgm%d" % i, [128, 2048]) for i in range(2)]
    y1 = [mk(pc1, "y1_%d" % i, [128, 1024]) for i in range(2)]
    y2 = [mk(pc1, "y2_%d" % i, [128, 1024]) for i in range(2)]
    yb = [mk(pc1, "yb%d" % i, [128, 1024], BF16) for i in range(2)]
    yT = [mk(pc1, "yT%d" % i, [128, 8, 128], BF16) for i in range(2)]
    xmo = [mk(pc1, "xmo%d" % i, [128, 1024]) for i in range(2)]
    rot = {"g": 0, "u": 0, "o": 0}

    def c1_tile(oi):
        t = QLO + oi
        p_ = oi % 2
        xt = xtc[p_]
        psTp = (PS[0], PS[7])[p_]
        D([(xt[:], x_d[t * 128:(t + 1) * 128, :])], w=[xt])
        D([(hmTl[p_][:].rearrange("p c t -> p (c t)"), hmT_d[oi])], r=[hmT_db[oi]], w=[hmTl[p_]])
        D([(onTl[p_][:].rearrange("p c t -> p (c t)"), onT_d[oi])], r=[onT_db[oi]], w=[onTl[p_]])
        A(lambda: nc.scalar.activation(out=sqjc[:], in_=xt[:], func=AF.Square, accum_out=ssc[p_][:]), [xt], [sqjc, ssc[p_]])
        rstd_pow(rsc[p_], ssc[p_], 1024)
        yield
        A(lambda: nc.scalar.activation(out=xnc[p_][:], in_=xt[:], func=AF.Copy, scale=rsc[p_][:]), [xt, rsc[p_]], [xnc[p_]])
        yield
        pv = bfv(psTp, 1024)
        for c in range(8):
            P(lambda: nc.tensor.transpose(out=pv[:, c * 128:(c + 1) * 128], in_=xnc[p_][:, c * 128:(c + 1) * 128], identity=identb[:]),
              [xnc[p_], identb], [psTp], sig=(c == 7))
        yield
        V(lambda: nc.vector.tensor_copy(out=xnTc[p_][:].rearrange("p c t -> p (c t)"), in_=pv), [psTp], [xnTc[p_]])
        yield
        for nb in range(4):
            ps = PS[1 + rot["g"] % 2]
            rot["g"] += 1
            for c in range(8):
                P(lambda: nc.tensor.matmul(out=ps[:, :], lhsT=xnTc[p_][:, c, :], rhs=wC[:, c, nb * 512:(nb + 1) * 512], start=(c == 0), stop=False),
                  [xnTc[p_], wC], [ps], sig=False)
            P(lambda: nc.tensor.matmul(out=ps[:, :], lhsT=onesb[0:1, :], rhs=mgbb[0:1, nb * 512:(nb + 1) * 512], start=False, stop=True),
              [onesb, mgbb], [ps])
            A(lambda: nc.scalar.activation(out=sgm[p_][:, nb * 512:(nb + 1) * 512], in_=ps[:, :], func=AF.Sigmoid), [ps], [sgm[p_]])
            if nb % 2 == 1:
                yield
        for br, (src, wt) in enumerate(((hmTl[p_], wum), (onTl[p_], wun))):
            for nb in range(2):
                ps = PS[3 + rot["u"] % 2]
                rot["u"] += 1
                for c in range(4):
                    P(lambda: nc.tensor.matmul(out=ps[:, :], lhsT=src[:, c, :], rhs=wt[:, c, nb * 512:(nb + 1) * 512], start=(c == 0), stop=(c == 3)),
                      [src, wt], [ps], sig=(c == 3))
                yy = y1[p_] if br == 0 else y2[p_]
                V(lambda: nc.vector.tensor_tensor(out=yy[:, nb * 512:(nb + 1) * 512], in0=ps[:, :],
                                                  in1=sgm[p_][:, br * 1024 + nb * 512:br * 1024 + (nb + 1) * 512], op=ALU.mult), [ps, sgm[p_]], [yy])
            yield
        G(lambda: nc.gpsimd.tensor_tensor(out=yb[p_][:], in0=y1[p_][:], in1=y2[p_][:], op=ALU.add), [y1[p_], y2[p_]], [yb[p_]])
        yield
        py = bfv(psTp, 1024)
        for c in range(8):
            P(lambda: nc.tensor.transpose(out=py[:, c * 128:(c + 1) * 128], in_=yb[p_][:, c * 128:(c + 1) * 128], identity=identb[:]),
              [yb[p_], identb], [psTp], sig=(c == 7))
        yield
        A(lambda: nc.scalar.copy(out=yT[p_][:].rearrange("p c t -> p (c t)"), in_=py), [psTp], [yT[p_]])
        yield
        xm = xmo[p_]
        for nb in range(2):
            ps = PS[5 + rot["o"] % 2]
            rot["o"] += 1
            for c in range(8):
                P(lambda: nc.tensor.matmul(out=ps[:, :], lhsT=yT[p_][:, c, :], rhs=wo[:, c, nb * 512:(nb + 1) * 512], start=(c == 0), stop=(c == 7)),
                  [yT[p_], wo], [ps], sig=(c == 7))
            V(lambda: nc.vector.tensor_tensor(out=xm[:, nb * 512:(nb + 1) * 512], in0=ps[:, :], in1=xt[:, nb * 512:(nb + 1) * 512], op=ALU.add),
              [ps, xt], [xm])
        D([(out_d[oi], xm[:])], r=[xm], w=[xmid_db[oi]])
        yield

    pipeline(range(NOWN), c1_tile, 6)
    T.barrier()
    if debug == 3:
        T.barrier()
        return nc, T, dbg
    pc1.close()

    pc2 = ExitStack()
    fup = mk(pc2, "fup", [128, 8, 2 * D_FF], BF16)
    fdn = mk(pc2, "fdn", [128, 22, 1024], BF16)
    fcw = mk(pc2, "fcw", [128, 22, 3])
    fcb = mk(pc2, "fcb", [128, 22])
    D([(fcw[:].rearrange("p a b -> p (a b)"), fcw_d)], w=[fcw])
    D([(fcb[:], fcb_d)], w=[fcb])
    with ExitStack() as stg:
        stage = [[mk(stg, "stgE%d" % i, [128, 2048]) for i in range(2)], 0]
        fup_v = fup_d.rearrange("(c p) n -> p c n", p=128)
        for o in range(0, 2 * D_FF, 256):
            load_w(stage, fup, fup[:, :, o:o + 256], fup_v[:, :, o:o + 256], [128, 8, 256],
                   scale_ap=n2w[:, :].unsqueeze(2).broadcast_to([128, 8, 256]), scale_tb=n2w)
        fdn_v = fdn_d.rearrange("(c p) n -> p c n", p=128)
        for c0 in range(0, 22, 2):
            load_w(stage, fdn, fdn[:, c0:c0 + 2, :], fdn_v[:, c0:c0 + 2, :], [128, 2, 1024], eng="pool")
        T.barrier()
    xmh = mk(pc2, "xmh", [128, 1024])
    xmd = [mk(pc2, "xmd%d" % i, [128, 1024]) for i in range(2)]
    sse = mk(pc2, "sse", [128, 1])
    rse_ = mk(pc2, "rse", [128, 1])
    xne = mk(pc2, "xne", [128, 1024], BF16)
    TS = 4
    h2T2 = [mk(pc2, "h2T%d" % i, [128, 8, TS * 128], BF16) for i in range(2)]
    cstate = mk(pc2, "cstate", [128, 22, 2])
    V(lambda: nc.vector.memset(cstate[:], 0.0), [], [cstate])
    ab = [mk(pc2, "ab%d" % i, [128, 2 + TS * 128]) for i in range(2)]
    acc = [mk(pc2, "acc%d" % i, [128, TS * 128]) for i in range(2)]
    gl = [mk(pc2, "gl%d" % i, [128, TS * 128]) for i in range(2)]
    uT = mk(pc2, "uT", [128, 22, TS * 128], BF16)
    groups = []
    o0 = 0
    if RESET_TILE is not None:
        groups.append([0])
        o0 = 1
    for a_ in range(o0, NOWN, TS):
        groups.append(list(range(a_, min(NOWN, a_ + TS))))
    cnt2 = {"x": 0, "c": 0}

    def c2_head(gi):
        grp = groups[gi]
        h2T = h2T2[gi % 2]
        for j, oi in enumerate(grp):
            D([(xmh[:], out_d[oi])], r=[xmid_db[oi]], w=[xmh])
            A(lambda: nc.scalar.activation(out=xne[:], in_=xmh[:], func=AF.Square, accum_out=sse[:]), [xmh], [xne, sse])
            yield
            rstd_pow(rse_, sse, 1024)
            yield
            A(lambda: nc.scalar.activation(out=xne[:], in_=xmh[:], func=AF.Copy, scale=rse_[:]), [xmh, rse_], [xne])
            yield
            pv = bfv(psT, 1024)
            for c in range(8):
                P(lambda: nc.tensor.transpose(out=pv[:, c * 128:(c + 1) * 128], in_=xne[:, c * 128:(c + 1) * 128], identity=identb[:]),
                  [xne, identb], [psT], sig=(c == 7))
            A(lambda: nc.scalar.copy(out=h2T[:, :, j * 128:(j + 1) * 128], in_=pv.rearrange("p (c t) -> p c t", c=8)), [psT], [h2T])
            yield

    def c2_body(gi):
        grp = groups[gi]
        h2T = h2T2[gi % 2]
        nt = len(grp)
        W = nt * 128
        for cch in range(22):
            ci = cnt2["c"]
            cnt2["c"] += 1
            psa = PS[1 + ci % 2]
            psv = PS[3 + ci % 2]
            ab_ = ab[ci % 2]
            acc_ = acc[ci % 2]
            gl_ = gl[ci % 2]
            for c in range(8):
                P(lambda: nc.tensor.matmul(out=psa[:, 0:W], lhsT=fup[:, c, cch * 128:(cch + 1) * 128], rhs=h2T[:, c, 0:W],
                                           start=(c == 0), stop=(c == 7)), [fup, h2T], [psa], sig=(c == 7))
            for c in range(8):
                P(lambda: nc.tensor.matmul(out=psv[:, 0:W], lhsT=fup[:, c, D_FF + cch * 128:D_FF + (cch + 1) * 128], rhs=h2T[:, c, 0:W],
                                           start=(c == 0), stop=(c == 7)), [fup, h2T], [psv], sig=(c == 7))
            G(lambda: nc.gpsimd.tensor_copy(out=ab_[:, 0:2], in_=cstate[:, cch, :]), [cstate], [ab_])
            A(lambda: nc.scalar.copy(out=ab_[:, 2:2 + W], in_=psa[:, 0:W]), [psa], [ab_])
            V(lambda: nc.vector.tensor_scalar(out=acc_[:, 0:W], in0=ab_[:, 0:W], scalar1=fcw[:, cch, 0:1], scalar2=None, op0=ALU.mult),
              [ab_, fcw], [acc_])
            for k in (1, 2):
                V(lambda: nc.vector.scalar_tensor_tensor(out=acc_[:, 0:W], in0=ab_[:, k:k + W], scalar=fcw[:, cch, k:k + 1],
                                                         in1=acc_[:, 0:W], op0=ALU.mult, op1=ALU.add), [ab_, fcw, acc_], [acc_])
            G(lambda: nc.gpsimd.tensor_copy(out=cstate[:, cch, :], in_=ab_[:, W:W + 2]), [ab_], [cstate])
            A(lambda: nc.scalar.activation(out=gl_[:, 0:W], in_=acc_[:, 0:W], func=AF.Gelu, bias=fcb[:, cch:cch + 1]), [acc_, fcb], [gl_])
            V(lambda: nc.vector.tensor_tensor(out=uT[:, cch, 0:W], in0=psv[:, 0:W], in1=gl_[:, 0:W], op=ALU.mult), [psv, gl_], [uT])
            yield
        if RESET_TILE is not None and grp == [0]:
            G(lambda: nc.gpsimd.tensor_scalar(out=cstate[:], in0=cstate[:], scalar1=flag, scalar2=None, op0=ALU.mult), [cstate, cst], [cstate])
        for j, oi in enumerate(grp):
            xm = xmd[cnt2["x"] % 2]
            cnt2["x"] += 1
            D([(xm[:], out_d[oi])], r=[xmid_db[oi]], w=[xm])
            for nb in range(2):
                ps = PS[5 + nb]
                for cch in range(22):
                    P(lambda: nc.tensor.matmul(out=ps[:, :], lhsT=uT[:, cch, j * 128:(j + 1) * 128], rhs=fdn[:, cch, nb * 512:(nb + 1) * 512],
                                               start=(cch == 0), stop=(cch == 21)), [uT, fdn], [ps], sig=(cch == 21))
                V(lambda: nc.vector.tensor_tensor(out=xm[:, nb * 512:(nb + 1) * 512], in0=ps[:, :], in1=xm[:, nb * 512:(nb + 1) * 512], op=ALU.add),
                  [ps, xm], [xm])
            D([(out_d[oi], xm[:])], r=[xm], w=[xmid_db[oi]])
            yield

    for _ in c2_head(0):
        pass
    for gi in range(len(groups)):
        nh = c2_head(gi + 1) if gi + 1 < len(groups) else None
        for _ in c2_body(gi):
            if nh is not None and next(nh, "end") == "end":
                nh = None
        if nh is not None:
            for _ in nh:
                pass
    T.barrier()
    pc2.close()
    return nc, T, dbg
    return nc, T, dbg


def prep_core(inp, b, h, NT, QLO, padded):
    S = NT * 128
    f32 = np.float32
    g = lambda k: np.asarray(inp[k], dtype=f32)[0]
    xb = np.asarray(inp["x"], dtype=f32)[b]
    if padded:
        half = S // 2
        if h == 0:
            x = np.concatenate([np.zeros((half, 1024), f32), xb[0:half]], 0)
            pos = np.arange(S) - half
        else:
            x = xb[0:S]
            pos = np.arange(S)
    else:
        x = xb[0:S]
        pos = np.arange(S)
    padlen = int((pos < 0).sum())
    NOWN = NT - QLO
    NCMP = 8 * NT - 1
    NCC = (8 * NT + 127) // 128
    NCPAD = NCC * 128
    NSEL = 2 * NT
    d = {"x": np.ascontiguousarray(x)}
    posc = np.maximum(pos, 0).astype(f32)
    inv = (f32(10000.0) ** (-np.arange(0, 64, 2, dtype=f32) / f32(64))).astype(f32)
    ang = (posc[:, None] * inv[None, :]).astype(f32)
    d["cos"] = np.cos(ang).astype(f32)
    d["sin"] = np.sin(ang).astype(f32)
    d["w_in"] = g("w_in")
    d["n1w"] = np.ascontiguousarray(g("norm1_w").reshape(8, 128).T)
    d["n2w"] = np.ascontiguousarray(g("norm2_w").reshape(8, 128).T)
    tabA = np.zeros((TA_N,), f32)
    tabA[TA_KNW:TA_KNW + 384] = np.concatenate([np.tile(g("kcmp_norm_w"), 2), np.tile(g("kslc_norm_w"), 2), np.tile(g("kwin_norm_w"), 2)])
    tabA[TA_QNW:TA_QNW + 512] = np.tile(g("q_norm_w"), 8)
    tabA[TA_BIG:TA_BIG + 4] = g("m_igate_b")
    tabA[TA_BFG:TA_BFG + 4] = g("m_fgate_b")
    tabA[TA_MONW:TA_MONW + 512] = g("m_out_norm_w").reshape(-1)
    d["tabA"] = np.ascontiguousarray(np.tile(tabA[None, :], (128, 1)))
    mcw = g("m_conv_w")
    mcb = g("m_conv_b")
    ch_of = [256 + np.arange(128), 384 + np.arange(128), np.arange(128), 128 + np.arange(128)]
    cw = np.zeros((128, 4, 4), f32)
    cb = np.zeros((128, 4), f32)
    for c in range(4):
        cw[:, c, :] = mcw[:, ch_of[c]].T
        cb[:, c] = mcb[ch_of[c]]
    d["cw"] = cw.reshape(128, 16)
    d["cb"] = cb
    fw = g("ffn_conv_w")
    d["fcw"] = np.ascontiguousarray(fw.reshape(3, 22, 128).transpose(2, 1, 0).reshape(128, 66))
    d["fcb"] = np.ascontiguousarray(g("ffn_conv_b").reshape(22, 128).T)
    d["mgb"] = g("merge_gate_b").reshape(1, 2048)
    kpe, vpe = g("cmp_k_pe"), g("cmp_v_pe")
    d["pe2"] = np.ascontiguousarray(np.concatenate([kpe, kpe, vpe, vpe], 1))
    p = np.arange(128)
    cst = np.zeros((128, 128 * 3 + 2 + 64 + 1), f32)
    cst[:, 0:128] = np.eye(128)
    cst[:, 128:256] = ((p[:, None] // 64 == p[None, :] // 64) & (p[:, None] <= p[None, :]))
    cst[:, 256:384] = 1.0
    cst[:, 384] = p < 64
    cst[:, 385] = p >= 64
    cst[:, 386:450] = (p[:, None] % 64) <= np.arange(64)[None, :]
    cst[:, 450] = 0.0 if (padded and h == 0) else 1.0
    d["cst"] = cst
    n = np.arange(NCPAD)
    j = np.arange(NSEL)
    ov = ((n[:, None] * 16 <= j[None, :] * 64 + 63) & (n[:, None] * 16 + 31 >= j[None, :] * 64) & (n[:, None] < NCMP)).astype(f32)
    d["ovl"] = np.ascontiguousarray(ov.reshape(NCC, 128, NSEL).transpose(1, 0, 2).reshape(128, NCC * NSEL))
    kk = np.arange(S)
    d["ewin"] = ((kk[None, :] // 64) == (64 * (kk[None, :] // 4096) + np.arange(64)[:, None])).astype(f32)
    diag = np.where(p[:, None] > p[None, :], NEGB, 0.0).astype(f32)
    anti = np.where(p[:, None] <= p[None, :], NEGB, 0.0).astype(f32)
    d["cbm"] = np.ascontiguousarray(np.concatenate([np.tile(diag, (1, 4)), np.tile(anti, (1, 4))], 1))
    tpos = pos[QLO * 128:].reshape(NOWN, 128)
    cend_real = 16 * n + 31 - padlen
    cvalid = (16 * n >= padlen) & (n < NCMP)
    d["cmask"] = np.where(cvalid[None, None, :] & (cend_real[None, None, :] <= tpos[:, :, None]), 0.0, NEGB * 8).astype(ml_dtypes.bfloat16)
    jr = j - padlen // 64
    cur = tpos // 64
    forced = (jr[None, None, :] == 0) | (jr[None, None, :] == cur[:, :, None]) | (jr[None, None, :] == cur[:, :, None] - 1)
    bad = (jr[None, None, :] > cur[:, :, None]) | (jr[None, None, :] < 0)
    d["imptab"] = np.where(bad, -1e30, np.where(forced, 1e4, 0.0)).astype(f32)
    d["rv"] = np.ascontiguousarray((tpos >= 31).astype(f32).T)
    d["kb"] = np.ascontiguousarray(np.where(pos.reshape(NT, 128).T < 0, NEGB, 0.0).astype(f32))
    for k in ("cmp_k_w1", "cmp_k_w2", "cmp_v_w1", "cmp_v_w2", "w_up_m", "w_up_n", "w_out", "ffn_w_up", "ffn_w_down"):
        d[k] = g(k)
    return d


NT_FULL, QLO_FULL, RESET_FULL = 64, 31, 32
_CACHE = {}


def kernel(**inputs):
    B = int(np.asarray(inputs["x"]).shape[0])
    if "nc" not in _CACHE:
        _CACHE["nc"] = build(NT_FULL, QLO_FULL, RESET_FULL)[0]
    nc = _CACHE["nc"]
    in_maps = []
    for b in range(B):
        for h in range(2):
            in_maps.append(prep_core(inputs, b, h, NT_FULL, QLO_FULL, True))
    res = run_bass_kernel_spmd(nc, in_maps, core_ids=list(range(2 * B)))
    out = np.zeros((B, NT_FULL * 128, 1024), np.float32)
    half = NT_FULL * 64
    for b in range(B):
        for h in range(2):
            o = np.asarray(res.results[b * 2 + h]["out"], dtype=np.float32).reshape(-1, 1024)
            out[b, h * half:(h + 1) * half] = o[128:128 + half]
    return out
```

```python
import math
from contextlib import ExitStack
import numpy as np
import ml_dtypes
import concourse.bass as bass
import concourse.mybir as mybir
from concourse.bass_utils import run_bass_kernel_spmd

F32 = mybir.dt.float32
BF16 = mybir.dt.bfloat16
AF = mybir.ActivationFunctionType
ALU = mybir.AluOpType
AX = mybir.AxisListType
EPS = 1e-6
NEGB = -30000.0


class Buf:
    __slots__ = ("name", "w", "r", "dsem", "dcnt", "excl", "rg")

    def __init__(self, name, excl=False):
        self.name = name
        self.rg = None
        self.excl = excl
        self.w = None
        self.r = {}
        self.dsem = None
        self.dcnt = 0


class Trk:
    ENG = ("pe", "act", "dve", "pool", "sp")

    def __init__(self, nc):
        self.nc = nc
        self.e = {"pe": nc.tensor, "act": nc.scalar, "dve": nc.vector,
                  "pool": nc.gpsimd, "sp": nc.sync}
        self.sems = {}
        self.cnt = {}
        for k in ("pe", "act", "dve", "pool"):
            self.sems[k] = nc.alloc_semaphore("s_" + k)
            self.cnt[k] = 0
        self.known = {k: {} for k in self.ENG}
        self.nd = 0
        self.dbufs = []
        self.ninstr = 0

    def _wait(self, eng, deps):
        kn = self.known[eng]
        best = {}
        for (s, v) in deps:
            if s == eng and v > self.cnt[eng]:
                continue
            if kn.get(s, 0) < v and best.get(s, 0) < v:
                best[s] = v
        for s, v in best.items():
            self.e[eng].wait_ge(self.sems[s], v)
            kn[s] = v

    @staticmethod
    def _deps(reads, writes):
        deps = []
        for b in reads:
            if b.w is not None:
                deps.append(b.w)
        for b in writes:
            if b.w is not None:
                deps.append(b.w)
            deps.extend(b.r.items())
        return deps

    def op(self, eng, fn, reads=(), writes=(), sig=True, rg="f"):
        ex = [b for b in reads if b.excl]
        if ex:
            reads = [b for b in reads if not b.excl]
            writes = list(writes) + ex
        deps = self._deps(reads, writes)
        if eng == "pe":
            drop = set()
            for b in writes:
                if b.w is not None and b.w[0] == "pe" and not ({b.rg, rg} == {0, 64}):
                    drop.add(b.w)
            keep = set()
            for b in writes:
                if b.w is not None and b.w[0] == "pe" and ({b.rg, rg} == {0, 64}):
                    keep.add(b.w)
                for it in b.r.items():
                    if it[0] == "pe":
                        keep.add(it)
            for b in reads:
                if b.w is not None and b.w[0] == "pe":
                    keep.add(b.w)
            deps = [d_ for d_ in deps if not (d_ in drop and d_ not in keep)]
            for b in writes:
                b.rg = rg
        self._wait(eng, deps)
        ins = fn()
        self.ninstr += 1
        if sig:
            self.cnt[eng] += 1
            ins.then_inc(self.sems[eng], 1)
            v = self.cnt[eng]
        else:
            v = self.cnt[eng] + 1
        for b in reads:
            b.r[eng] = v
        for b in writes:
            b.w = (eng, v)
            b.r = {}
        return ins

    def dma(self, q, pairs, reads=(), writes=(), sembuf=None):
        sb = sembuf if sembuf is not None else (writes[0] if writes else reads[0])
        if sb.dsem is None:
            key = "d%d" % self.nd
            self.nd += 1
            sb.dsem = key
            self.sems[key] = self.nc.alloc_semaphore(key)
            self.dbufs.append(sb)
        deps = self._deps(reads, writes)
        if sb.dcnt > 0:
            deps.append((sb.dsem, sb.dcnt))
        self._wait(q, deps)
        for (o, i) in pairs:
            self.e[q].dma_start(out=o, in_=i).then_inc(self.sems[sb.dsem], 16)
            sb.dcnt += 16
            self.ninstr += 1
        ev = (sb.dsem, sb.dcnt)
        for b in reads:
            b.r[ev[0]] = ev[1]
        for b in writes:
            b.w = ev
            b.r = {}
        return ev

    def barrier(self):
        deps = [(k, self.cnt[k]) for k in ("pe", "act", "dve", "pool") if self.cnt[k] > 0]
        deps += [(b.dsem, b.dcnt) for b in self.dbufs if b.dcnt > 0]
        for eng in self.ENG:
            self._wait(eng, deps)


def pipeline(items, body, skew, maxact=2):
    it = iter(items)
    act = []
    done = False
    while True:
        if not done and len(act) < maxact and (not act or act[-1][1] >= skew):
            try:
                act.append([body(next(it)), 0])
            except StopIteration:
                done = True
        if not act:
            if done:
                break
            continue
        for a in list(act):
            try:
                next(a[0])
                a[1] += 1
            except StopIteration:
                act.remove(a)


class TB:
    def __init__(self, t, name, excl=False):
        self.t = t
        self.b = Buf(name, excl)

    def __getitem__(self, idx):
        return self.t[idx]


IN_OFF = {}
_o = 0
for _n, _s in (("mq", 256), ("mk", 256), ("mv", 512), ("mo", 512), ("mi", 4), ("mf", 4),
               ("nq", 512), ("kc", 128), ("vc", 128), ("ks", 128), ("vs", 128), ("kw", 128),
               ("vw", 128), ("ng", 24), ("gm", 1024), ("gn", 1024)):
    IN_OFF[_n] = (_o, _s)
    _o += _s
D_IN = _o
D_FF = 2816

WA = {}
_o = 0
for _n in ("mv", "kc", "ks", "kw", "mi", "mf", "vs", "vw", "mo", "mk", "mq", "vc"):
    WA[_n] = _o
    _o += IN_OFF[_n][1]
WA_N = _o
TA_KNW, TA_QNW, TA_BIG, TA_BFG, TA_MONW, TA_N = 0, 384, 896, 900, 904, 1416


def build(NT, QLO, RESET_TILE, debug=0):
    S = NT * 128
    NOWN = NT - QLO
    NCMP = 8 * NT - 1
    NCC = (8 * NT + 127) // 128
    NCPAD = NCC * 128
    NSEL = 2 * NT
    nc = bass.Bass("TRN2", target_bir_lowering=False)
    T = Trk(nc)

    def din(name, shape, dt=F32):
        return nc.dram_tensor(name, list(shape), dt, kind="ExternalInput").ap()

    x_d = din("x", [S, 1024])
    cos_d = din("cos", [S, 32])
    sin_d = din("sin", [S, 32])
    w_in_d = din("w_in", [1024, D_IN])
    n1w_d = din("n1w", [128, 8])
    n2w_d = din("n2w", [128, 8])
    tabA_d = din("tabA", [128, TA_N])
    cw_d = din("cw", [128, 16])
    cb_d = din("cb", [128, 4])
    fcw_d = din("fcw", [128, 66])
    fcb_d = din("fcb", [128, 22])
    mgb_d = din("mgb", [1, 2048])
    pe2_d = din("pe2", [32, 256])
    cst_d = din("cst", [128, 128 * 3 + 2 + 64 + 1])
    ovl_d = din("ovl", [128, NCC * NSEL])
    ewin_d = din("ewin", [64, S])
    cbm_d = din("cbm", [128, 1024])
    cmask_d = din("cmask", [NOWN, 128, NCPAD], BF16)
    imptab_d = din("imptab", [NOWN, 128, NSEL])
    kb_d = din("kb", [128, NT])
    rv_d = din("rv", [128, NOWN])
    w1k_d = din("cmp_k_w1", [2048, 256])
    w2k_d = din("cmp_k_w2", [256, 64])
    w1v_d = din("cmp_v_w1", [2048, 256])
    w2v_d = din("cmp_v_w2", [256, 64])
    wupm_d = din("w_up_m", [512, 1024])
    wupn_d = din("w_up_n", [512, 1024])
    wout_d = din("w_out", [1024, 1024])
    fup_d = din("ffn_w_up", [1024, 2 * D_FF])
    fdn_d = din("ffn_w_down", [D_FF, 1024])
    out_d = nc.dram_tensor("out", [NOWN, 128, 1024], F32, kind="ExternalOutput").ap()
    out_b = Buf("out")
    hmT_d = nc.dram_tensor("hmT_scr", [NOWN, 128, 512], BF16, kind="Internal").ap()
    onT_d = nc.dram_tensor("onT_scr", [NOWN, 128, 512], BF16, kind="Internal").ap()
    kcs_d = nc.dram_tensor("kcs_scr", [128, S], BF16, kind="Internal").ap()
    vcs_d = nc.dram_tensor("vcs_scr", [128, S], BF16, kind="Internal").ap()
    kvs_db = Buf("kvs_scr")
    hmT_db = [Buf("hmTd%d" % i) for i in range(NOWN)]
    onT_db = [Buf("onTd%d" % i) for i in range(NOWN)]
    xmid_db = [Buf("xmid%d" % i) for i in range(NOWN)]
    dbg = {}
    if debug:
        def dout(name, shape, dt=F32):
            dbg[name] = (nc.dram_tensor(name, list(shape), dt, kind="ExternalOutput").ap(), Buf(name))
            return dbg[name]

    def V(fn, r=(), w=()):
        return T.op("dve", fn, [a.b for a in r], [a.b for a in w])

    def A(fn, r=(), w=()):
        return T.op("act", fn, [a.b for a in r], [a.b for a in w])

    def G(fn, r=(), w=()):
        return T.op("pool", fn, [a.b for a in r], [a.b for a in w])

    def P(fn, r=(), w=(), sig=True, rg="f"):
        return T.op("pe", fn, [a.b for a in r], [a.b for a in w], sig, rg)

    def D(pairs, r=(), w=(), q="sp", sembuf=None):
        if sembuf is None:
            cand = [a for a in list(w) + list(r) if isinstance(a, TB)]
            sembuf = cand[0].b if cand else None
        return T.dma(q, pairs, [a if isinstance(a, Buf) else a.b for a in r],
                     [a if isinstance(a, Buf) else a.b for a in w], sembuf)

    main = ExitStack()

    def mk(es, name, shape, dt=F32):
        return TB(es.enter_context(nc.sbuf_tensor("sb_" + name, list(shape), dt)), name)

    PS = [TB(nc.alloc_psum_tensor("ps%d" % i, [128, 512], F32), "ps%d" % i, True) for i in range(8)]

    def bfv(ps, ncols):
        return ps[:, 0:ncols // 2].bitcast(BF16)

    cst = mk(main, "cst", [128, 128 * 3 + 2 + 64 + 1])
    D([(cst[:], cst_d)], w=[cst])
    identf = cst[:, 0:128]
    U2 = cst[:, 128:256]
    onesf = cst[:, 256:384]
    m01 = cst[:, 384:386]
    mask_st = cst[:, 386:450]
    flag = cst[:, 450:451]
    identb = mk(main, "identb", [128, 128], BF16)
    V(lambda: nc.vector.tensor_copy(out=identb[:], in_=identf), [cst], [identb])
    onesb = mk(main, "onesb", [128, 128], BF16)
    V(lambda: nc.vector.tensor_copy(out=onesb[:], in_=onesf), [cst], [onesb])
    tabA = mk(main, "tabA", [128, TA_N])
    D([(tabA[:], tabA_d)], w=[tabA])
    n1w = mk(main, "n1w", [128, 8])
    D([(n1w[:], n1w_d)], w=[n1w])
    n2w = mk(main, "n2w", [128, 8])
    D([(n2w[:], n2w_d)], w=[n2w])
    kbias = mk(main, "kbias", [128, NT])
    D([(kbias[:], kb_d)], w=[kbias])
    rvt = mk(main, "rvt", [128, NOWN])
    D([(rvt[:], rv_d)], w=[rvt])

    def load_w(es_stage, dst_tb, dst_ap, src_ap, shape, scale_ap=None, eng="dve", scale_tb=None):
        st = es_stage[0][es_stage[1] % len(es_stage[0])]
        es_stage[1] += 1
        if len(shape) == 2:
            sv = st[:, 0:shape[1]]
        else:
            sv = st[:, 0:shape[1] * shape[2]].rearrange("p (a n) -> p a n", a=shape[1])
        D([(sv, src_ap)], w=[st])
        if scale_ap is None:
            if eng == "dve":
                V(lambda: nc.vector.tensor_copy(out=dst_ap, in_=sv), [st], [dst_tb])
            elif eng == "act":
                A(lambda: nc.scalar.copy(out=dst_ap, in_=sv), [st], [dst_tb])
            else:
                G(lambda: nc.gpsimd.tensor_copy(out=dst_ap, in_=sv), [st], [dst_tb])
        else:
            V(lambda: nc.vector.tensor_tensor(out=dst_ap, in0=sv, in1=scale_ap, op=ALU.mult), [st, scale_tb], [dst_tb])

    nhalf = mk(main, "nhalf", [128, 8])
    G(lambda: nc.gpsimd.memset(nhalf[:], -0.5), [], [nhalf])

    def rstd_pow(rs, ss, n):
        w = ss.t.shape[1]
        G(lambda: nc.gpsimd.tensor_scalar(out=rs[:], in0=ss[:], scalar1=1.0 / n, scalar2=EPS, op0=ALU.mult, op1=ALU.add), [ss], [rs])
        G(lambda: nc.gpsimd.tensor_tensor(out=rs[:], in0=rs[:], in1=nhalf[:, 0:w], op=ALU.pow), [rs, nhalf], [rs])

    def rmsnorm_T(xt, tmp, ss, rs, xn, xnT, psT, evac="dve"):
        A(lambda: nc.scalar.activation(out=tmp[:], in_=xt[:], func=AF.Square, accum_out=ss[:]), [xt], [tmp, ss])
        rstd_pow(rs, ss, 1024)
        A(lambda: nc.scalar.activation(out=xn[:], in_=xt[:], func=AF.Copy, scale=rs[:]), [xt, rs], [xn])
        pv = bfv(psT, 1024)
        for c in range(8):
            P(lambda: nc.tensor.transpose(out=pv[:, c * 128:(c + 1) * 128], in_=xn[:, c * 128:(c + 1) * 128],
                                          identity=identb[:]), [xn, identb], [psT], sig=(c == 7))
        if evac == "dve":
            V(lambda: nc.vector.tensor_copy(out=xnT[:].rearrange("p c t -> p (c t)"), in_=pv), [psT], [xnT])
        else:
            A(lambda: nc.scalar.copy(out=xnT[:].rearrange("p c t -> p (c t)"), in_=pv), [psT], [xnT])

    kvs = ExitStack()
    KE = [mk(kvs, "KE%d" % g, [128, S], BF16) for g in range(2)]
    kwT = mk(kvs, "kwT", [128, S], BF16)
    vsA = mk(kvs, "vsA", [128, NT, 2, 65], BF16)
    vwA = mk(kvs, "vwA", [128, NT, 2, 65], BF16)
    kcmpT = mk(kvs, "kcmpT", [128, NCPAD], BF16)
    vcmp = mk(kvs, "vcmp", [128, NCC, 2, 64], BF16)
    G(lambda: nc.gpsimd.memset(vsA[:], 1.0), [], [vsA])
    G(lambda: nc.gpsimd.memset(vwA[:], 1.0), [], [vwA])
    G(lambda: nc.gpsimd.memset(kcmpT[:], 0.0), [], [kcmpT])
    G(lambda: nc.gpsimd.memset(vcmp[:], 0.0), [], [vcmp])

    pa0 = ExitStack()
    pa = ExitStack()
    wA = mk(pa, "wA", [128, 8, WA_N], BF16)
    with ExitStack() as stg:
        stage = [[mk(stg, "stgA%d" % i, [128, 2048]) for i in range(2)], 0]
        w_in_v = w_in_d.rearrange("(c p) n -> p c n", p=128)
        for name in ("mv", "kc", "ks", "kw", "mi", "mf", "vs", "vw", "mo", "mk", "mq", "vc"):
            so, sn = IN_OFF[name]
            do = WA[name]
            for o in range(0, sn, 256):
                n = min(256, sn - o)
                load_w(stage, wA, wA[:, :, do + o:do + o + n], w_in_v[:, :, so + o:so + o + n], [128, 8, n],
                       scale_ap=n1w[:, :].unsqueeze(2).broadcast_to([128, 8, n]), scale_tb=n1w)
        T.barrier()
    xt2 = [mk(pa, "xt%d" % i, [128, 1024]) for i in range(2)]
    two = lambda nm, shp, dt=F32: [mk(pa, "%s_%d" % (nm, i), shp, dt) for i in range(2)]
    three = lambda nm, shp, dt=F32: [mk(pa, "%s_%d" % (nm, i), shp, dt) for i in range(4)]
    ss1_2, rs1_2 = two("ss1", [128, 1]), two("rs1", [128, 1])
    xn_2 = two("xn", [128, 1024], BF16)
    xnT_2 = two("xnT", [128, 8, 128], BF16)
    cs2 = two("cs", [128, 64])
    ksq_2 = two("ksq", [128, 384])
    ss6_2, rs6_2 = two("ss6", [128, 6]), two("rs6", [128, 6])
    kn_2 = two("kn", [128, 6, 64])
    rt_2 = [two("rt%d" % i, [128, 6, 32]) for i in range(4)]
    kr_2 = two("kr", [128, 6, 64], BF16)
    convb_2 = two("convb", [128, 4, 131])
    cacc_2 = two("cacc", [128, 4, 128])
    cw = mk(pa, "cw", [128, 4, 4])
    cb = mk(pa, "cb", [128, 4])
    D([(cw[:].rearrange("p a b -> p (a b)"), cw_d)], w=[cw])
    D([(cb[:], cb_d)], w=[cb])
    zg_2, sp_2, ip_2 = two("zg", [128, 4]), two("sp", [128, 4]), two("ip", [128, 4])
    sp8_2, es_2, expg_2 = two("sp8", [128, 8]), two("es", [128, 4]), two("expg", [128, 8])
    kvst_2 = two("kvst", [128, 2, 128], BF16)
    kqT2 = three("kqT", [128, 4, 128], BF16)
    ktil2 = three("ktil", [128, 4, 64], BF16)
    vaug2 = three("vaug", [128, 4, 129], BF16)
    esm2 = three("esm", [128, 4, 64])
    eb82 = three("eb8", [128, 4])
    egp2 = three("egp", [128, 2, 2])
    sigo2 = three("sigo", [128, 512])
    for v_ in vaug2:
        G(lambda: nc.gpsimd.memset(v_[:], 1.0), [], [v_])
    for c_ in convb_2:
        V(lambda: nc.vector.memset(c_[:], 0.0), [], [c_])
    Cst = mk(pa, "Cst", [128, 2, 129])
    snap = [mk(pa, "snap%d" % i, [128, 2, 129], BF16) for i in range(8)]
    V(lambda: nc.vector.memset(Cst[:], 0.0), [], [Cst])
    V(lambda: nc.vector.memset(snap[0][:], 0.0), [], [snap[0]])
    PT = mk(pa, "PT", [128, 4, 64], BF16)
    d4 = [mk(pa, "d4_%d" % i, [128, 4]) for i in range(3)]
    hraw = mk(pa, "hraw", [128, 4, 128])
    ss4 = mk(pa, "ss4", [128, 4])
    rs4 = mk(pa, "rs4", [128, 4])
    hm = mk(pa, "hm", [128, 512], BF16)
    hmTs = mk(pa, "hmTs", [128, 512], BF16)
    lnc = math.log(0.125)
    psT, pA_, pB_, psS, psKV, psSTm, psO0, psO1 = PS
    prot = {"i": 0}

    def a_front(t):
        own = t >= QLO
        needq = t >= QLO - 1
        p_ = t % 2
        h_ = t % 4
        xt, cs = xt2[p_], cs2[p_]
        vaug, kqT, ktil, esm, eb8, egp, sigo = vaug2[h_], kqT2[h_], ktil2[h_], esm2[h_], eb82[h_], egp2[h_], sigo2[h_]
        ss1, rs1, xn, xnT, ksq, ss6, rs6, kn, kr = ss1_2[p_], rs1_2[p_], xn_2[p_], xnT_2[p_], ksq_2[p_], ss6_2[p_], rs6_2[p_], kn_2[p_], kr_2[p_]
        rt = [rt_2[i][p_] for i in range(4)]
        convb, convn, cacc = convb_2[p_], convb_2[1 - p_], cacc_2[p_]
        zg, sp, ip, sp8, es_, expg, kvst = zg_2[p_], sp_2[p_], ip_2[p_], sp8_2[p_], es_2[p_], expg_2[p_], kvst_2[p_]
        sqj = xn
        D([(xt[:], x_d[t * 128:(t + 1) * 128, :])], w=[xt])
        D([(cs[:, 0:32], cos_d[t * 128:(t + 1) * 128, :]), (cs[:, 32:64], sin_d[t * 128:(t + 1) * 128, :])], w=[cs])
        A(lambda: nc.scalar.activation(out=sqj[:], in_=xt[:], func=AF.Square, accum_out=ss1[:]), [xt], [sqj, ss1])
        yield
        rstd_pow(rs1, ss1, 1024)
        yield
        A(lambda: nc.scalar.activation(out=xn[:], in_=xt[:], func=AF.Copy, scale=rs1[:]), [xt, rs1], [xn])
        yield
        pv = bfv(psT, 1024)
        for c in range(8):
            P(lambda: nc.tensor.transpose(out=pv[:, c * 128:(c + 1) * 128], in_=xn[:, c * 128:(c + 1) * 128], identity=identb[:]),
              [xn, identb], [psT], sig=(c == 7))
        V(lambda: nc.vector.tensor_copy(out=xnT[:].rearrange("p c t -> p (c t)"), in_=pv), [psT], [xnT])
        yield

        def bank():
            prot["i"] += 1
            return (pA_, pB_)[prot["i"] % 2]

        def proj_tm(ps, col0, ncols, wcol):
            for c in range(8):
                P(lambda: nc.tensor.matmul(out=ps[:, col0:col0 + ncols], lhsT=xnT[:, c, :], rhs=wA[:, c, wcol:wcol + ncols],
                                           start=(c == 0), stop=(c == 7)), [xnT, wA], [ps], sig=(c == 7))

        def proj_fm(ps, col0, wcol):
            for c in range(8):
                P(lambda: nc.tensor.matmul(out=ps[:, col0:col0 + 128], lhsT=wA[:, c, wcol:wcol + 128], rhs=xnT[:, c, :],
                                           start=(c == 0), stop=(c == 7)), [xnT, wA], [ps], sig=(c == 7))

        psK = bank()
        proj_tm(psK, 0, 392, WA["kc"])
        A(lambda: nc.scalar.activation(out=ksq[:], in_=psK[:, 0:384], func=AF.Square), [psK], [ksq])
        V(lambda: nc.vector.tensor_copy(out=kn[:].rearrange("p a d -> p (a d)"), in_=psK[:, 0:384]), [psK], [kn])
        V(lambda: nc.vector.tensor_tensor(out=zg[:], in0=psK[:, 388:392], in1=tabA[:, TA_BFG:TA_BFG + 4], op=ALU.add), [psK, tabA], [zg])
        V(lambda: nc.vector.tensor_tensor(out=ip[:], in0=psK[:, 384:388], in1=tabA[:, TA_BIG:TA_BIG + 4], op=ALU.add), [psK, tabA], [ip])
        yield
        psMV = bank()
        proj_tm(psMV, 0, 512, WA["mv"])
        A(lambda: nc.scalar.copy(out=vaug[:, :, 0:128], in_=psMV[:, :].rearrange("p (h d) -> p h d", h=4)), [psMV], [vaug])
        yield
        psV = bank()
        proj_tm(psV, 0, 256, WA["vs"])
        proj_fm(psV, 256, WA["vc"])
        tsl = slice(t * 128, (t + 1) * 128)
        A(lambda: nc.scalar.copy(out=vsA[:, t, :, 0:64], in_=psV[:, 0:128].rearrange("p (g d) -> p g d", g=2)), [psV], [vsA])
        A(lambda: nc.scalar.copy(out=vwA[:, t, :, 0:64], in_=psV[:, 128:256].rearrange("p (g d) -> p g d", g=2)), [psV], [vwA])
        V(lambda: nc.vector.tensor_copy(out=kvst[:, 1, :], in_=psV[:, 256:384]), [psV], [kvst])
        yield
        psF = bank()
        proj_fm(psF, 0, WA["mk"])
        proj_fm(psF, 128, WA["mk"] + 128)
        if needq:
            proj_fm(psF, 256, WA["mq"])
            proj_fm(psF, 384, WA["mq"] + 128)
        nch = 4 if needq else 2
        A(lambda: nc.scalar.copy(out=convb[:, 0:nch, 3:131], in_=psF[:, 0:nch * 128].rearrange("p (a t) -> p a t", a=nch)), [psF], [convb])
        yield
        if own:
            psMO = bank()
            proj_tm(psMO, 0, 512, WA["mo"])
            A(lambda: nc.scalar.activation(out=sigo[:], in_=psMO[:, 0:512], func=AF.Sigmoid), [psMO], [sigo])
            yield
        def chain_keys():
            V(lambda: nc.vector.tensor_reduce(out=ss6[:], in_=ksq[:].rearrange("p (a d) -> p a d", a=6), axis=AX.X, op=ALU.add), [ksq], [ss6])
            yield
            rstd_pow(rs6, ss6, 64)
            yield
            V(lambda: nc.vector.tensor_tensor(out=kn[:], in0=kn[:], in1=rs6[:, :].unsqueeze(2).broadcast_to([128, 6, 64]), op=ALU.mult), [kn, rs6], [kn])
            V(lambda: nc.vector.tensor_tensor(out=kn[:], in0=kn[:], in1=tabA[:, TA_KNW:TA_KNW + 384].rearrange("p (a d) -> p a d", a=6),
                                              op=ALU.mult), [kn, tabA], [kn])
            yield
            cosb = cs[:, 0:32].unsqueeze(1).broadcast_to([128, 6, 32])
            sinb = cs[:, 32:64].unsqueeze(1).broadcast_to([128, 6, 32])
            V(lambda: nc.vector.tensor_tensor(out=rt[0][:], in0=kn[:, :, 0:32], in1=cosb, op=ALU.mult), [kn, cs], [rt[0]])
            G(lambda: nc.gpsimd.tensor_tensor(out=rt[1][:], in0=kn[:, :, 32:64], in1=sinb, op=ALU.mult), [kn, cs], [rt[1]])
            V(lambda: nc.vector.tensor_tensor(out=rt[2][:], in0=kn[:, :, 32:64], in1=cosb, op=ALU.mult), [kn, cs], [rt[2]])
            G(lambda: nc.gpsimd.tensor_tensor(out=rt[3][:], in0=kn[:, :, 0:32], in1=sinb, op=ALU.mult), [kn, cs], [rt[3]])
            yield
            V(lambda: nc.vector.tensor_tensor(out=kr[:, :, 0:32], in0=rt[0][:], in1=rt[1][:], op=ALU.subtract), [rt[0], rt[1]], [kr])
            G(lambda: nc.gpsimd.tensor_tensor(out=kr[:, :, 32:64], in0=rt[2][:], in1=rt[3][:], op=ALU.add), [rt[2], rt[3]], [kr])
            yield
            pk = bfv(psS, 512)
            P(lambda: nc.tensor.transpose(out=pk[:, 0:128], in_=kr[:, 0:2, :].rearrange("p a d -> p (a d)"), identity=identb[:]), [kr, identb], [psS], sig=False)
            P(lambda: nc.tensor.transpose(out=pk[:, 128:256], in_=kr[:, 4:6, :].rearrange("p a d -> p (a d)"), identity=identb[:]), [kr, identb], [psS], sig=False)
            for g in range(2):
                P(lambda: nc.tensor.transpose(out=pk[0:64, 256 + g * 128:256 + (g + 1) * 128], in_=kr[:, 2 + g, :], identity=identb[:]), [kr, identb], [psS],
                  sig=(g == 1))
            A(lambda: nc.scalar.copy(out=kvst[:, 0, :], in_=pk[:, 0:128]), [psS], [kvst])
            A(lambda: nc.scalar.copy(out=kwT[:, tsl], in_=pk[:, 128:256]), [psS], [kwT])
            for g in range(2):
                V(lambda: nc.vector.tensor_copy(out=KE[g][0:64, tsl], in_=pk[0:64, 256 + g * 128:256 + (g + 1) * 128]), [psS], [KE[g]])
            D([(kcs_d[:, tsl], kvst[:, 0, :]), (vcs_d[:, tsl], kvst[:, 1, :])], r=[kvst], w=[kvs_db])
            yield

        def chain_conv():
            for j in range(nch):
                V(lambda: nc.vector.tensor_scalar(out=cacc[:, j, :], in0=convb[:, j, 0:128], scalar1=cw[:, j, 0:1], scalar2=None, op0=ALU.mult),
                  [convb, cw], [cacc])
                for k in range(1, 4):
                    V(lambda: nc.vector.scalar_tensor_tensor(out=cacc[:, j, :], in0=convb[:, j, k:k + 128], scalar=cw[:, j, k:k + 1],
                                                             in1=cacc[:, j, :], op0=ALU.mult, op1=ALU.add), [convb, cw, cacc], [cacc])
                A(lambda: nc.scalar.activation(out=kqT[:, j, :], in_=cacc[:, j, :], func=AF.Silu, bias=cb[:, j:j + 1]), [cacc, cb], [kqT])
                yield
            G(lambda: nc.gpsimd.tensor_copy(out=convn[:, 0:nch, 0:3], in_=convb[:, 0:nch, 128:131]), [convb], [convn])
            yield

        def chain_gates():
            A(lambda: nc.scalar.activation(out=zg[:], in_=zg[:], func=AF.Exp, scale=-1.0), [zg], [zg])
            yield
            A(lambda: nc.scalar.activation(out=sp[:], in_=zg[:], func=AF.Ln, bias=1.0), [zg], [sp])
            yield
            V(lambda: nc.vector.tensor_scalar(out=sp8[:, 0:4], in0=sp[:], scalar1=m01[:, 0:1], scalar2=None, op0=ALU.mult), [sp, cst], [sp8])
            V(lambda: nc.vector.tensor_scalar(out=sp8[:, 4:8], in0=sp[:], scalar1=m01[:, 1:2], scalar2=None, op0=ALU.mult), [sp, cst], [sp8])
            yield

        chains = [chain_keys(), chain_conv(), chain_gates()]
        while chains:
            for g_ in list(chains):
                if next(g_, "end") == "end":
                    chains.remove(g_)
            yield
        P(lambda: nc.tensor.matmul(out=psS[:, 256:260], lhsT=U2, rhs=sp[:], start=True, stop=True), [cst, sp], [psS])
        P(lambda: nc.tensor.matmul(out=psS[:, 264:272], lhsT=onesf, rhs=sp8[:], start=True, stop=True), [cst, sp8], [psS])
        pkt = psS[:, 272:400].bitcast(BF16)
        for j in range(2):
            P(lambda: nc.tensor.transpose(out=pkt[:, j * 128:(j + 1) * 128], in_=kqT[:, j, :], identity=identb[:]), [kqT, identb], [psS], sig=(j == 1))
        V(lambda: nc.vector.tensor_tensor(out=es_[:], in0=psS[:, 256:260], in1=ip[:], op=ALU.add), [psS, ip], [es_])
        A(lambda: nc.scalar.activation(out=eb8[:], in_=psS[:, 256:260], func=AF.Exp, scale=-1.0, bias=lnc), [psS], [eb8])
        A(lambda: nc.scalar.activation(out=expg[:], in_=psS[:, 264:272], func=AF.Exp, scale=-1.0), [psS], [expg])
        A(lambda: nc.scalar.activation(out=es_[:], in_=es_[:], func=AF.Exp), [es_], [es_])
        egv = expg[:, :].rearrange("p (ch c par) -> p ch c par", ch=2, c=2)
        V(lambda: nc.vector.tensor_copy(out=egp[0:64, :, :], in_=egv[0:64, :, :, 0]), [expg], [egp])
        V(lambda: nc.vector.tensor_copy(out=egp[64:128, :, :], in_=egv[64:128, :, :, 1]), [expg], [egp])
        V(lambda: nc.vector.tensor_tensor(out=ktil[:], in0=pkt.rearrange("p (h d) -> p h d", h=4),
                                          in1=es_[:, :].unsqueeze(2).broadcast_to([128, 4, 64]), op=ALU.mult), [psS, es_], [ktil])
        if own:
            V(lambda: nc.vector.tensor_tensor(out=esm[:], in0=mask_st.unsqueeze(1).broadcast_to([128, 4, 64]),
                                              in1=es_[:, :].unsqueeze(2).broadcast_to([128, 4, 64]), op=ALU.mult), [cst, es_], [esm])
        yield

    def a_scan(t):
        h_ = t % 4
        vaug, ktil, egp = vaug2[h_], ktil2[h_], egp2[h_]
        if RESET_TILE is not None and t == RESET_TILE:
            sn = snap[(2 * t) % 8]
            V(lambda: nc.vector.tensor_scalar(out=Cst[:], in0=Cst[:], scalar1=flag, scalar2=None, op0=ALU.mult), [Cst, cst], [Cst])
            V(lambda: nc.vector.tensor_scalar(out=sn[:], in0=sn[:], scalar1=flag, scalar2=None, op0=ALU.mult), [sn, cst], [sn])
        for ch in range(2):
            rows = slice(ch * 64, ch * 64 + 64)
            kvv = psKV[:, 0:258].rearrange("p (c e) -> p c e", c=2)
            for h in range(4):
                c, par = h // 2, h % 2
                P(lambda: nc.tensor.matmul(out=kvv[par * 64:(par + 1) * 64, c, :], lhsT=ktil[rows, h, :], rhs=vaug[rows, h, :],
                                           start=True, stop=True), [ktil, vaug], [psKV], sig=(h == 3), rg=ch * 64)
            V(lambda: nc.vector.tensor_tensor(out=Cst[:], in0=kvv, in1=Cst[:], op=ALU.add), [psKV, Cst], [Cst])
            yield
            V(lambda: nc.vector.tensor_tensor(out=Cst[:], in0=Cst[:], in1=egp[:, ch, :].unsqueeze(2).broadcast_to([128, 2, 129]),
                                              op=ALU.mult), [Cst, egp], [Cst])
            yield
            nx = snap[(2 * t + ch + 1) % 8]
            A(lambda: nc.scalar.copy(out=nx[:], in_=Cst[:]), [Cst], [nx])
            yield

    def a_out(t):
        h_ = t % 4
        vaug, kqT, esm, eb8, sigo = vaug2[h_], kqT2[h_], esm2[h_], eb82[h_], sigo2[h_]
        psO = (psO0, psO1)
        for ch in range(2):
            rows = slice(ch * 64, ch * 64 + 64)
            csl = slice(ch * 64, ch * 64 + 64)
            Cbf = snap[(2 * t + ch) % 8]
            stp = psSTm[:, 0:256].rearrange("p (h s) -> p h s", h=4)
            for h in range(4):
                c, par = h // 2, h % 2
                prow = slice(par * 64, par * 64 + 64)
                P(lambda: nc.tensor.matmul(out=stp[rows, h, :], lhsT=kqT[prow, c, csl], rhs=kqT[prow, 2 + c, csl],
                                           start=True, stop=True), [kqT], [psSTm], sig=True, rg=par * 64)
            V(lambda: nc.vector.tensor_tensor(out=PT[rows], in0=stp[rows], in1=esm[rows], op=ALU.mult), [psSTm, esm], [PT])
            yield
            for h in range(4):
                c, par = h // 2, h % 2
                prow = slice(par * 64, par * 64 + 64)
                ov = psO[c][:, 0:258].rearrange("p (a e) -> p a e", a=2)
                P(lambda: nc.tensor.matmul(out=ov[rows, par, :], lhsT=kqT[prow, 2 + c, csl], rhs=Cbf[prow, c, :],
                                           start=True, stop=False), [kqT, Cbf], [psO[c]], sig=True, rg=par * 64)
                P(lambda: nc.tensor.matmul(out=ov[rows, par, :], lhsT=PT[rows, h, :], rhs=vaug[rows, h, :],
                                           start=False, stop=True), [PT, vaug], [psO[c]], sig=True, rg=ch * 64)
            yield
        oi = t - QLO
        for c in range(2):
            ov = psO[c][:, 0:258].rearrange("p (a e) -> p a e", a=2)
            V(lambda: nc.vector.tensor_tensor(out=d4[0][:, 2 * c:2 * c + 2], in0=ov[:, :, 128], in1=eb8[:, 2 * c:2 * c + 2], op=ALU.mult),
              [psO[c], eb8], [d4[0]])
        V(lambda: nc.vector.scalar_tensor_tensor(out=d4[1][:], in0=d4[0][:], scalar=-1.0, in1=d4[0][:], op0=ALU.mult, op1=ALU.max), [d4[0]], [d4[1]])
        V(lambda: nc.vector.tensor_scalar(out=d4[1][:], in0=d4[1][:], scalar1=1.0, scalar2=None, op0=ALU.max), [d4[1]], [d4[1]])
        V(lambda: nc.vector.reciprocal(out=d4[1][:], in_=d4[1][:]), [d4[1]], [d4[1]])
        V(lambda: nc.vector.tensor_tensor(out=d4[2][:], in0=d4[1][:], in1=eb8[:], op=ALU.mult), [d4[1], eb8], [d4[2]])
        for c in range(2):
            ov = psO[c][:, 0:258].rearrange("p (a e) -> p a e", a=2)
            V(lambda: nc.vector.tensor_tensor(out=hraw[:, 2 * c:2 * c + 2, :], in0=ov[:, :, 0:128],
                                              in1=d4[2][:, 2 * c:2 * c + 2].unsqueeze(2).broadcast_to([128, 2, 128]), op=ALU.mult),
              [psO[c], d4[2]], [hraw])
        yield
        for h in range(4):
            A(lambda: nc.scalar.activation(out=hm[:, h * 128:(h + 1) * 128], in_=hraw[:, h, :], func=AF.Square, accum_out=ss4[:, h:h + 1]),
              [hraw], [hm, ss4])
        yield
        rstd_pow(rs4, ss4, 128)
        yield
        V(lambda: nc.vector.tensor_tensor(out=hraw[:], in0=hraw[:], in1=rs4[:, :].unsqueeze(2).broadcast_to([128, 4, 128]), op=ALU.mult),
          [hraw, rs4], [hraw])
        G(lambda: nc.gpsimd.tensor_tensor(out=hraw[:], in0=hraw[:], in1=tabA[:, TA_MONW:TA_MONW + 512].rearrange("p (h d) -> p h d", h=4),
                                          op=ALU.mult), [hraw, tabA], [hraw])
        yield
        V(lambda: nc.vector.tensor_tensor(out=hm[:], in0=hraw[:].rearrange("p h d -> p (h d)"), in1=sigo[:], op=ALU.mult), [hraw, sigo], [hm])
        yield
        ph = bfv(psSTm, 512)
        for c in range(4):
            P(lambda: nc.tensor.transpose(out=ph[:, c * 128:(c + 1) * 128], in_=hm[:, c * 128:(c + 1) * 128], identity=identb[:]),
              [hm, identb], [psSTm], sig=(c == 3))
        A(lambda: nc.scalar.copy(out=hmTs[:], in_=ph), [psSTm], [hmTs])
        D([(hmT_d[oi], hmTs[:])], r=[hmTs], w=[hmT_db[oi]])
        yield

    fs = {"next": 0, "act": [], "done": -1}
    SKEW = 8

    def ftick(limit):
        if fs["next"] < NT and fs["next"] <= limit and len(fs["act"]) < 2 and (not fs["act"] or fs["act"][-1][2] >= SKEW):
            fs["act"].append([fs["next"], a_front(fs["next"]), 0])
            fs["next"] += 1
        for a in list(fs["act"]):
            if next(a[1], "end") == "end":
                fs["act"].remove(a)
                fs["done"] = max(fs["done"], a[0])
            else:
                a[2] += 1

    prev_out = None
    for t in range(NT):
        while fs["done"] < t:
            ftick(t + 1)
            if prev_out is not None and next(prev_out, "end") == "end":
                prev_out = None
        sc_ = a_scan(t)
        while sc_ is not None or prev_out is not None:
            if sc_ is not None and next(sc_, "end") == "end":
                sc_ = None
            if prev_out is not None and next(prev_out, "end") == "end":
                prev_out = None
            ftick(t + 2)
        prev_out = a_out(t) if t >= QLO else None
    if prev_out is not None:
        for _ in prev_out:
            ftick(NT)
    while fs["act"]:
        ftick(NT)
    T.barrier()
    pa.close()
    pa2 = ExitStack()
    kcT = mk(pa2, "kcT", [128, S], BF16)
    vcT = mk(pa2, "vcT", [128, S], BF16)
    D([(kcT[:], kcs_d)], r=[kvs_db], w=[kcT])
    D([(vcT[:], vcs_d)], r=[kvs_db], w=[vcT])
    w1 = [mk(pa2, "w1k", [128, 32, 256], BF16), mk(pa2, "w1v", [128, 32, 256], BF16)]
    w2 = [mk(pa2, "w2k", [128, 2, 64], BF16), mk(pa2, "w2v", [128, 2, 64], BF16)]
    pe2f = mk(pa2, "pe2f", [32, 256])
    pe2b = mk(pa2, "pe2b", [32, 256], BF16)
    peT = mk(pa2, "peT", [128, 2, 32], BF16)
    bias4 = mk(pa2, "bias4", [128, 4])
    gel = [[mk(pa2, "gel%d%d" % (kv, g), [128, 2, NCPAD], BF16) for g in range(2)] for kv in range(2)]
    with ExitStack() as stg:
        stage = [[mk(stg, "stgB%d" % i, [128, 2048]) for i in range(2)], 0]
        for kv, wd in enumerate((w1k_d, w1v_d)):
            wv = wd.rearrange("(l d) n -> d l n", d=64)
            for lq in range(4):
                st = stage[0][stage[1] % 2]
                stage[1] += 1
                sv = st[:, :].rearrange("p (a n) -> p a n", a=8)
                D([(sv[0:64], wv[:, lq * 8:(lq + 1) * 8, :]), (sv[64:128], wv[:, lq * 8:(lq + 1) * 8, :])], w=[st])
                V(lambda: nc.vector.tensor_copy(out=w1[kv][:, lq * 8:(lq + 1) * 8, :], in_=sv), [st], [w1[kv]])
        for kv, wd in enumerate((w2k_d, w2v_d)):
            load_w(stage, w2[kv], w2[kv][:], wd.rearrange("(c p) n -> p c n", p=128), [128, 2, 64])
        T.barrier()
    D([(pe2f[:], pe2_d)], w=[pe2f])
    V(lambda: nc.vector.tensor_copy(out=pe2b[:], in_=pe2f[:]), [pe2f], [pe2b])
    for kv in range(2):
        for g in range(2):
            G(lambda: nc.gpsimd.memset(gel[kv][g][:], 0.0), [], [gel[kv][g]])
    ppe = bfv(PS[4], 128)
    for kv in range(2):
        P(lambda: nc.tensor.transpose(out=ppe[:, kv * 32:(kv + 1) * 32], in_=pe2b[:, kv * 128:(kv + 1) * 128], identity=identb[0:32, 0:32]),
          [pe2b, identb], [PS[4]])
    V(lambda: nc.vector.tensor_copy(out=peT[:].rearrange("p a l -> p (a l)"), in_=ppe[:, 0:64]), [PS[4]], [peT])
    for kv in range(2):
        for hc in range(2):
            i4 = kv * 2 + hc
            for l in range(32):
                P(lambda: nc.tensor.matmul(out=PS[5][:, i4:i4 + 1], lhsT=w1[kv][0:64, l, hc * 128:(hc + 1) * 128], rhs=peT[0:64, kv, l:l + 1],
                                           start=(l == 0), stop=(l == 31)), [w1[kv], peT], [PS[5]], sig=(l == 31))
    V(lambda: nc.vector.tensor_copy(out=bias4[:], in_=PS[5][:, 0:4]), [PS[5]], [bias4])
    bi = 0
    for kv, src in enumerate((kcT, vcT)):
        srcv = src[:, :].rearrange("p (n s) -> p n s", s=16)
        for hc in range(2):
            pss = (PS[(2 * bi) % 4], PS[(2 * bi + 1) % 4])
            bi += 1
            for l in range(32):
                for g in range(2):
                    rows = slice(g * 64, g * 64 + 64)
                    P(lambda: nc.tensor.matmul(out=pss[g][:, 0:NCMP], lhsT=w1[kv][rows, l, hc * 128:(hc + 1) * 128],
                                               rhs=srcv[rows, l // 16:l // 16 + NCMP, l % 16], start=(l == 0), stop=(l == 31)),
                      [w1[kv], src], [pss[g]], sig=(l == 31), rg=g * 64)
            for g in range(2):
                A(lambda: nc.scalar.activation(out=gel[kv][g][:, hc, 0:NCMP], in_=pss[g][:, 0:NCMP], func=AF.Gelu,
                                               bias=bias4[:, kv * 2 + hc:kv * 2 + hc + 1]), [pss[g], bias4], [gel[kv][g]])
    for g in range(2):
        rows = slice(g * 64, g * 64 + 64)
        for hc in range(2):
            P(lambda: nc.tensor.matmul(out=PS[6][rows, 0:NCMP], lhsT=w2[0][:, hc, :], rhs=gel[0][g][:, hc, 0:NCMP],
                                       start=(hc == 0), stop=(hc == 1)), [w2[0], gel[0][g]], [PS[6]], sig=(hc == 1))
        V(lambda: nc.vector.tensor_copy(out=kcmpT[rows, 0:NCMP], in_=PS[6][rows, 0:NCMP]), [PS[6]], [kcmpT])
        for cc in range(NCC):
            for hc in range(2):
                P(lambda: nc.tensor.matmul(out=PS[7][:, (cc * 2 + g) * 64:(cc * 2 + g + 1) * 64], lhsT=gel[1][g][:, hc, cc * 128:(cc + 1) * 128],
                                           rhs=w2[1][:, hc, :], start=(hc == 0), stop=(hc == 1)), [w2[1], gel[1][g]], [PS[7]], sig=(hc == 1))
    A(lambda: nc.scalar.copy(out=vcmp[:].rearrange("p c g d -> p (c g d)"), in_=PS[7][:, 0:NCC * 128]), [PS[7]], [vcmp])
    T.barrier()
    if debug == 1:
        for nm, tb in (("d_kwT", kwT), ("d_kcT", kcT), ("d_vcT", vcT)):
            o, ob = dout(nm, [128, S], BF16)
            D([(o, tb[:])], r=[tb], w=[ob])
        o, ob = dout("d_vsA", [128, NT * 130], BF16)
        D([(o, vsA[:].rearrange("p t g d -> p (t g d)"))], r=[vsA], w=[ob])
        o, ob = dout("d_hmT", [NOWN, 128, 512], BF16)
        D([(o[i], hmT_d[i]) for i in range(NOWN)], r=hmT_db, w=[ob])
        o, ob = dout("d_kcmpT", [128, NCPAD], BF16)
        D([(o, kcmpT[:])], r=[kcmpT], w=[ob])
        o, ob = dout("d_vcmp", [128, NCC * 128], BF16)
        D([(o, vcmp[:].rearrange("p c g d -> p (c g d)"))], r=[vcmp], w=[ob])
        T.barrier()
        return nc, T, dbg
    pa2.close()
    pa0.close()

    pb = ExitStack()
    wB = mk(pb, "wB", [128, 8, 536], BF16)
    ovl = mk(pb, "ovl", [128, NCC, NSEL], BF16)
    cbm = mk(pb, "cbm", [128, 1024], BF16)
    with ExitStack() as stg:
        stage = [[mk(stg, "stgC%d" % i, [128, 2048]) for i in range(2)], 0]
        w_in_v = w_in_d.rearrange("(c p) n -> p c n", p=128)
        for (do, so, sn) in ((0, IN_OFF["nq"][0], 512), (512, IN_OFF["ng"][0], 24)):
            for o in range(0, sn, 256):
                n = min(256, sn - o)
                load_w(stage, wB, wB[:, :, do + o:do + o + n], w_in_v[:, :, so + o:so + o + n], [128, 8, n],
                       scale_ap=n1w[:, :].unsqueeze(2).broadcast_to([128, 8, n]), scale_tb=n1w)
        for o in range(0, S, 2048):
            n = min(2048, S - o)
            st = stage[0][stage[1] % 2]
            stage[1] += 1
            D([(st[64:128, 0:n], ewin_d[:, o:o + n])], w=[st])
            V(lambda: nc.vector.tensor_copy(out=KE[0][64:128, o:o + n], in_=st[64:128, 0:n]), [st], [KE[0]])
            G(lambda: nc.gpsimd.tensor_copy(out=KE[1][64:128, o:o + n], in_=st[64:128, 0:n]), [st], [KE[1]])
        load_w(stage, ovl, ovl[:].rearrange("p c j -> p (c j)"), ovl_d, [128, NCC * NSEL])
        load_w(stage, cbm, cbm[:], cbm_d, [128, 1024])
        T.barrier()
    xtb = [mk(pb, "xtb%d" % i, [128, 1024]) for i in range(2)]
    sqjb = mk(pb, "sqjb", [128, 1024], BF16)
    ssb = mk(pb, "ssb", [128, 1])
    rsb = mk(pb, "rsb", [128, 1])
    xnb = mk(pb, "xnb", [128, 1024], BF16)
    xnTb = mk(pb, "xnTb", [128, 8, 128], BF16)
    csb = [mk(pb, "csb%d" % i, [128, 64]) for i in range(2)]
    qsq = mk(pb, "qsq", [128, 512])
    ss8 = mk(pb, "ss8", [128, 8])
    rs8 = mk(pb, "rs8", [128, 8])
    qn = mk(pb, "qn", [128, 8, 64])
    qrt = [mk(pb, "qrt%d" % i, [128, 8, 32]) for i in range(4)]
    qr = mk(pb, "qr", [128, 8, 64], BF16)
    qT2 = [mk(pb, "qT%d" % i, [128, 4, 128], BF16) for i in range(2)]
    sg2 = [mk(pb, "sg%d" % i, [128, 24]) for i in range(2)]
    cmk = [mk(pb, "cmk%d" % i, [128, NCPAD], BF16) for i in range(2)]
    itab = [mk(pb, "itab%d" % i, [128, NSEL]) for i in range(2)]
    sc2 = [mk(pb, "sc%d" % i, [128, NCPAD]) for i in range(2)]
    mx2 = [mk(pb, "mx%d" % i, [128, 1]) for i in range(2)]
    se2 = [mk(pb, "se%d" % i, [128, 1]) for i in range(2)]
    pc2_ = [mk(pb, "pc%d" % i, [128, NCPAD], BF16) for i in range(2)]
    pcT2 = [mk(pb, "pcT%d" % i, [128, NCC, 128], BF16) for i in range(2)]
    ocmp2 = [mk(pb, "ocmp%d" % i, [128, 8, 64]) for i in range(2)]
    oslc = mk(pb, "oslc", [128, 8, 64])
    owin = mk(pb, "owin", [128, 8, 64])
    imp2 = mk(pb, "imp2", [128, NSEL])
    imp3 = mk(pb, "imp3", [128, NSEL])
    m8 = mk(pb, "m8", [128, 16])
    selb = mk(pb, "selb", [128, NSEL])
    negp = mk(pb, "negp", [128, NSEL])
    bqb = mk(pb, "bqb", [128, 2, 128], BF16)
    V(lambda: nc.vector.memset(bqb[:], 0.0), [], [bqb])
    NW = 2 if NSEL > 64 else 1
    QB2 = [[[mk(pb, "QB%d%d%d" % (i, g, w), [128, 4, 128], BF16) for w in range(NW)] for g in range(2)] for i in range(2)]
    pTs = [mk(pb, "pTs%d" % i, [128, 512], BF16) for i in range(4)]
    den = mk(pb, "den", [128, 4])
    osb = [mk(pb, "osb%d" % i, [128, 260]) for i in range(2)]
    onb = mk(pb, "onb", [128, 512], BF16)
    gt = [mk(pb, "gt%d" % i, [128, 8, 64]) for i in range(2)]
    onTs = mk(pb, "onTs", [128, 512], BF16)
    psT, psQ, psG, psSC, psS0, psS1, psOS, psOW = PS
    psST = (psS0, psS1, psOW)
    cnt_ = {"st": 0, "pt": 0, "os": 0}

    def b_front(oi):
        t = QLO + oi
        p_ = oi % 2
        xt, cs, qT, sg, ocmp, QB = xtb[p_], csb[p_], qT2[p_], sg2[p_], ocmp2[p_], QB2[p_]
        D([(xt[:], x_d[t * 128:(t + 1) * 128, :])], w=[xt])
        D([(cs[:, 0:32], cos_d[t * 128:(t + 1) * 128, :]), (cs[:, 32:64], sin_d[t * 128:(t + 1) * 128, :])], w=[cs])
        D([(cmk[p_][:], cmask_d[oi])], w=[cmk[p_]])
        D([(itab[p_][:], imptab_d[oi])], w=[itab[p_]])
        A(lambda: nc.scalar.activation(out=sqjb[:], in_=xt[:], func=AF.Square, accum_out=ssb[:]), [xt], [sqjb, ssb])
        yield
        rstd_pow(rsb, ssb, 1024)
        yield
        A(lambda: nc.scalar.activation(out=xnb[:], in_=xt[:], func=AF.Copy, scale=rsb[:]), [xt, rsb], [xnb])
        yield
        pv = bfv(psT, 1024)
        for c in range(8):
            P(lambda: nc.tensor.transpose(out=pv[:, c * 128:(c + 1) * 128], in_=xnb[:, c * 128:(c + 1) * 128], identity=identb[:]),
              [xnb, identb], [psT], sig=(c == 7))
        yield
        V(lambda: nc.vector.tensor_copy(out=xnTb[:].rearrange("p c t -> p (c t)"), in_=pv), [psT], [xnTb])
        yield
        for c in range(8):
            P(lambda: nc.tensor.matmul(out=psQ[:, 0:512], lhsT=xnTb[:, c, :], rhs=wB[:, c, 0:512], start=(c == 0), stop=(c == 7)),
              [xnTb, wB], [psQ], sig=(c == 7))
        for c in range(8):
            P(lambda: nc.tensor.matmul(out=psG[:, 0:24], lhsT=xnTb[:, c, :], rhs=wB[:, c, 512:536], start=(c == 0), stop=(c == 7)),
              [xnTb, wB], [psG], sig=(c == 7))
        yield
        A(lambda: nc.scalar.activation(out=sg[:], in_=psG[:, 0:24], func=AF.Exp, scale=-1.0), [psG], [sg])
        A(lambda: nc.scalar.activation(out=qsq[:], in_=psQ[:, 0:512], func=AF.Square), [psQ], [qsq])
        yield
        V(lambda: nc.vector.tensor_scalar(out=sg[:], in0=sg[:], scalar1=1.0, scalar2=None, op0=ALU.add), [sg], [sg])
        V(lambda: nc.vector.reciprocal(out=sg[:], in_=sg[:]), [sg], [sg])
        V(lambda: nc.vector.tensor_reduce(out=ss8[:], in_=qsq[:].rearrange("p (a d) -> p a d", a=8), axis=AX.X, op=ALU.add), [qsq], [ss8])
        yield
        rstd_pow(rs8, ss8, 64)
        yield
        V(lambda: nc.vector.tensor_tensor(out=qn[:], in0=psQ[:, 0:512].rearrange("p (a d) -> p a d", a=8),
                                          in1=rs8[:, :].unsqueeze(2).broadcast_to([128, 8, 64]), op=ALU.mult), [psQ, rs8], [qn])
        V(lambda: nc.vector.tensor_tensor(out=qn[:], in0=qn[:], in1=tabA[:, TA_QNW:TA_QNW + 512].rearrange("p (a d) -> p a d", a=8),
                                          op=ALU.mult), [qn, tabA], [qn])
        yield
        cosb = cs[:, 0:32].unsqueeze(1).broadcast_to([128, 8, 32])
        sinb = cs[:, 32:64].unsqueeze(1).broadcast_to([128, 8, 32])
        V(lambda: nc.vector.tensor_tensor(out=qrt[0][:], in0=qn[:, :, 0:32], in1=cosb, op=ALU.mult), [qn, cs], [qrt[0]])
        G(lambda: nc.gpsimd.tensor_tensor(out=qrt[1][:], in0=qn[:, :, 32:64], in1=sinb, op=ALU.mult), [qn, cs], [qrt[1]])
        V(lambda: nc.vector.tensor_tensor(out=qrt[2][:], in0=qn[:, :, 32:64], in1=cosb, op=ALU.mult), [qn, cs], [qrt[2]])
        G(lambda: nc.gpsimd.tensor_tensor(out=qrt[3][:], in0=qn[:, :, 0:32], in1=sinb, op=ALU.mult), [qn, cs], [qrt[3]])
        yield
        qro = qr[:, :, :].rearrange("p (h g) d -> p g h d", g=2)
        v4 = lambda tb_: tb_[:, :, :].rearrange("p (g h) d -> p g h d", g=2)
        V(lambda: nc.vector.tensor_tensor(out=qro[:, :, :, 0:32], in0=v4(qrt[0]), in1=v4(qrt[1]), op=ALU.subtract), [qrt[0], qrt[1]], [qr])
        G(lambda: nc.gpsimd.tensor_tensor(out=qro[:, :, :, 32:64], in0=v4(qrt[2]), in1=v4(qrt[3]), op=ALU.add), [qrt[2], qrt[3]], [qr])
        yield
        pq = bfv(psT, 512)
        for hg in range(4):
            P(lambda: nc.tensor.transpose(out=pq[:, hg * 128:(hg + 1) * 128], in_=qr[:, 2 * hg:2 * hg + 2, :].rearrange("p a d -> p (a d)"),
                                          identity=identb[:]), [qr, identb], [psT], sig=(hg == 3))
        pq1 = bfv(psSC, 512)
        for hg in range(4):
            P(lambda: nc.tensor.transpose(out=pq1[0:64, hg * 128:(hg + 1) * 128], in_=qr[:, 2 * hg + 1, :], identity=identb[:]),
              [qr, identb], [psSC], sig=(hg == 3))
        yield
        V(lambda: nc.vector.tensor_copy(out=qT[:].rearrange("p h t -> p (h t)"), in_=pq), [psT], [qT])
        for w in range(NW):
            V(lambda: nc.vector.tensor_copy(out=QB[1][w][0:64].rearrange("p h t -> p (h t)"), in_=pq1[0:64, :]), [psSC], [QB[1][w]])
        yield
        for w in range(NW):
            G(lambda: nc.gpsimd.tensor_copy(out=QB[0][w][0:64], in_=qT[0:64]), [qT], [QB[0][w]])
        psOC = psQ
        cm = cmk[p_]

        def cmp_head(head, par):
            g, hg = head // 4, head % 4
            rows = slice(g * 64, g * 64 + 64)
            psc = (psSC, psT)[par]
            sc, mx, se, pc, pcT = sc2[par], mx2[par], se2[par], pc2_[par], pcT2[par]
            P(lambda: nc.tensor.matmul(out=psc[:, 0:NCPAD], lhsT=qT[rows, hg, :], rhs=kcmpT[rows, 0:NCPAD], start=True, stop=False),
              [qT, kcmpT], [psc], sig=False, rg=g * 64)
            P(lambda: nc.tensor.matmul(out=psc[:, 0:NCPAD], lhsT=identb[:], rhs=cm[:], start=False, stop=True), [identb, cm], [psc])
            yield
            V(lambda: nc.vector.reduce_max(out=mx[:], in_=psc[:, 0:NCPAD], axis=AX.X), [psc], [mx])
            V(lambda: nc.vector.tensor_scalar(out=mx[:], in0=mx[:], scalar1=-0.125, scalar2=None, op0=ALU.mult), [mx], [mx])
            yield
            A(lambda: nc.scalar.activation(out=sc[:], in_=psc[:, 0:NCPAD], func=AF.Exp, scale=0.125, bias=mx[:], accum_out=se[:]), [psc, mx], [sc, se])
            yield
            V(lambda: nc.vector.reciprocal(out=se[:], in_=se[:]), [se], [se])
            V(lambda: nc.vector.tensor_scalar(out=pc[:], in0=sc[:], scalar1=se[:], scalar2=rvt[:, oi:oi + 1], op0=ALU.mult, op1=ALU.mult),
              [sc, se, rvt], [pc])
            yield
            pp = bfv(psc, NCC * 128)
            for cc in range(NCC):
                P(lambda: nc.tensor.transpose(out=pp[:, cc * 128:(cc + 1) * 128], in_=pc[:, cc * 128:(cc + 1) * 128], identity=identb[:]),
                  [pc, identb], [psc], sig=(cc == NCC - 1))
            yield
            V(lambda: nc.vector.tensor_copy(out=pcT[:].rearrange("p c t -> p (c t)"), in_=pp), [psc], [pcT])
            yield
            for cc in range(NCC):
                P(lambda: nc.tensor.matmul(out=psOC[:, head * 64:(head + 1) * 64], lhsT=pcT[:, cc, :], rhs=vcmp[:, cc, g, :],
                                           start=(cc == 0), stop=(cc == NCC - 1)), [pcT, vcmp], [psOC], sig=(cc == NCC - 1))
            for cc in range(NCC):
                first = (hg == 0 and cc == 0)
                last = (hg == 3 and cc == NCC - 1)
                P(lambda: nc.tensor.matmul(out=psG[:, 256 + g * NSEL:256 + (g + 1) * NSEL], lhsT=pcT[:, cc, :], rhs=ovl[:, cc, :],
                                           start=first, stop=last, skip_group_check=True), [pcT, ovl], [psG], sig=(cc == NCC - 1))
            yield

        for pair in range(4):
            ga = cmp_head(2 * pair, 0)
            gb = cmp_head(2 * pair + 1, 1)
            for _ in ga:
                next(gb, None)
                yield
        V(lambda: nc.vector.tensor_copy(out=ocmp[:].rearrange("p h d -> p (h d)"), in_=psOC[:, 0:512]), [psOC], [ocmp])
        yield
        for g in range(2):
            V(lambda: nc.vector.tensor_tensor(out=imp2[:], in0=psG[:, 256 + g * NSEL:256 + (g + 1) * NSEL], in1=itab[p_][:], op=ALU.add),
              [psG, itab[p_]], [imp2])
            V(lambda: nc.vector.max(out=m8[:, 0:8], in_=imp2[:]), [imp2], [m8])
            yield
            V(lambda: nc.vector.match_replace(out=imp3[:], in_to_replace=m8[:, 0:8], in_values=imp2[:], imm_value=-3.0e38), [imp2, m8], [imp3])
            V(lambda: nc.vector.max(out=m8[:, 8:16], in_=imp3[:]), [imp3], [m8])
            yield
            V(lambda: nc.vector.tensor_scalar(out=selb[:], in0=imp2[:], scalar1=m8[:, 15:16], scalar2=None, op0=ALU.is_ge), [imp2, m8], [selb])
            V(lambda: nc.vector.tensor_scalar(out=selb[:], in0=selb[:], scalar1=-1.0, scalar2=-NEGB, op0=ALU.add, op1=ALU.mult), [selb], [selb])
            yield
            V(lambda: nc.vector.tensor_scalar(out=negp[:], in0=itab[p_][:], scalar1=0.0, scalar2=NEGB, op0=ALU.min, op1=ALU.max),
              [itab[p_]], [negp])
            V(lambda: nc.vector.tensor_tensor(out=bqb[:, 0, 0:NSEL], in0=selb[:], in1=negp[:], op=ALU.add), [selb, negp], [bqb])
            yield
            nlo = min(64, NSEL)
            V(lambda: nc.vector.tensor_copy(out=bqb[:, 1, 64:64 + nlo], in_=bqb[:, 0, 0:nlo]), [bqb], [bqb])
            if NSEL > 64:
                V(lambda: nc.vector.tensor_copy(out=bqb[:, 1, 0:NSEL - 64], in_=bqb[:, 0, 64:NSEL]), [bqb], [bqb])
            yield
            pb_ = bfv(psT, 256)
            for w in range(NW):
                P(lambda: nc.tensor.transpose(out=pb_[:, w * 128:(w + 1) * 128], in_=bqb[:, 1 - w, :], identity=identb[:]), [bqb, identb], [psT],
                  sig=(w == NW - 1))
            yield
            for w in range(NW):
                V(lambda: nc.vector.tensor_copy(out=QB[g][w][64:128], in_=pb_[64:128, w * 128:(w + 1) * 128].unsqueeze(1).broadcast_to([64, 4, 128])),
                  [psT], [QB[g][w]])
            yield

    def b_loops(oi):
        t = QLO + oi
        p_ = oi % 2
        qT, sg, ocmp, QB = qT2[p_], sg2[p_], ocmp2[p_], QB2[p_]
        qTf = qT[:, :, :].rearrange("p h t -> p (h t)")
        k0 = max(0, t - 4)
        its = []
        for g in range(2):
            its += [("s", g, kt) for kt in range(t + 1)]
            its += [("w", g, kt) for kt in range(k0, t + 1)]
        slots = {}

        def front(i):
            kind, g, kt = its[i]
            rows = slice(g * 64, g * 64 + 64)
            ps = psST[cnt_["st"] % 3]
            cnt_["st"] += 1
            pT = pTs[cnt_["pt"] % 4]
            cnt_["pt"] += 1
            slots[i] = (ps, pT)
            ksl = slice(kt * 128, (kt + 1) * 128)
            if kind == "s":
                qb = QB[g][kt // 32]
                P(lambda: nc.tensor.matmul(out=ps[:, :], lhsT=KE[g][:, ksl], rhs=qb[:, :, :].rearrange("p h t -> p (h t)"), start=True, stop=(kt != t)),
                  [KE[g], qb], [ps], sig=(kt != t))
                if kt == t:
                    P(lambda: nc.tensor.matmul(out=ps[:, :], lhsT=identb[:], rhs=cbm[:, 0:512], start=False, stop=True), [identb, cbm], [ps])
            else:
                edge = (kt == t) or (kt == t - 4)
                P(lambda: nc.tensor.matmul(out=ps[:, :], lhsT=kwT[rows, ksl], rhs=qTf[rows, :], start=True, stop=(not edge)), [kwT, qT], [ps],
                  sig=(not edge), rg=g * 64)
                if kt == t:
                    P(lambda: nc.tensor.matmul(out=ps[:, :], lhsT=identb[:], rhs=cbm[:, 0:512], start=False, stop=True), [identb, cbm], [ps])
                elif kt == t - 4:
                    P(lambda: nc.tensor.matmul(out=ps[:, :], lhsT=identb[:], rhs=cbm[:, 512:1024], start=False, stop=True), [identb, cbm], [ps])

        def back(i):
            kind, g, kt = its[i]
            ps, pT = slots.pop(i)
            if kind == "s":
                A(lambda: nc.scalar.activation(out=pT[:], in_=ps[:, :], func=AF.Exp, scale=0.125), [ps], [pT])
                pso, vst, first, last = psOS, vsA, (kt == 0), (kt == t)
            else:
                A(lambda: nc.scalar.activation(out=pT[:], in_=ps[:, :], func=AF.Exp, scale=0.125, bias=kbias[:, kt:kt + 1]), [ps, kbias], [pT])
                pso, vst, first, last = psOS, vwA, (kt == k0), (kt == t)
            for hg in range(4):
                P(lambda: nc.tensor.matmul(out=pso[:, hg * 65:(hg + 1) * 65], lhsT=pT[:, hg * 128:(hg + 1) * 128], rhs=vst[:, kt, g, :],
                                           start=(first and hg == 0), stop=last, skip_group_check=True), [pT, vst], [pso], sig=(hg == 3 and last))
            if last:
                ob = osb[cnt_["os"] % 2]
                cnt_["os"] += 1
                A(lambda: nc.scalar.copy(out=ob[:], in_=pso[:, 0:260]), [pso], [ob])
                ov_ = ob[:, :].rearrange("p (h e) -> p h e", h=4)
                dst = oslc if kind == "s" else owin
                V(lambda: nc.vector.tensor_scalar(out=den[:], in0=ov_[:, :, 64], scalar1=1e-30, scalar2=None, op0=ALU.max), [ob], [den])
                V(lambda: nc.vector.reciprocal(out=den[:], in_=den[:]), [den], [den])
                V(lambda: nc.vector.tensor_tensor(out=dst[:, g * 4:(g + 1) * 4, :], in0=ov_[:, :, 0:64],
                                                  in1=den[:, :].unsqueeze(2).broadcast_to([128, 4, 64]), op=ALU.mult), [ob, den], [dst])

        DEPTH = 2
        for i in range(min(DEPTH, len(its))):
            front(i)
        for i in range(len(its)):
            if i + DEPTH < len(its):
                front(i + DEPTH)
            back(i)
            yield
        sgv = sg[:, :].rearrange("p (h k) -> p h k", k=3)
        V(lambda: nc.vector.tensor_tensor(out=gt[0][:], in0=ocmp[:], in1=sgv[:, :, 0:1].broadcast_to([128, 8, 64]), op=ALU.mult), [ocmp, sg], [gt[0]])
        G(lambda: nc.gpsimd.tensor_tensor(out=gt[1][:], in0=oslc[:], in1=sgv[:, :, 1:2].broadcast_to([128, 8, 64]), op=ALU.mult), [oslc, sg], [gt[1]])
        V(lambda: nc.vector.tensor_tensor(out=gt[0][:], in0=gt[0][:], in1=gt[1][:], op=ALU.add), [gt[0], gt[1]], [gt[0]])
        G(lambda: nc.gpsimd.tensor_tensor(out=gt[1][:], in0=owin[:], in1=sgv[:, :, 2:3].broadcast_to([128, 8, 64]), op=ALU.mult), [owin, sg], [gt[1]])
        V(lambda: nc.vector.tensor_tensor(out=onb[:], in0=gt[0][:].rearrange("p h d -> p (h d)"), in1=gt[1][:].rearrange("p h d -> p (h d)"),
                                          op=ALU.add), [gt[0], gt[1]], [onb])
        yield
        po = bfv(psS0, 512)
        for c in range(4):
            P(lambda: nc.tensor.transpose(out=po[:, c * 128:(c + 1) * 128], in_=onb[:, c * 128:(c + 1) * 128], identity=identb[:]),
              [onb, identb], [psS0], sig=(c == 3))
        yield
        V(lambda: nc.vector.tensor_copy(out=onTs[:], in_=po), [psS0], [onTs])
        D([(onT_d[oi], onTs[:])], r=[onTs], w=[onT_db[oi]])
        yield

    for _ in b_front(0):
        pass
    for oi in range(NOWN):
        lp = b_loops(oi)
        nf = b_front(oi + 1) if oi + 1 < NOWN else None
        L_ = 2 * (QLO + oi) + 12 + 3
        F_ = 56
        for i_, _ in enumerate(lp):
            nadv = ((i_ + 1) * F_) // L_ - (i_ * F_) // L_
            for _k in range(nadv):
                if nf is not None:
                    try:
                        next(nf)
                    except StopIteration:
                        nf = None
        if nf is not None:
            for _ in nf:
                pass
    T.barrier()
    if debug == 2:
        o, ob = dout("d_onT", [NOWN, 128, 512], BF16)
        D([(o[i], onT_d[i]) for i in range(NOWN)], r=onT_db, w=[ob])
        T.barrier()
        return nc, T, dbg
    pb.close()
    kvs.close()

    pc1 = ExitStack()
    wC = mk(pc1, "wC", [128, 8, 2048], BF16)
    wum = mk(pc1, "wum", [128, 4, 1024], BF16)
    wun = mk(pc1, "wun", [128, 4, 1024], BF16)
    wo = mk(pc1, "wo", [128, 8, 1024], BF16)
    mgbf = mk(pc1, "mgbf", [1, 2048])
    mgbb = mk(pc1, "mgbb", [1, 2048], BF16)
    D([(mgbf[:], mgb_d)], w=[mgbf])
    V(lambda: nc.vector.tensor_copy(out=mgbb[:], in_=mgbf[:]), [mgbf], [mgbb])
    with ExitStack() as stg:
        stage = [[mk(stg, "stgD%d" % i, [128, 2048]) for i in range(2)], 0]
        w_in_v = w_in_d.rearrange("(c p) n -> p c n", p=128)
        so = IN_OFF["gm"][0]
        for o in range(0, 2048, 256):
            load_w(stage, wC, wC[:, :, o:o + 256], w_in_v[:, :, so + o:so + o + 256], [128, 8, 256],
                   scale_ap=n1w[:, :].unsqueeze(2).broadcast_to([128, 8, 256]), scale_tb=n1w)
        for (wt, wd, kc_) in ((wum, wupm_d, 4), (wun, wupn_d, 4), (wo, wout_d, 8)):
            wv = wd.rearrange("(c p) n -> p c n", p=128)
            for o in range(0, 1024, 2048 // kc_):
                n = 2048 // kc_
                load_w(stage, wt, wt[:, :, o:o + n], wv[:, :, o:o + n], [128, kc_, n])
        T.barrier()
    xtc = [mk(pc1, "xtc%d" % i, [128, 1024]) for i in range(2)]
    sqjc = mk(pc1, "sqjc", [128, 1024], BF16)
    ssc = [mk(pc1, "ssc%d" % i, [128, 1]) for i in range(2)]
    rsc = [mk(pc1, "rsc%d" % i, [128, 1]) for i in range(2)]
    xnc = [mk(pc1, "xnc%d" % i, [128, 1024], BF16) for i in range(2)]
    xnTc = [mk(pc1, "xnTc%d" % i, [128, 8, 128], BF16) for i in range(2)]
    hmTl = [mk(pc1, "hmTl%d" % i, [128, 4, 128], BF16) for i in range(2)]
    onTl = [mk(pc1, "onTl%d" % i, [128, 4, 128], BF16) for i in range(2)]
    sgm = [mk(pc1, "sgm%d" % i, [128, 2048]) for i in range(2)]
    y1 = [mk(pc1, "y1_%d" % i, [128, 1024]) for i in range(2)]
    y2 = [mk(pc1, "y2_%d" % i, [128, 1024]) for i in range(2)]
    yb = [mk(pc1, "yb%d" % i, [128, 1024], BF16) for i in range(2)]
    yT = [mk(pc1, "yT%d" % i, [128, 8, 128], BF16) for i in range(2)]
    xmo = [mk(pc1, "xmo%d" % i, [128, 1024]) for i in range(2)]
    rot = {"g": 0, "u": 0, "o": 0}

    def c1_tile(oi):
        t = QLO + oi
        p_ = oi % 2
        xt = xtc[p_]
        psTp = (PS[0], PS[7])[p_]
        D([(xt[:], x_d[t * 128:(t + 1) * 128, :])], w=[xt])
        D([(hmTl[p_][:].rearrange("p c t -> p (c t)"), hmT_d[oi])], r=[hmT_db[oi]], w=[hmTl[p_]])
        D([(onTl[p_][:].rearrange("p c t -> p (c t)"), onT_d[oi])], r=[onT_db[oi]], w=[onTl[p_]])
        A(lambda: nc.scalar.activation(out=sqjc[:], in_=xt[:], func=AF.Square, accum_out=ssc[p_][:]), [xt], [sqjc, ssc[p_]])
        rstd_pow(rsc[p_], ssc[p_], 1024)
        yield
        A(lambda: nc.scalar.activation(out=xnc[p_][:], in_=xt[:], func=AF.Copy, scale=rsc[p_][:]), [xt, rsc[p_]], [xnc[p_]])
        yield
        pv = bfv(psTp, 1024)
        for c in range(8):
            P(lambda: nc.tensor.transpose(out=pv[:, c * 128:(c + 1) * 128], in_=xnc[p_][:, c * 128:(c + 1) * 128], identity=identb[:]),
              [xnc[p_], identb], [psTp], sig=(c == 7))
        yield
        V(lambda: nc.vector.tensor_copy(out=xnTc[p_][:].rearrange("p c t -> p (c t)"), in_=pv), [psTp], [xnTc[p_]])
        yield
        for nb in range(4):
            ps = PS[1 + rot["g"] % 2]
            rot["g"] += 1
            for c in range(8):
                P(lambda: nc.tensor.matmul(out=ps[:, :], lhsT=xnTc[p_][:, c, :], rhs=wC[:, c, nb * 512:(nb + 1) * 512], start=(c == 0), stop=False),
                  [xnTc[p_], wC], [ps], sig=False)
            P(lambda: nc.tensor.matmul(out=ps[:, :], lhsT=onesb[0:1, :], rhs=mgbb[0:1, nb * 512:(nb + 1) * 512], start=False, stop=True),
              [onesb, mgbb], [ps])
            A(lambda: nc.scalar.activation(out=sgm[p_][:, nb * 512:(nb + 1) * 512], in_=ps[:, :], func=AF.Sigmoid), [ps], [sgm[p_]])
            if nb % 2 == 1:
                yield
        for br, (src, wt) in enumerate(((hmTl[p_], wum), (onTl[p_], wun))):
            for nb in range(2):
                ps = PS[3 + rot["u"] % 2]
                rot["u"] += 1
                for c in range(4):
                    P(lambda: nc.tensor.matmul(out=ps[:, :], lhsT=src[:, c, :], rhs=wt[:, c, nb * 512:(nb + 1) * 512], start=(c == 0), stop=(c == 3)),
                      [src, wt], [ps], sig=(c == 3))
                yy = y1[p_] if br == 0 else y2[p_]
                V(lambda: nc.vector.tensor_tensor(out=yy[:, nb * 512:(nb + 1) * 512], in0=ps[:, :],
                                                  in1=sgm[p_][:, br * 1024 + nb * 512:br * 1024 + (nb + 1) * 512], op=ALU.mult), [ps, sgm[p_]], [yy])
            yield
        G(lambda: nc.gpsimd.tensor_tensor(out=yb[p_][:], in0=y1[p_][:], in1=y2[p_][:], op=ALU.add), [y1[p_], y2[p_]], [yb[p_]])
        yield
        py = bfv(psTp, 1024)
        for c in range(8):
            P(lambda: nc.tensor.transpose(out=py[:, c * 128:(c + 1) * 128], in_=yb[p_][:, c * 128:(c + 1) * 128], identity=identb[:]),
              [yb[p_], identb], [psTp], sig=(c == 7))
        yield
        A(lambda: nc.scalar.copy(out=yT[p_][:].rearrange("p c t -> p (c t)"), in_=py), [psTp], [yT[p_]])
        yield
        xm = xmo[p_]
        for nb in range(2):
            ps = PS[5 + rot["o"] % 2]
            rot["o"] += 1
            for c in range(8):
                P(lambda: nc.tensor.matmul(out=ps[:, :], lhsT=yT[p_][:, c, :], rhs=wo[:, c, nb * 512:(nb + 1) * 512], start=(c == 0), stop=(c == 7)),
                  [yT[p_], wo], [ps], sig=(c == 7))
            V(lambda: nc.vector.tensor_tensor(out=xm[:, nb * 512:(nb + 1) * 512], in0=ps[:, :], in1=xt[:, nb * 512:(nb + 1) * 512], op=ALU.add),
              [ps, xt], [xm])
        D([(out_d[oi], xm[:])], r=[xm], w=[xmid_db[oi]])
        yield

    pipeline(range(NOWN), c1_tile, 6)
    T.barrier()
    if debug == 3:
        T.barrier()
        return nc, T, dbg
    pc1.close()

    pc2 = ExitStack()
    fup = mk(pc2, "fup", [128, 8, 2 * D_FF], BF16)
    fdn = mk(pc2, "fdn", [128, 22, 1024], BF16)
    fcw = mk(pc2, "fcw", [128, 22, 3])
    fcb = mk(pc2, "fcb", [128, 22])
    D([(fcw[:].rearrange("p a b -> p (a b)"), fcw_d)], w=[fcw])
    D([(fcb[:], fcb_d)], w=[fcb])
    with ExitStack() as stg:
        stage = [[mk(stg, "stgE%d" % i, [128, 2048]) for i in range(2)], 0]
        fup_v = fup_d.rearrange("(c p) n -> p c n", p=128)
        for o in range(0, 2 * D_FF, 256):
            load_w(stage, fup, fup[:, :, o:o + 256], fup_v[:, :, o:o + 256], [128, 8, 256],
                   scale_ap=n2w[:, :].unsqueeze(2).broadcast_to([128, 8, 256]), scale_tb=n2w)
        fdn_v = fdn_d.rearrange("(c p) n -> p c n", p=128)
        for c0 in range(0, 22, 2):
            load_w(stage, fdn, fdn[:, c0:c0 + 2, :], fdn_v[:, c0:c0 + 2, :], [128, 2, 1024], eng="pool")
        T.barrier()
    xmh = mk(pc2, "xmh", [128, 1024])
    xmd = [mk(pc2, "xmd%d" % i, [128, 1024]) for i in range(2)]
    sse = mk(pc2, "sse", [128, 1])
    rse_ = mk(pc2, "rse", [128, 1])
    xne = mk(pc2, "xne", [128, 1024], BF16)
    TS = 4
    h2T2 = [mk(pc2, "h2T%d" % i, [128, 8, TS * 128], BF16) for i in range(2)]
    cstate = mk(pc2, "cstate", [128, 22, 2])
    V(lambda: nc.vector.memset(cstate[:], 0.0), [], [cstate])
    ab = [mk(pc2, "ab%d" % i, [128, 2 + TS * 128]) for i in range(2)]
    acc = [mk(pc2, "acc%d" % i, [128, TS * 128]) for i in range(2)]
    gl = [mk(pc2, "gl%d" % i, [128, TS * 128]) for i in range(2)]
    uT = mk(pc2, "uT", [128, 22, TS * 128], BF16)
    groups = []
    o0 = 0
    if RESET_TILE is not None:
        groups.append([0])
        o0 = 1
    for a_ in range(o0, NOWN, TS):
        groups.append(list(range(a_, min(NOWN, a_ + TS))))
    cnt2 = {"x": 0, "c": 0}

    def c2_head(gi):
        grp = groups[gi]
        h2T = h2T2[gi % 2]
        for j, oi in enumerate(grp):
            D([(xmh[:], out_d[oi])], r=[xmid_db[oi]], w=[xmh])
            A(lambda: nc.scalar.activation(out=xne[:], in_=xmh[:], func=AF.Square, accum_out=sse[:]), [xmh], [xne, sse])
            yield
            rstd_pow(rse_, sse, 1024)
            yield
            A(lambda: nc.scalar.activation(out=xne[:], in_=xmh[:], func=AF.Copy, scale=rse_[:]), [xmh, rse_], [xne])
            yield
            pv = bfv(psT, 1024)
            for c in range(8):
                P(lambda: nc.tensor.transpose(out=pv[:, c * 128:(c + 1) * 128], in_=xne[:, c * 128:(c + 1) * 128], identity=identb[:]),
                  [xne, identb], [psT], sig=(c == 7))
            A(lambda: nc.scalar.copy(out=h2T[:, :, j * 128:(j + 1) * 128], in_=pv.rearrange("p (c t) -> p c t", c=8)), [psT], [h2T])
            yield

    def c2_body(gi):
        grp = groups[gi]
        h2T = h2T2[gi % 2]
        nt = len(grp)
        W = nt * 128
        for cch in range(22):
            ci = cnt2["c"]
            cnt2["c"] += 1
            psa = PS[1 + ci % 2]
            psv = PS[3 + ci % 2]
            ab_ = ab[ci % 2]
            acc_ = acc[ci % 2]
            gl_ = gl[ci % 2]
            for c in range(8):
                P(lambda: nc.tensor.matmul(out=psa[:, 0:W], lhsT=fup[:, c, cch * 128:(cch + 1) * 128], rhs=h2T[:, c, 0:W],
                                           start=(c == 0), stop=(c == 7)), [fup, h2T], [psa], sig=(c == 7))
            for c in range(8):
                P(lambda: nc.tensor.matmul(out=psv[:, 0:W], lhsT=fup[:, c, D_FF + cch * 128:D_FF + (cch + 1) * 128], rhs=h2T[:, c, 0:W],
                                           start=(c == 0), stop=(c == 7)), [fup, h2T], [psv], sig=(c == 7))
            G(lambda: nc.gpsimd.tensor_copy(out=ab_[:, 0:2], in_=cstate[:, cch, :]), [cstate], [ab_])
            A(lambda: nc.scalar.copy(out=ab_[:, 2:2 + W], in_=psa[:, 0:W]), [psa], [ab_])
            V(lambda: nc.vector.tensor_scalar(out=acc_[:, 0:W], in0=ab_[:, 0:W], scalar1=fcw[:, cch, 0:1], scalar2=None, op0=ALU.mult),
              [ab_, fcw], [acc_])
            for k in (1, 2):
                V(lambda: nc.vector.scalar_tensor_tensor(out=acc_[:, 0:W], in0=ab_[:, k:k + W], scalar=fcw[:, cch, k:k + 1],
                                                         in1=acc_[:, 0:W], op0=ALU.mult, op1=ALU.add), [ab_, fcw, acc_], [acc_])
            G(lambda: nc.gpsimd.tensor_copy(out=cstate[:, cch, :], in_=ab_[:, W:W + 2]), [ab_], [cstate])
            A(lambda: nc.scalar.activation(out=gl_[:, 0:W], in_=acc_[:, 0:W], func=AF.Gelu, bias=fcb[:, cch:cch + 1]), [acc_, fcb], [gl_])
            V(lambda: nc.vector.tensor_tensor(out=uT[:, cch, 0:W], in0=psv[:, 0:W], in1=gl_[:, 0:W], op=ALU.mult), [psv, gl_], [uT])
            yield
        if RESET_TILE is not None and grp == [0]:
            G(lambda: nc.gpsimd.tensor_scalar(out=cstate[:], in0=cstate[:], scalar1=flag, scalar2=None, op0=ALU.mult), [cstate, cst], [cstate])
        for j, oi in enumerate(grp):
            xm = xmd[cnt2["x"] % 2]
            cnt2["x"] += 1
            D([(xm[:], out_d[oi])], r=[xmid_db[oi]], w=[xm])
            for nb in range(2):
                ps = PS[5 + nb]
                for cch in range(22):
                    P(lambda: nc.tensor.matmul(out=ps[:, :], lhsT=uT[:, cch, j * 128:(j + 1) * 128], rhs=fdn[:, cch, nb * 512:(nb + 1) * 512],
                                               start=(cch == 0), stop=(cch == 21)), [uT, fdn], [ps], sig=(cch == 21))
                V(lambda: nc.vector.tensor_tensor(out=xm[:, nb * 512:(nb + 1) * 512], in0=ps[:, :], in1=xm[:, nb * 512:(nb + 1) * 512], op=ALU.add),
                  [ps, xm], [xm])
            D([(out_d[oi], xm[:])], r=[xm], w=[xmid_db[oi]])
            yield

    for _ in c2_head(0):
        pass
    for gi in range(len(groups)):
        nh = c2_head(gi + 1) if gi + 1 < len(groups) else None
        for _ in c2_body(gi):
            if nh is not None and next(nh, "end") == "end":
                nh = None
        if nh is not None:
            for _ in nh:
                pass
    T.barrier()
    pc2.close()
    return nc, T, dbg
    return nc, T, dbg


def prep_core(inp, b, h, NT, QLO, padded):
    S = NT * 128
    f32 = np.float32
    g = lambda k: np.asarray(inp[k], dtype=f32)[0]
    xb = np.asarray(inp["x"], dtype=f32)[b]
    if padded:
        half = S // 2
        if h == 0:
            x = np.concatenate([np.zeros((half, 1024), f32), xb[0:half]], 0)
            pos = np.arange(S) - half
        else:
            x = xb[0:S]
            pos = np.arange(S)
    else:
        x = xb[0:S]
        pos = np.arange(S)
    padlen = int((pos < 0).sum())
    NOWN = NT - QLO
    NCMP = 8 * NT - 1
    NCC = (8 * NT + 127) // 128
    NCPAD = NCC * 128
    NSEL = 2 * NT
    d = {"x": np.ascontiguousarray(x)}
    posc = np.maximum(pos, 0).astype(f32)
    inv = (f32(10000.0) ** (-np.arange(0, 64, 2, dtype=f32) / f32(64))).astype(f32)
    ang = (posc[:, None] * inv[None, :]).astype(f32)
    d["cos"] = np.cos(ang).astype(f32)
    d["sin"] = np.sin(ang).astype(f32)
    d["w_in"] = g("w_in")
    d["n1w"] = np.ascontiguousarray(g("norm1_w").reshape(8, 128).T)
    d["n2w"] = np.ascontiguousarray(g("norm2_w").reshape(8, 128).T)
    tabA = np.zeros((TA_N,), f32)
    tabA[TA_KNW:TA_KNW + 384] = np.concatenate([np.tile(g("kcmp_norm_w"), 2), np.tile(g("kslc_norm_w"), 2), np.tile(g("kwin_norm_w"), 2)])
    tabA[TA_QNW:TA_QNW + 512] = np.tile(g("q_norm_w"), 8)
    tabA[TA_BIG:TA_BIG + 4] = g("m_igate_b")
    tabA[TA_BFG:TA_BFG + 4] = g("m_fgate_b")
    tabA[TA_MONW:TA_MONW + 512] = g("m_out_norm_w").reshape(-1)
    d["tabA"] = np.ascontiguousarray(np.tile(tabA[None, :], (128, 1)))
    mcw = g("m_conv_w")
    mcb = g("m_conv_b")
    ch_of = [256 + np.arange(128), 384 + np.arange(128), np.arange(128), 128 + np.arange(128)]
    cw = np.zeros((128, 4, 4), f32)
    cb = np.zeros((128, 4), f32)
    for c in range(4):
        cw[:, c, :] = mcw[:, ch_of[c]].T
        cb[:, c] = mcb[ch_of[c]]
    d["cw"] = cw.reshape(128, 16)
    d["cb"] = cb
    fw = g("ffn_conv_w")
    d["fcw"] = np.ascontiguousarray(fw.reshape(3, 22, 128).transpose(2, 1, 0).reshape(128, 66))
    d["fcb"] = np.ascontiguousarray(g("ffn_conv_b").reshape(22, 128).T)
    d["mgb"] = g("merge_gate_b").reshape(1, 2048)
    kpe, vpe = g("cmp_k_pe"), g("cmp_v_pe")
    d["pe2"] = np.ascontiguousarray(np.concatenate([kpe, kpe, vpe, vpe], 1))
    p = np.arange(128)
    cst = np.zeros((128, 128 * 3 + 2 + 64 + 1), f32)
    cst[:, 0:128] = np.eye(128)
    cst[:, 128:256] = ((p[:, None] // 64 == p[None, :] // 64) & (p[:, None] <= p[None, :]))
    cst[:, 256:384] = 1.0
    cst[:, 384] = p < 64
    cst[:, 385] = p >= 64
    cst[:, 386:450] = (p[:, None] % 64) <= np.arange(64)[None, :]
    cst[:, 450] = 0.0 if (padded and h == 0) else 1.0
    d["cst"] = cst
    n = np.arange(NCPAD)
    j = np.arange(NSEL)
    ov = ((n[:, None] * 16 <= j[None, :] * 64 + 63) & (n[:, None] * 16 + 31 >= j[None, :] * 64) & (n[:, None] < NCMP)).astype(f32)
    d["ovl"] = np.ascontiguousarray(ov.reshape(NCC, 128, NSEL).transpose(1, 0, 2).reshape(128, NCC * NSEL))
    kk = np.arange(S)
    d["ewin"] = ((kk[None, :] // 64) == (64 * (kk[None, :] // 4096) + np.arange(64)[:, None])).astype(f32)
    diag = np.where(p[:, None] > p[None, :], NEGB, 0.0).astype(f32)
    anti = np.where(p[:, None] <= p[None, :], NEGB, 0.0).astype(f32)
    d["cbm"] = np.ascontiguousarray(np.concatenate([np.tile(diag, (1, 4)), np.tile(anti, (1, 4))], 1))
    tpos = pos[QLO * 128:].reshape(NOWN, 128)
    cend_real = 16 * n + 31 - padlen
    cvalid = (16 * n >= padlen) & (n < NCMP)
    d["cmask"] = np.where(cvalid[None, None, :] & (cend_real[None, None, :] <= tpos[:, :, None]), 0.0, NEGB * 8).astype(ml_dtypes.bfloat16)
    jr = j - padlen // 64
    cur = tpos // 64
    forced = (jr[None, None, :] == 0) | (jr[None, None, :] == cur[:, :, None]) | (jr[None, None, :] == cur[:, :, None] - 1)
    bad = (jr[None, None, :] > cur[:, :, None]) | (jr[None, None, :] < 0)
    d["imptab"] = np.where(bad, -1e30, np.where(forced, 1e4, 0.0)).astype(f32)
    d["rv"] = np.ascontiguousarray((tpos >= 31).astype(f32).T)
    d["kb"] = np.ascontiguousarray(np.where(pos.reshape(NT, 128).T < 0, NEGB, 0.0).astype(f32))
    for k in ("cmp_k_w1", "cmp_k_w2", "cmp_v_w1", "cmp_v_w2", "w_up_m", "w_up_n", "w_out", "ffn_w_up", "ffn_w_down"):
        d[k] = g(k)
    return d


NT_FULL, QLO_FULL, RESET_FULL = 64, 31, 32
_CACHE = {}


def kernel(**inputs):
    B = int(np.asarray(inputs["x"]).shape[0])
    if "nc" not in _CACHE:
        _CACHE["nc"] = build(NT_FULL, QLO_FULL, RESET_FULL)[0]
    nc = _CACHE["nc"]
    in_maps = []
    for b in range(B):
        for h in range(2):
            in_maps.append(prep_core(inputs, b, h, NT_FULL, QLO_FULL, True))
    res = run_bass_kernel_spmd(nc, in_maps, core_ids=list(range(2 * B)))
    out = np.zeros((B, NT_FULL * 128, 1024), np.float32)
    half = NT_FULL * 64
    for b in range(B):
        for h in range(2):
            o = np.asarray(res.results[b * 2 + h]["out"], dtype=np.float32).reshape(-1, 1024)
            out[b, h * half:(h + 1) * half] = o[128:128 + half]
    return out
```
